# Optimizing a Trainium2 kernel written in Bass

```python
import jax, jax.numpy as jnp
from jax import lax
import numpy as np

D_MODEL = 1024
BATCH = 32
SEQ = 2048
DEPTH = 1
DEC_BATCH = 4
DEC_SEQ = 4096
PAST_LEN = 128

GRID_W = 64
N_ATT_HEADS = 8
ATT_HD = 64
ATT_W = N_ATT_HEADS * ATT_HD
WIN_R = 8
WIN_C = 16
N_DN_HEADS = 4
DN_HD = 128
DN_W = N_DN_HEADS * DN_HD
DN_CONV = 3
CHUNK = 64
N_DIR = 2
D_FF = 2816
FFN_CONV = 3
IN_COLS = 3 * ATT_W + 4 * DN_W + 2 * N_DIR * N_DN_HEADS
NORM_EPS = 1e-6

kernel_name = "hybrid_natten_gdn_encoder"


def _rmsnorm(x, g):
    xf = x.astype(jnp.float32)
    y = xf * lax.rsqrt(jnp.mean(xf * xf, axis=-1, keepdims=True) + NORM_EPS)
    return (y * g.astype(jnp.float32)).astype(x.dtype)


def _l2norm(x):
    return x * lax.rsqrt(jnp.sum(x * x, axis=-1, keepdims=True) + NORM_EPS)


def _dwconv_centred(x, w):
    K = w.shape[0]
    pad = K // 2
    T = x.shape[1]
    xp = jnp.pad(x, ((0, 0), (pad, pad), (0, 0)))
    return sum(xp[:, j:j + T] * w[j] for j in range(K))


def _neighbourhood_attention(q, k, v, rpb):
    B, T, H, Dh = q.shape
    rows = T // GRID_W
    kr = min(WIN_R, rows)
    qg = q.reshape(B, rows, GRID_W, H, Dh)
    kg = k.reshape(B, rows, GRID_W, H, Dh)
    vg = v.reshape(B, rows, GRID_W, H, Dh)
    col = jnp.arange(GRID_W)
    col_idx = jnp.clip(col - WIN_C // 2, 0, GRID_W - WIN_C)[:, None] + jnp.arange(WIN_C)
    col_bias = rpb[:, :, col_idx - col[:, None] + WIN_C - 1]
    scale = Dh ** -0.5

    def one_row(r):
        r0 = jnp.clip(r - kr // 2, 0, rows - kr)
        q_r = lax.dynamic_index_in_dim(qg, r, axis=1, keepdims=False)
        k_band = lax.dynamic_slice_in_dim(kg, r0, kr, axis=1)
        v_band = lax.dynamic_slice_in_dim(vg, r0, kr, axis=1)
        k_sel = k_band[:, :, col_idx]
        v_sel = v_band[:, :, col_idx]
        s = jnp.einsum('bqhd,brqjhd->bhqrj', q_r, k_sel).astype(jnp.float32) * scale
        row_off = r0 + jnp.arange(kr) - r + WIN_R - 1
        bias = jnp.transpose(col_bias[:, row_off], (0, 2, 1, 3))
        s = s + bias.astype(jnp.float32)
        p = jax.nn.softmax(s.reshape(B, H, GRID_W, kr * WIN_C), axis=-1).reshape(s.shape)
        return jnp.einsum('bhqrj,brqjhd->bqhd', p.astype(v.dtype), v_sel)

    out = lax.map(one_row, jnp.arange(rows))
    return jnp.moveaxis(out, 0, 1).reshape(B, T, H * Dh)


def _gated_delta_chunked(q, k, v, g, beta):
    B, T, H, Dk = q.shape
    Dv = v.shape[-1]
    N = T // CHUNK

    def to_chunks(a):
        a = a.reshape((B, N, CHUNK, H) + a.shape[3:])
        return jnp.moveaxis(a, 3, 1)

    q = to_chunks(q * (Dk ** -0.5))
    k = to_chunks(k)
    v = to_chunks(v)
    g = to_chunks(g)
    beta = to_chunks(beta)
    gc = jnp.cumsum(g, axis=-1)
    causal = jnp.tril(jnp.ones((CHUNK, CHUNK), bool))
    strict = jnp.tril(jnp.ones((CHUNK, CHUNK), bool), -1)
    decay = jnp.exp(jnp.where(causal, gc[..., :, None] - gc[..., None, :], -jnp.inf))
    kb = k * beta[..., None]
    lower = jnp.where(strict, jnp.einsum('bhncd,bhnsd->bhncs', kb, k) * decay, 0.0)
    a_mat = lower + jnp.eye(CHUNK, dtype=lower.dtype)
    u = lax.linalg.triangular_solve(a_mat, v * beta[..., None], left_side=True, lower=True, unit_diagonal=True)
    w = lax.linalg.triangular_solve(a_mat, kb * jnp.exp(gc)[..., None], left_side=True, lower=True, unit_diagonal=True)
    intra = jnp.where(causal, jnp.einsum('bhncd,bhnsd->bhncs', q, k) * decay, 0.0)
    g_last = gc[..., -1]
    k_dec = k * jnp.exp(g_last[..., None] - gc)[..., None]
    q_dec = q * jnp.exp(gc)[..., None]

    def step(S, xs):
        q_i, k_i, u_i, w_i, a_i, gl_i = xs
        v_new = u_i - jnp.einsum('bhcd,bhde->bhce', w_i, S)
        o = jnp.einsum('bhcd,bhde->bhce', q_i, S) + jnp.einsum('bhcs,bhse->bhce', a_i, v_new)
        S = S * jnp.exp(gl_i)[..., None, None] + jnp.einsum('bhcd,bhce->bhde', k_i, v_new)
        return S, o

    xs = (jnp.moveaxis(q_dec, 2, 0), jnp.moveaxis(k_dec, 2, 0), jnp.moveaxis(u, 2, 0),
          jnp.moveaxis(w, 2, 0), jnp.moveaxis(intra, 2, 0), jnp.moveaxis(g_last, 2, 0))
    S0 = jnp.zeros((B, H, Dk, Dv), jnp.float32)
    _, o = lax.scan(step, S0, xs)
    o = jnp.moveaxis(o, 0, 2)
    return jnp.moveaxis(o, 1, 3).reshape(B, T, H, Dv)


def _mixer(h, w_in, att_q_norm, att_k_norm, att_rpb, dn_conv_w, dn_a_log, dn_dt_bias, dn_out_norm, w_o):
    B, T, _ = h.shape
    proj = h @ w_in
    s4 = 3 * ATT_W + 4 * DN_W
    aq, ak, av, dqkv, dz, dbeta, dalpha = jnp.split(
        proj, [ATT_W, 2 * ATT_W, 3 * ATT_W, 3 * ATT_W + 3 * DN_W, s4, s4 + N_DIR * N_DN_HEADS], axis=-1)
    aq = _rmsnorm(aq.reshape(B, T, N_ATT_HEADS, ATT_HD), att_q_norm)
    ak = _rmsnorm(ak.reshape(B, T, N_ATT_HEADS, ATT_HD), att_k_norm)
    av = av.reshape(B, T, N_ATT_HEADS, ATT_HD)
    att = _neighbourhood_attention(aq, ak, av, att_rpb)
    dqkv = jax.nn.silu(_dwconv_centred(dqkv, dn_conv_w)).astype(jnp.float32)
    dq, dk, dv = jnp.split(dqkv, 3, axis=-1)
    dq = _l2norm(dq.reshape(B, T, N_DN_HEADS, DN_HD))
    dk = _l2norm(dk.reshape(B, T, N_DN_HEADS, DN_HD))
    dv = dv.reshape(B, T, N_DN_HEADS, DN_HD)
    beta = jax.nn.sigmoid(dbeta.astype(jnp.float32)).reshape(B, T, N_DIR, N_DN_HEADS)
    g = -jnp.exp(dn_a_log.astype(jnp.float32)) * jax.nn.softplus(
        dalpha.astype(jnp.float32).reshape(B, T, N_DIR, N_DN_HEADS) + dn_dt_bias.astype(jnp.float32))
    o_f = _gated_delta_chunked(dq, dk, dv, g[:, :, 0], beta[:, :, 0])
    o_b = jnp.flip(_gated_delta_chunked(jnp.flip(dq, 1), jnp.flip(dk, 1), jnp.flip(dv, 1),
                                        jnp.flip(g[:, :, 1], 1), jnp.flip(beta[:, :, 1], 1)), 1)
    o = _rmsnorm(o_f + o_b, dn_out_norm) * jax.nn.silu(dz.astype(jnp.float32).reshape(B, T, N_DN_HEADS, DN_HD))
    dn = o.reshape(B, T, DN_W).astype(h.dtype)
    return jnp.concatenate([att, dn], axis=-1) @ w_o


def _conv_ffn(h, w_up, conv_w, conv_b, w_down):
    u = _dwconv_centred(h @ w_up, conv_w) + conv_b
    a, b = jnp.split(u, 2, axis=-1)
    return (jax.nn.silu(a) * b) @ w_down


def _trunk(x, c, ada_w, ada_b, norm1_g, norm2_g, w_in, att_q_norm, att_k_norm, att_rpb,
           dn_conv_w, dn_a_log, dn_dt_bias, dn_out_norm, w_o, ffn_w_up, ffn_conv_w, ffn_conv_b, ffn_w_down):
    for l in range(DEPTH):
        mod = (jax.nn.silu(c.astype(jnp.float32)) @ ada_w[l].astype(jnp.float32)
               + ada_b[l].astype(jnp.float32)).astype(x.dtype)
        sh1, sc1, g1, sh2, sc2, g2 = jnp.split(mod[:, None, :], 6, axis=-1)
        h = _rmsnorm(x, norm1_g[l]) * (1 + sc1) + sh1
        x = x + g1 * _mixer(h, w_in[l], att_q_norm[l], att_k_norm[l], att_rpb[l], dn_conv_w[l],
                            dn_a_log[l], dn_dt_bias[l], dn_out_norm[l], w_o[l])
        h = _rmsnorm(x, norm2_g[l]) * (1 + sc2) + sh2
        x = x + g2 * _conv_ffn(h, ffn_w_up[l], ffn_conv_w[l], ffn_conv_b[l], ffn_w_down[l])
    return x


def setup_inputs(seed: int = 0) -> dict:
    key = jax.random.key(seed)
    ks = jax.random.split(key, 24)
    f32 = jnp.float32

    def nrm(k, shape, s):
        return jax.random.normal(k, shape, f32) * s

    L, D = DEPTH, D_MODEL
    dt = jnp.exp(jax.random.uniform(ks[12], (L, N_DIR, N_DN_HEADS), f32, np.log(1e-3), np.log(1e-1)))
    return {
        'x_prompt': nrm(ks[0], (BATCH, SEQ, D), 1.0),
        'x_sample': nrm(ks[1], (DEC_BATCH, DEC_SEQ, D), 1.0),
        'c_prompt': nrm(ks[2], (BATCH, D), 1.0),
        'c_sample': nrm(ks[3], (DEC_BATCH, D), 1.0),
        'ada_w': nrm(ks[4], (L, D, 6 * D), D ** -0.5),
        'ada_b': nrm(ks[5], (L, 6 * D), 0.02),
        'norm1_g': 1.0 + nrm(ks[6], (L, D), 0.02),
        'norm2_g': 1.0 + nrm(ks[7], (L, D), 0.02),
        'w_in': nrm(ks[8], (L, D, IN_COLS), D ** -0.5),
        'att_q_norm': 1.0 + nrm(ks[9], (L, ATT_HD), 0.02),
        'att_k_norm': 1.0 + nrm(ks[10], (L, ATT_HD), 0.02),
        'att_rpb': nrm(ks[11], (L, N_ATT_HEADS, 2 * WIN_R - 1, 2 * WIN_C - 1), 0.02),
        'dn_conv_w': nrm(ks[13], (L, DN_CONV, 3 * DN_W), DN_CONV ** -0.5),
        'dn_a_log': jnp.log(jax.random.uniform(ks[14], (L, N_DIR, N_DN_HEADS), f32, 1.0, 16.0)),
        'dn_dt_bias': dt + jnp.log(-jnp.expm1(-dt)),
        'dn_out_norm': 1.0 + nrm(ks[15], (L, DN_HD), 0.02),
        'w_o': nrm(ks[16], (L, D, D), D ** -0.5),
        'ffn_w_up': nrm(ks[17], (L, D, 2 * D_FF), D ** -0.5),
        'ffn_conv_w': nrm(ks[18], (L, FFN_CONV, 2 * D_FF), FFN_CONV ** -0.5),
        'ffn_conv_b': nrm(ks[19], (L, 2 * D_FF), 0.02),
        'ffn_w_down': nrm(ks[20], (L, D_FF, D), D_FF ** -0.5),
    }


def reference(x_prompt, x_sample, c_prompt, c_sample, ada_w, ada_b, norm1_g, norm2_g, w_in,
              att_q_norm, att_k_norm, att_rpb, dn_conv_w, dn_a_log, dn_dt_bias, dn_out_norm, w_o,
              ffn_w_up, ffn_conv_w, ffn_conv_b, ffn_w_down):
    y_prompt = _trunk(x_prompt, c_prompt, ada_w, ada_b, norm1_g, norm2_g, w_in, att_q_norm, att_k_norm,
                      att_rpb, dn_conv_w, dn_a_log, dn_dt_bias, dn_out_norm, w_o,
                      ffn_w_up, ffn_conv_w, ffn_conv_b, ffn_w_down)
    y_sample = _trunk(x_sample, c_sample, ada_w, ada_b, norm1_g, norm2_g, w_in, att_q_norm, att_k_norm,
                      att_rpb, dn_conv_w, dn_a_log, dn_dt_bias, dn_out_norm, w_o,
                      ffn_w_up, ffn_conv_w, ffn_conv_b, ffn_w_down)
    return (y_prompt, y_sample)
```

```python
import math
from contextlib import ExitStack

import numpy as np
import concourse.bass as bass
import concourse.mybir as mybir
from concourse.bass_utils import run_bass_kernel_spmd

F32 = mybir.dt.float32
BF16 = mybir.dt.bfloat16
AF = mybir.ActivationFunctionType
ALU = mybir.AluOpType

D = 1024
GW = 64
NH_A, HD_A = 8, 64
NH_D, HD_D = 4, 128
DFF = 2816
INC = 3600
EPS = 1e-6
NEG = -30000.0

ENGS = ("pe", "act", "dve", "pool", "sp")
NDMA = 8


class Buf:
    __slots__ = ("name", "lw", "rd")

    def __init__(self, name):
        self.name = name
        self.lw = None
        self.rd = []


class Sched:
    def __init__(self):
        self.q = {e: [] for e in ENGS}
        self.cnt = {e: 0 for e in ENGS}
        self.known = {e: {} for e in ENGS}
        self.dma_n = {e: 0 for e in ENGS}
        self.dma_slot_val = {}
        self.pending = {e: {} for e in ENGS}
        self.nops = 0

    def _deps(self, eng, reads, writes):
        deps = dict(self.pending[eng])
        self.pending[eng] = {}

        def add(ev):
            if ev is None:
                return
            k, v = ev
            if deps.get(k, 0) < v:
                deps[k] = v
        for b in reads:
            add(b.lw)
        for b in writes:
            add(b.lw)
            for r in b.rd:
                add(r)
        return deps

    def _finish(self, eng, fn, deps, ev, reads, writes, inc):
        kn = self.known[eng]
        waits = []
        for k, v in deps.items():
            if kn.get(k, 0) >= v:
                continue
            if eng == "pe" and k == "pe":
                continue
            kn[k] = v
            waits.append((k, v))
        self.q[eng].append((fn, waits, inc))
        self.nops += 1
        for b in reads:
            b.rd.append(ev)
        for b in writes:
            b.lw = ev
            b.rd = []

    def op(self, eng, fn, reads=(), writes=()):
        deps = self._deps(eng, reads, writes)
        self.cnt[eng] += 1
        ev = (eng, self.cnt[eng])
        self._finish(eng, fn, deps, ev, reads, writes, (eng, 1))
        return ev

    def dma(self, queue, fn, reads=(), writes=()):
        deps = self._deps(queue, reads, writes)
        n = self.dma_n[queue]
        self.dma_n[queue] = n + 1
        key = ("dma", queue, n % NDMA)
        prev = self.dma_slot_val.get(key, 0)
        if prev and deps.get(key, 0) < prev:
            deps[key] = prev
        val = prev + 16
        self.dma_slot_val[key] = val
        ev = (key, val)
        self._finish(queue, fn, deps, ev, reads, writes, (key, 16))
        return ev

    def all_events(self):
        evs = {}
        for e in ENGS:
            if self.cnt[e]:
                evs[e] = self.cnt[e]
        for k, v in self.dma_slot_val.items():
            evs[k] = v
        return evs

    def barrier(self):
        evs = self.all_events()
        for e in ENGS:
            p = self.pending[e]
            for k, v in evs.items():
                if p.get(k, 0) < v:
                    p[k] = v

    def emit(self, nc):
        final = self.all_events()
        with ExitStack() as es:
            sems = {}
            for e in ENGS:
                sems[e] = es.enter_context(nc.semaphore("s_" + e))
            for q in ("sp", "pool"):
                for s in range(NDMA):
                    sems[("dma", q, s)] = es.enter_context(nc.semaphore("d_%s%d" % (q, s)))
            block = es.enter_context(nc.Block())

            def run(name):
                def body(eng):
                    for fn, waits, inc in self.q[name]:
                        for k, v in waits:
                            eng.wait_ge(sems[k], v)
                        fn(eng).then_inc(sems[inc[0]], inc[1])
                    if name == "sp":
                        for k, v in final.items():
                            eng.wait_ge(sems[k], v)
                return body

            block.tensor(run("pe"))
            block.scalar(run("act"))
            block.vector(run("dve"))
            block.gpsimd(run("pool"))
            block.sync(run("sp"))


_UID = [0]


class Rot:
    def __init__(self, es, nc, name, n, shape, dt, psum=False):
        self.t = []
        for i in range(n):
            _UID[0] += 1
            nm = "r_%s%d_%d" % (name, i, _UID[0])
            if psum:
                t = es.enter_context(nc.psum_tensor(nm, shape, dt))
            else:
                t = es.enter_context(nc.sbuf_tensor(nm, shape, dt))
            self.t.append((t, Buf(nm)))
        self.i = 0

    def next(self):
        r = self.t[self.i % len(self.t)]
        self.i += 1
        return r


def tiles_of(T, mx=510):
    n = -(-T // mx)
    base, rem = divmod(T, n)
    out, t0 = [], 0
    for i in range(n):
        s = base + (1 if i < rem else 0)
        out.append((t0, s))
        t0 += s
    return out


def build(seqs, stages="0ABCDE", debug=False):
    NSEQ = len(seqs)
    NTOK = sum(seqs)
    offs = [sum(seqs[:i]) for i in range(NSEQ)]
    nc = bass.Bass("TRN2", target_bir_lowering=False)
    S = Sched()
    kind_dbg = "ExternalOutput" if debug else "Internal"

    def din(name, shape, dt=F32):
        return nc.dram_tensor(name, list(shape), dt, kind="ExternalInput").ap()

    def dscr(name, shape, dt):
        return nc.dram_tensor(name, list(shape), dt, kind=kind_dbg).ap()

    dumped = set()

    def dump(name, ap, B, dt=F32):
        if not debug or name in dumped:
            return
        dumped.add(name)
        t = nc.dram_tensor("dbg_" + name, list(ap.shape), dt, kind="ExternalOutput").ap()
        S.dma("pool", lambda e: e.dma_start(out=t, in_=ap), reads=[B])

    x_d = din("x", [NTOK + 2, D])
    cT_d = din("cT", [128, 8, NSEQ])
    adaw_d = din("ada_w", [D, 6 * D])
    adab_d = din("ada_b", [NSEQ, 6 * D])
    n12_d = din("n12", [NSEQ, 2, D])
    win_d = din("w_in", [D, INC])
    qkn_d = din("qkn", [128, 2])
    rb_d = din("rb", [128, NH_A, 16, GW])
    dcw_d = din("dcw", [128, 12, 3])
    dgate_d = din("dgate", [8, 2])
    don_d = din("don", [128, 1])
    wo_d = din("w_o", [D, D])
    wup_d = din("w_up", [D, 2 * DFF])
    fcw_d = din("fcw", [128, 44, 3])
    fcb_d = din("fcb", [128, 44])
    wdn_d = din("w_dn", [DFF, D])
    ident_d = din("ident", [128, 128])
    blk2_d = din("blk2", [128, 128])
    masks_d = din("masks", [128, 4, 128])
    m01_d = din("m01", [4, 512])
    mk7_d = din("mk7", [128, 7, 128])
    y_d = nc.dram_tensor("y", [NTOK, D], F32, kind="ExternalOutput").ap()

    QT = dscr("QT", [512, NTOK], BF16)
    KT = dscr("KT", [512, NTOK], BF16)
    VX = dscr("VX", [NTOK, 512], BF16)
    DQT = dscr("DQT", [512, NTOK], BF16)
    DKT = dscr("DKT", [512, NTOK], BF16)
    DKK = dscr("DKK", [NTOK, 512], BF16)
    DVV = dscr("DVV", [NTOK, 512], BF16)
    ZT = dscr("ZT", [512, NTOK], BF16)
    GR = dscr("GR", [16, NTOK], F32)
    MOD = dscr("MOD", [NSEQ, 6 * D], F32)
    ATT = dscr("ATT", [NTOK, 512], BF16)
    OSC = dscr("OSC", [NTOK, 512], F32)
    BCS = dscr("BCS", [12, 512], F32)
    XN = dscr("XN", [NTOK + 2, D], F32)
    WUB = dscr("WUB", [22, 128, 2, 8, 128], BF16)

    with ExitStack() as es0:
        def sb(name, shape, dt, es=es0):
            _UID[0] += 1
            return es.enter_context(nc.sbuf_tensor("s_%s_%d" % (name, _UID[0]), list(shape), dt))

        idf = sb("idf", [128, 128], F32)
        idb = sb("idb", [128, 128], BF16)
        blk2 = sb("blk2", [128, 128], BF16)
        ones_b = sb("ones_b", [128, 128], BF16)
        Bconst = Buf("const")
        S.dma("sp", lambda e: e.dma_start(out=idf[:], in_=ident_d), writes=[Bconst])
        S.dma("pool", lambda e: e.dma_start(out=blk2[:], in_=blk2_d), writes=[Bconst])
        S.op("dve", lambda e: e.tensor_copy(out=idb[:], in_=idf[:]), reads=[Bconst], writes=[Bconst])
        S.op("pool", lambda e: e.memset(ones_b[:], 1.0), writes=[Bconst])

        if "0" in stages:
            with ExitStack() as es:
                cT = sb("cT", [128, 8, NSEQ], F32, es)
                BcT = Buf("cT")
                modr = sb("modr", [NSEQ, 6 * D], F32, es)
                Bmod = Buf("modr")
                adab = sb("adab", [NSEQ, 6 * D], F32, es)
                Badab = Buf("adab")
                n12 = sb("n12", [NSEQ, 2, D], F32, es)
                Bn12 = Buf("n12")
                awr = Rot(es, nc, "aw", 2, [128, 8, 512], F32)
                pmod = Rot(es, nc, "pmod", 2, [NSEQ, 512], F32, psum=True)
                S.dma("sp", lambda e: e.dma_start(out=cT[:], in_=cT_d), writes=[BcT])
                S.dma("sp", lambda e: e.dma_start(out=adab[:], in_=adab_d), writes=[Badab])
                S.dma("sp", lambda e: e.dma_start(out=n12[:], in_=n12_d), writes=[Bn12])
                S.op("act", lambda e: e.activation(out=cT[:], in_=cT[:], func=AF.Silu), reads=[BcT], writes=[BcT])
                for nb in range(12):
                    aw, Baw = awr.next()
                    S.dma("sp", lambda e, aw=aw, nb=nb: e.dma_start(
                        out=aw[:], in_=adaw_d[:, nb * 512:(nb + 1) * 512].rearrange("(kc p) n -> p kc n", p=128)),
                        writes=[Baw])
                    pm, Bpm = pmod.next()
                    for kc in range(8):
                        S.op("pe", lambda e, pm=pm, aw=aw, kc=kc: e.matmul(
                            pm[:], lhsT=cT[:, kc, :], rhs=aw[:, kc, :], start=(kc == 0), stop=(kc == 7)),
                            reads=[BcT, Baw], writes=[Bpm])
                    S.op("dve", lambda e, pm=pm, nb=nb: e.tensor_tensor(
                        out=modr[:, nb * 512:(nb + 1) * 512], in0=pm[:], in1=adab[:, nb * 512:(nb + 1) * 512], op=ALU.add),
                        reads=[Bpm, Badab], writes=[Bmod])
                for j, c0 in ((0, D), (1, 4 * D)):
                    S.op("dve", lambda e, j=j, c0=c0: e.scalar_tensor_tensor(
                        out=modr[:, c0:c0 + D], in0=modr[:, c0:c0 + D], scalar=1.0, in1=n12[:, j, :],
                        op0=ALU.add, op1=ALU.mult), reads=[Bmod, Bn12], writes=[Bmod])
                S.dma("sp", lambda e: e.dma_start(out=MOD, in_=modr[:]), reads=[Bmod])
            S.barrier()

        def load_bcast(bc, Bbc, b, slots):
            for i, sl in enumerate(slots):
                S.dma("sp", lambda e, i=i, sl=sl: e.dma_start(
                    out=bc[:, i, :], in_=MOD[b:b + 1, sl * D:(sl + 1) * D].partition_broadcast(128)),
                    writes=[Bbc])


        def stage_B(b, rbt, BcM):
            T = seqs[b]
            Rr = T // GW
            NCH = Rr // 2
            o0 = offs[b]

            def r0_of(r):
                return min(max(r - 4, 0), Rr - 8)

            with ExitStack() as es:
                rbt = sb("rbt", [128, NH_A, 16, GW], F32, es)
                BcM = Buf("constM")
                S.dma("sp", lambda e: e.dma_start(out=rbt[:], in_=rb_d), writes=[BcM])
                qT = sb("qT", [128, 4, T], BF16, es)
                kT = sb("kT", [128, 4, T], BF16, es)
                vx = sb("vx", [128, NCH, NH_A, HD_A + 1], BF16, es)
                Bq, Bk, Bv = Buf("qT"), Buf("kT"), Buf("vx")
                for pr in range(4):
                    S.dma("sp", lambda e, pr=pr: e.dma_start(out=qT[:, pr, :], in_=QT[pr * 128:(pr + 1) * 128, o0:o0 + T]),
                          writes=[Bq])
                    S.dma("sp", lambda e, pr=pr: e.dma_start(out=kT[:, pr, :], in_=KT[pr * 128:(pr + 1) * 128, o0:o0 + T]),
                          writes=[Bk])
                S.op("pool", lambda e: e.memset(vx[:, :, :, HD_A:HD_A + 1], 1.0), writes=[Bv])
                for j in range(NCH):
                    S.dma("sp", lambda e, j=j: e.dma_start(
                        out=vx[:, j, :, 0:HD_A],
                        in_=VX[o0 + j * 128:o0 + (j + 1) * 128, :].rearrange("p (h d) -> p h d", d=HD_A)), writes=[Bv])
                ps_r = Rot(es, nc, "psB", 2, [128, 1024], F32, psum=True)
                po_r = Rot(es, nc, "poB", 2, [128, 4, HD_A + 1], F32, psum=True)
                ss_r = Rot(es, nc, "ssB", 2, [128, 768], F32)
                pT_r = Rot(es, nc, "pTB", 24, [128, 768], BF16)
                rd_r = Rot(es, nc, "rdB", 2, [128, 4, 1], F32)
                ab_r = Rot(es, nc, "abB", 3, [128, 4, HD_A], BF16)

                def do_hg(hg):
                    pts = {}

                    def do_chunk(j, hl):
                        h = hg * 4 + hl
                        pr, hb = h // 2, (h % 2) * 64
                        rows = [r for r in range(Rr) if r0_of(r) <= 2 * j + 1 and r0_of(r) + 7 >= 2 * j]
                        lo, nq = rows[0], len(rows)
                        e_lo = lo - 2 * j + 7
                        ps, Bps = ps_r.next()
                        n1 = min(512, nq * 64)
                        S.op("pe", lambda e: e.matmul(
                            ps[:, 0:n1], lhsT=kT[hb:hb + 64, pr, j * 128:(j + 1) * 128],
                            rhs=qT[hb:hb + 64, pr, lo * 64:lo * 64 + n1], start=True, stop=True),
                            reads=[Bk, Bq], writes=[Bps])
                        if nq * 64 > 512:
                            S.op("pe", lambda e: e.matmul(
                                ps[:, 512:nq * 64], lhsT=kT[hb:hb + 64, pr, j * 128:(j + 1) * 128],
                                rhs=qT[hb:hb + 64, pr, lo * 64 + 512:(lo + nq) * 64], start=True, stop=True),
                                reads=[Bk, Bq], writes=[Bps])
                        ss, Bss = ss_r.next()
                        pt, Bpt = pT_r.next()
                        for (a0, a1) in ((0, min(nq, 8)), (8, nq)):
                            if a1 <= a0:
                                continue
                            S.op("dve", lambda e, a0=a0, a1=a1: e.tensor_tensor(
                                out=ss[:, a0 * 64:a1 * 64].rearrange("p (a c) -> p a c", c=GW),
                                in0=ps[:, a0 * 64:a1 * 64].rearrange("p (a c) -> p a c", c=GW),
                                in1=rbt[:, h, e_lo + a0:e_lo + a1, :], op=ALU.add), reads=[Bps, BcM], writes=[Bss])
                        S.op("act", lambda e: e.activation(out=pt[:, 0:nq * 64], in_=ss[:, 0:nq * 64], func=AF.Exp),
                             reads=[Bss], writes=[Bpt])
                        for r in rows:
                            r0 = r0_of(r)
                            v0 = r0 <= 2 * j <= r0 + 7
                            v1 = r0 <= 2 * j + 1 <= r0 + 7
                            cc = (r - lo) * 64
                            if not v0:
                                S.op("pool", lambda e, cc=cc: e.memset(pt[0:64, cc:cc + 64], 0.0), reads=[], writes=[Bpt])
                            if not v1:
                                S.op("pool", lambda e, cc=cc: e.memset(pt[64:128, cc:cc + 64], 0.0), reads=[], writes=[Bpt])
                        pts[(j, hl)] = (pt, Bpt, lo)

                    def do_rowpair(rp):
                        po, Bpo = po_r.next()

                        def pv(r, hl):
                            h = hg * 4 + hl
                            r0 = r0_of(r)
                            chunks = list(range(r0 // 2, (r0 + 7) // 2 + 1))
                            ph = (r % 2) * 64
                            for ci, j in enumerate(chunks):
                                pt, Bpt, lo = pts[(j, hl)]
                                k0, k1 = 0, 128
                                c0 = (r - lo) * 64
                                S.op("pe", lambda e, pt=pt, j=j, k0=k0, k1=k1, c0=c0, ci=ci: e.matmul(
                                    po[ph:ph + 64, hl, :], lhsT=pt[k0:k1, c0:c0 + 64], rhs=vx[k0:k1, j, h, :],
                                    start=(ci == 0), stop=(ci == len(chunks) - 1)),
                                    reads=[Bpt, Bv], writes=[Bpo])
                        for r in (2 * rp, 2 * rp + 1):
                            for hl in range(4):
                                pv(r, hl)
                        rd, Brd = rd_r.next()
                        ab, Bab = ab_r.next()
                        S.op("dve", lambda e: e.reciprocal(out=rd[:], in_=po[:, :, HD_A:HD_A + 1]), reads=[Bpo], writes=[Brd])
                        S.op("dve", lambda e: e.tensor_tensor(
                            out=ab[:], in0=po[:, :, 0:HD_A], in1=rd[:].to_broadcast([128, 4, HD_A]),
                            op=ALU.mult), reads=[Bpo, Brd], writes=[Bab])
                        S.dma("pool", lambda e: e.dma_start(
                            out=ATT[o0 + rp * 128:o0 + (rp + 1) * 128, hg * 256:(hg + 1) * 256].rearrange(
                                "p (h d) -> p h d", d=HD_A), in_=ab[:]), reads=[Bab])

                    done_rp = 0
                    for j in range(NCH):
                        for hl in range(4):
                            do_chunk(j, hl)
                        while done_rp < Rr // 2:
                            need = max((r0_of(r) + 7) // 2 for r in (2 * done_rp, 2 * done_rp + 1))
                            if need > j:
                                break
                            do_rowpair(done_rp)
                            done_rp += 1

                for hg_ in range(2):
                    do_hg(hg_)


        def stage_C(b):
            T = seqs[b]
            o0 = offs[b]
            SEG = 512
            NSEG = T // SEG
            NCK = SEG // 64
            with ExitStack() as es:
                masks = sb("masksC", [128, 4, 128], F32, es)
                id4 = sb("id4C", [128, 4, 128], BF16, es)
                m01 = sb("m01C", [4, SEG], F32, es)
                mk7f = sb("mk7fC", [128, 7, 128], F32, es)
                mk7 = sb("mk7C", [128, 7, 4, 128], BF16, es)
                BcC = Buf("constC")
                S.dma("sp", lambda e: e.dma_start(out=mk7f[:], in_=mk7_d), writes=[BcC])
                for h in range(4):
                    S.op("pool", lambda e, h=h: e.tensor_copy(out=mk7[:, :, h, :], in_=mk7f[:]), reads=[BcC], writes=[BcC])
                S.dma("sp", lambda e: e.dma_start(out=masks[:], in_=masks_d), writes=[BcC])
                S.dma("sp", lambda e: e.dma_start(out=m01[:], in_=m01_d), writes=[BcC])
                for h in range(4):
                    S.op("pool", lambda e, h=h: e.tensor_copy(out=id4[:, h, :], in_=idb[:]), reads=[Bconst], writes=[BcC])
                Sf = sb("SfC", [128, 4, 128], F32, es)
                Sb_ = sb("SbC", [128, 4, 128], BF16, es)
                BSf, BSb = Buf("Sf"), Buf("Sb")
                dg = sb("dgC", [4, 4], F32, es)
                Bdg = Buf("dg")
                kT_r = Rot(es, nc, "kTC", 2, [128, 4, SEG], BF16)
                qT_r = Rot(es, nc, "qTC", 2, [128, 4, SEG], BF16)
                kk_r = Rot(es, nc, "kkC", 2, [128, SEG // 128, 512], BF16)
                vv_r = Rot(es, nc, "vvC", 2, [128, SEG // 128, 512], BF16)
                row_r = Rot(es, nc, "rowC", 16, [4, SEG], F32)
                bc_r = Rot(es, nc, "bcC", 2, [128, 8, SEG], F32)
                egl_r = Rot(es, nc, "eglC", 2, [128, 4, NCK], F32)
                sc_r = Rot(es, nc, "scC", 2, [128, 6, 4], F32)
                pp_r = Rot(es, nc, "ppC", 3, [128, 4, 128], F32, psum=True)
                pkq_r = Rot(es, nc, "pkqC", 2, [128, 4, 128], F32, psum=True)
                pvo_r = Rot(es, nc, "pvoC", 2, [128, 4, 128], F32, psum=True)
                pS_r = Rot(es, nc, "pSC", 1, [128, 4, 128], F32, psum=True)
                dm_r = Rot(es, nc, "dmC", 3, [128, 128], F32)
                E_r = Rot(es, nc, "EC", 3, [128, 4, 128], F32)
                mb_r = Rot(es, nc, "mbC", 16, [128, 4, 128], BF16)
                m0_r = Rot(es, nc, "m0C", 4, [128, 4, 128], BF16)
                it_r = Rot(es, nc, "itC", 2, [128, 4, 128], BF16)
                bv_r = Rot(es, nc, "bvC", 2, [128, 4, 128], F32)
                r_r = Rot(es, nc, "rC", 2, [128, 4, 128], BF16)
                qs_r = Rot(es, nc, "qsC", 2, [128, 4, 128], F32)
                vn_r = Rot(es, nc, "vnC", 2, [128, 4, 128], BF16)
                vd_r = Rot(es, nc, "vdC", 2, [128, 4, 128], BF16)
                o_r = Rot(es, nc, "oC", 2, [128, 4, 128], F32)
                of_r = Rot(es, nc, "ofC", 2, [128, 4, 128], F32)
                Bbcs = Buf("BCS")

                def seg_prep(d, sg):
                    t0 = o0 + sg * SEG
                    def row():
                        return row_r.next()
                    bl, Bbl = row(); al, Bal = row()
                    S.dma("sp", lambda e: e.dma_start(out=bl[:], in_=GR[d * 4:d * 4 + 4, t0:t0 + SEG]), writes=[Bbl])
                    S.dma("sp", lambda e: e.dma_start(out=al[:], in_=GR[8 + d * 4:12 + d * 4, t0:t0 + SEG]), writes=[Bal])
                    l1, Bl1 = row(); sp_, Bsp = row(); g, Bg = row(); pre, Bpre = row()
                    S.op("act", lambda e: e.activation(out=l1[:], in_=bl[:], func=AF.Exp, scale=-1.0), reads=[Bbl], writes=[Bl1])
                    S.op("act", lambda e: e.activation(out=l1[:], in_=l1[:], func=AF.Ln, bias=1.0), reads=[Bl1], writes=[Bl1])
                    S.op("act", lambda e: e.activation(out=sp_[:], in_=al[:], func=AF.Exp, bias=dg[:, 1:2]), reads=[Bal, Bdg], writes=[Bsp])
                    S.op("act", lambda e: e.activation(out=sp_[:], in_=sp_[:], func=AF.Ln, bias=1.0), reads=[Bsp], writes=[Bsp])
                    S.op("dve", lambda e: e.tensor_scalar(out=g[:], in0=sp_[:], scalar1=dg[:, 2:3], scalar2=None, op0=ALU.mult),
                         reads=[Bsp, Bdg], writes=[Bg])
                    S.op("dve", lambda e: e.tensor_tensor_scan(out=pre[:], data0=m01[:], data1=g[:], initial=0.0,
                                                               op0=ALU.mult, op1=ALU.add), reads=[Bg, BcC], writes=[Bpre])
                    tot = pre[:].rearrange("p (n c) -> p n c", c=64)[:, :, 63:64]
                    def v3(t):
                        return t[:].rearrange("p (n c) -> p n c", c=64)
                    if d == 0:
                        gc, Bgc = pre, Bpre
                    else:
                        gc, Bgc = row()
                        S.op("dve", lambda e: e.tensor_tensor(out=v3(gc), in0=tot.to_broadcast([4, NCK, 64]), in1=v3(pre),
                                                              op=ALU.subtract), reads=[Bpre], writes=[Bgc])
                        S.op("dve", lambda e: e.tensor_tensor(out=gc[:], in0=gc[:], in1=g[:], op=ALU.add),
                             reads=[Bgc, Bg], writes=[Bgc])
                    ngc, Bngc = row(); gb, Bgb = row(); egc, Begc = row(); beta, Bbeta = row()
                    nbeg, Bnbeg = row(); edk, Bedk = row(); egl, Begl = row()
                    S.op("dve", lambda e: e.tensor_scalar(out=ngc[:], in0=gc[:], scalar1=-1.0, scalar2=None, op0=ALU.mult),
                         reads=[Bgc], writes=[Bngc])
                    S.op("dve", lambda e: e.tensor_tensor(out=gb[:], in0=gc[:], in1=l1[:], op=ALU.subtract),
                         reads=[Bgc, Bl1], writes=[Bgb])
                    S.op("act", lambda e: e.activation(out=egc[:], in_=gc[:], func=AF.Exp), reads=[Bgc], writes=[Begc])
                    S.op("act", lambda e: e.activation(out=beta[:], in_=l1[:], func=AF.Exp, scale=-1.0), reads=[Bl1], writes=[Bbeta])
                    S.op("dve", lambda e: e.scalar_tensor_tensor(out=nbeg[:], in0=beta[:], scalar=-1.0, in1=egc[:],
                                                                 op0=ALU.mult, op1=ALU.mult), reads=[Bbeta, Begc], writes=[Bnbeg])
                    S.op("dve", lambda e: e.tensor_tensor(out=v3(edk), in0=tot.to_broadcast([4, NCK, 64]), in1=v3(gc),
                                                          op=ALU.subtract), reads=[Bpre, Bgc], writes=[Bedk])
                    S.op("act", lambda e: e.activation(out=edk[:], in_=edk[:], func=AF.Exp), reads=[Bedk], writes=[Bedk])
                    S.op("act", lambda e: e.activation(out=egl[:, 0:NCK].rearrange("p (n o) -> p n o", o=1), in_=tot,
                                                       func=AF.Exp), reads=[Bpre], writes=[Begl])
                    S.dma("pool", lambda e: e.dma_start(out=BCS[0:4, 0:SEG], in_=ngc[:]), reads=[Bngc], writes=[Bbcs])
                    S.dma("pool", lambda e: e.dma_start(out=BCS[4:8, 0:SEG], in_=gb[:]), reads=[Bgb], writes=[Bbcs])
                    S.dma("pool", lambda e: e.dma_start(out=BCS[8:12, 0:NCK], in_=egl[:, 0:NCK]), reads=[Begl], writes=[Bbcs])
                    bc, Bbc = bc_r.next()
                    eglb, Beglb = egl_r.next()
                    for i in range(8):
                        S.dma("sp", lambda e, i=i: e.dma_start(out=bc[:, i, :], in_=BCS[i:i + 1, 0:SEG].partition_broadcast(128)),
                              reads=[Bbcs], writes=[Bbc])
                    for h in range(4):
                        S.dma("sp", lambda e, h=h: e.dma_start(out=eglb[:, h, :], in_=BCS[8 + h:9 + h, 0:NCK].partition_broadcast(128)),
                              reads=[Bbcs], writes=[Beglb])
                    kTs, BkT = kT_r.next(); qTs, BqT = qT_r.next(); kks, Bkk = kk_r.next(); vvs, Bvv = vv_r.next()
                    S.dma("sp", lambda e: e.dma_start(out=kTs[:], in_=DKT[:, t0:t0 + SEG].rearrange("(h p) t -> p h t", p=128)), writes=[BkT])
                    S.dma("sp", lambda e: e.dma_start(out=qTs[:], in_=DQT[:, t0:t0 + SEG].rearrange("(h p) t -> p h t", p=128)), writes=[BqT])
                    S.dma("sp", lambda e: e.dma_start(out=kks[:], in_=DKK[t0:t0 + SEG, :].rearrange("(j p) c -> p j c", p=128)), writes=[Bkk])
                    S.dma("sp", lambda e: e.dma_start(out=vvs[:], in_=DVV[t0:t0 + SEG, :].rearrange("(j p) c -> p j c", p=128)), writes=[Bvv])
                    rows = dict(beta=(beta, Bbeta), egc=(egc, Begc), edk=(edk, Bedk), nbeg=(nbeg, Bnbeg), gb=(gb, Bgb), ngc=(ngc, Bngc))
                    return rows, (bc, Bbc), (eglb, Beglb), (kTs, BkT), (qTs, BqT), (kks, Bkk), (vvs, Bvv)

                import os
                CSTOP = int(os.environ.get("CSTOP", "9"))

                def block(d, sg, jb, segdata):
                    rows, (bc, Bbc), (eglb, Beglb), (kTs, BkT), (qTs, BqT), (kks, Bkk), (vvs, Bvv) = segdata
                    if CSTOP < 2:
                        return
                    c0 = jb * 128
                    gtok = o0 + sg * SEG + c0
                    psc, Bpsc = pp_r.next()
                    order = ("beta", "egc", "edk", "nbeg", "gb", "ngc")
                    for qi, nm in enumerate(order):
                        rt, Brt = rows[nm]
                        S.op("pe", lambda e, qi=qi, rt=rt: e.transpose(out=psc[:, 0, qi * 4:qi * 4 + 4], in_=rt[:, c0:c0 + 128],
                                                                     identity=idf[0:4, 0:4]), reads=[Brt, Bconst], writes=[Bpsc])
                    sc, Bsc = sc_r.next()
                    S.op("dve", lambda e: e.tensor_copy(out=sc[:].rearrange("p a b -> p (a b)"), in_=psc[:, 0, 0:24]),
                         reads=[Bpsc], writes=[Bsc])
                    BETA, EGC, EDK, NBEG, GB, NGC = range(6)
                    pG, BpG = pp_r.next()
                    pKQ, BpKQ = pp_r.next()
                    for h in range(4):
                        S.op("pe", lambda e, h=h: e.matmul(pG[:, h, :], lhsT=kTs[:, h, c0:c0 + 128], rhs=kTs[:, h, c0:c0 + 128],
                                                           start=True, stop=True), reads=[BkT], writes=[BpG])
                    for h in range(4):
                        S.op("pe", lambda e, h=h: e.matmul(pKQ[:, h, :], lhsT=kTs[:, h, c0:c0 + 128], rhs=qTs[:, h, c0:c0 + 128],
                                                           start=True, stop=True), reads=[BkT, BqT], writes=[BpKQ])
                    E1, BE1 = E_r.next(); E1T, BE1T = E_r.next(); E2T, BE2T = E_r.next()
                    mi = (0, 1, 2) if d == 0 else (1, 0, 3)
                    def emat(E, BE, h, src_i, sgn, mk, bias_q):
                        dm, Bdm = dm_r.next()
                        if sgn > 0:
                            S.op("pool", lambda e: e.tensor_tensor(
                                out=dm[:], in0=masks[:, mk, :], in1=bc[:, src_i * 4 + h, c0:c0 + 128], op=ALU.add),
                                reads=[Bbc, BcC], writes=[Bdm])
                        else:
                            S.op("pool", lambda e: e.tensor_tensor(
                                out=dm[:], in0=masks[:, mk, :], in1=bc[:, src_i * 4 + h, c0:c0 + 128], op=ALU.subtract),
                                reads=[Bbc, BcC], writes=[Bdm])
                        S.op("act", lambda e: e.activation(out=E[:, h, :], in_=dm[:], func=AF.Exp, bias=sc[:, bias_q, h:h + 1]),
                             reads=[Bdm, Bsc], writes=[BE])
                    for h in range(4):
                        emat(E1, BE1, h, 0, 1.0, mi[0], GB)
                        emat(E1T, BE1T, h, 1, 1.0, mi[1], NGC)
                        emat(E2T, BE2T, h, 0, -1.0, mi[2], NGC)
                    Q0, BQ0 = m0_r.next(); P0, BP0 = m0_r.next()
                    iT, BiT = it_r.next()
                    S.op("dve", lambda e: e.scalar_tensor_tensor(out=Q0[:], in0=pG[:], scalar=-1.0, in1=E1[:], op0=ALU.mult, op1=ALU.mult),
                         reads=[BpG, BE1], writes=[BQ0])
                    S.op("dve", lambda e: e.scalar_tensor_tensor(out=P0[:], in0=pG[:], scalar=-1.0, in1=E1T[:], op0=ALU.mult, op1=ALU.mult),
                         reads=[BpG, BE1T], writes=[BP0])
                    S.op("dve", lambda e: e.tensor_tensor(out=iT[:], in0=pKQ[:], in1=E2T[:], op=ALU.mult),
                         reads=[BpKQ, BE2T], writes=[BiT])
                    if CSTOP < 3:
                        return
                    if d == 0:
                        X_, BX_, XT_, BXT_ = inverse(Q0, BQ0, P0, BP0)
                        Y, BY = XT_, BXT_
                    else:
                        X_, BX_, XT_, BXT_ = inverse(P0, BP0, Q0, BQ0)
                        Y, BY = X_, BX_
                    dump("Y", Y[:], BY, BF16)
                    if CSTOP < 4:
                        return
                    bv, Bbv = bv_r.next()
                    S.op("pool", lambda e: e.tensor_tensor(
                        out=bv[:], in0=vvs[:, jb, :].rearrange("p (h d) -> p h d", d=128),
                        in1=sc[:, BETA, :].unsqueeze(2).to_broadcast([128, 4, 128]), op=ALU.mult), reads=[Bvv, Bsc], writes=[Bbv])
                    ob, Bob = o_r.next()
                    if CSTOP < 5:
                        return
                    CPS = os.environ.get("CPS", "")
                    for cp in ((0, 64) if d == 0 else (64, 0)):
                        if CPS and cp != int(CPS):
                            continue
                        chunk(d, sg, jb, cp, segdata, sc, Bsc, Y, BY, iT, BiT, bv, Bbv, ob, Bob)
                    dst = OSC[gtok:gtok + 128, :].rearrange("p (h d) -> p h d", d=128)
                    if d == 0:
                        S.dma("pool", lambda e: e.dma_start(out=dst, in_=ob[:]), reads=[Bob], writes=[Bosc])
                    else:
                        of, Bof = of_r.next()
                        S.dma("sp", lambda e: e.dma_start(out=of[:], in_=dst), reads=[Bosc], writes=[Bof])
                        S.op("pool", lambda e: e.tensor_tensor(out=of[:], in0=of[:], in1=ob[:], op=ALU.add), reads=[Bof, Bob], writes=[Bof])
                        S.dma("pool", lambda e: e.dma_start(out=dst, in_=of[:]), reads=[Bof], writes=[Bosc])

                def mm4(lhs, Blhs, rhs, Brhs, acc=None):
                    pt, Bpt = pp_r.next()
                    for h in range(4):
                        if acc is not None:
                            S.op("pe", lambda e, h=h: e.matmul(pt[:, h, :], lhsT=idb[:], rhs=acc[0][:, h, :], start=True, stop=False),
                                 reads=[Bconst, acc[1]], writes=[Bpt])
                        S.op("pe", lambda e, h=h: e.matmul(pt[:, h, :], lhsT=lhs[:, h, :], rhs=rhs[:, h, :],
                                                           start=(acc is None), stop=True), reads=[Blhs, Brhs], writes=[Bpt])
                    return pt, Bpt

                evn = [0]

                def evac(pt, Bpt):
                    t, Bt = mb_r.next()
                    evn[0] += 1
                    if evn[0] % 2:
                        S.op("act", lambda e: e.copy(out=t[:], in_=pt[:]), reads=[Bpt], writes=[Bt])
                    else:
                        S.op("dve", lambda e: e.tensor_copy(out=t[:], in_=pt[:]), reads=[Bpt], writes=[Bt])
                    return t, Bt

                def masked(src, Bsrc, mi_, eng):
                    t, Bt = mb_r.next()
                    S.op(eng, lambda e: e.tensor_tensor(out=t[:], in0=src[:], in1=mk7[:, mi_, :, :], op=ALU.mult),
                         reads=[Bsrc, BcC], writes=[Bt])
                    return t, Bt

                def plus_id(src, Bsrc):
                    t, Bt = mb_r.next()
                    S.op("pool", lambda e: e.tensor_tensor(out=t[:], in0=src[:], in1=id4[:], op=ALU.add), reads=[Bsrc, BcC], writes=[Bt])
                    return t, Bt

                def inverse(Ml, BMl, Nu, BNu):
                    Q, BQ = masked(Ml, BMl, 0, "dve")
                    P, BP = masked(Nu, BNu, 0, "pool")
                    Z, BZ = plus_id(Q, BQ)
                    Y, BY = plus_id(P, BP)
                    for lvl in range(2):
                        pQ, BpQ = mm4(P, BP, Q, BQ)
                        pP, BpP = mm4(Q, BQ, P, BP)
                        Q, BQ = evac(pQ, BpQ)
                        P, BP = evac(pP, BpP)
                        pY, BpY = mm4(Q, BQ, Y, BY, acc=(Y, BY))
                        pZ, BpZ = mm4(P, BP, Z, BZ, acc=(Z, BZ))
                        Y, BY = evac(pY, BpY)
                        Z, BZ = evac(pZ, BpZ)
                    X, BX, XT, BXT = Z, BZ, Y, BY
                    for li in range(3):
                        Mo, BMo = masked(Ml, BMl, 1 + li, "dve")
                        No, BNo = masked(Nu, BNu, 4 + li, "pool")
                        pB, BpB = mm4(No, BNo, X, BX)
                        pB2, BpB2 = mm4(Mo, BMo, XT, BXT)
                        Bm, BBm = evac(pB, BpB)
                        Bm2, BBm2 = evac(pB2, BpB2)
                        pC, BpC = mm4(XT, BXT, Bm, BBm, acc=(X, BX))
                        pC2, BpC2 = mm4(X, BX, Bm2, BBm2, acc=(XT, BXT))
                        X, BX = evac(pC, BpC)
                        XT, BXT = evac(pC2, BpC2)
                    return X, BX, XT, BXT

                def chunk(d, sg, jb, cp, segdata, sc, Bsc, Y, BY, iT, BiT, bv, Bbv, ob, Bob):
                    rows, (bc, Bbc), (eglb, Beglb), (kTs, BkT), (qTs, BqT), (kks, Bkk), (vvs, Bvv) = segdata
                    BETA, EGC, EDK, NBEG, GB, NGC = range(6)
                    c0 = jb * 128 + cp
                    ck = (jb * 128 + cp) // 64
                    pk, Bpk = pkq_r.next(); pq, Bpq = pkq_r.next()
                    sl = slice(cp, cp + 64)
                    for h in range(4):
                        S.op("pe", lambda e, h=h: e.matmul(pk[sl, h, :], lhsT=kTs[:, h, c0:c0 + 64], rhs=Sb_[:, h, :], start=True, stop=True),
                             reads=[BkT, BSb], writes=[Bpk])
                    for h in range(4):
                        S.op("pe", lambda e, h=h: e.matmul(pq[sl, h, :], lhsT=qTs[:, h, c0:c0 + 64], rhs=Sb_[:, h, :], start=True, stop=True),
                             reads=[BqT, BSb], writes=[Bpq])
                    if CSTOP < 6:
                        return
                    r, Br = r_r.next(); qs, Bqs = qs_r.next(); vn, Bvn = vn_r.next(); vd, Bvd = vd_r.next()
                    CSKIP = os.environ.get("CSKIP", "")
                    for h in range(4):
                        if "r" in CSKIP:
                            continue
                        RVAR = os.environ.get("RVAR", "")
                        if RVAR == "1":
                            S.op("dve", lambda e, h=h: e.tensor_copy(out=r[sl, h, :], in_=pk[sl, h, :]), reads=[Bpk], writes=[Br])
                            continue
                        if RVAR == "2":
                            S.op("dve", lambda e, h=h: e.tensor_copy(out=r[sl, h, :], in_=bv[sl, h, :]), reads=[Bbv], writes=[Br])
                            continue
                        if RVAR == "3":
                            S.op("dve", lambda e, h=h: e.tensor_scalar(out=r[sl, h, :], in0=masks[sl, h, :], scalar1=sc[sl, NBEG, h:h + 1], scalar2=None, op0=ALU.mult), reads=[Bsc, BcC], writes=[Br])
                            continue
                        S.op("dve", lambda e, h=h: e.scalar_tensor_tensor(
                            out=r[sl, h, :], in0=pk[sl, h, :], scalar=sc[sl, NBEG, h:h + 1], in1=bv[sl, h, :],
                            op0=ALU.mult, op1=ALU.add), reads=[Bpk, Bsc, Bbv], writes=[Br])
                    for h in range(4):
                        if "q" in CSKIP:
                            continue
                        S.op("act", lambda e, h=h: e.activation(out=qs[sl, h, :], in_=pq[sl, h, :], func=AF.Copy, scale=sc[sl, EGC, h:h + 1]),
                             reads=[Bpq, Bsc], writes=[Bqs])
                    if CSTOP < 7:
                        return
                    pv, Bpv = pvo_r.next()
                    for h in range(4):
                        if "m" in CSKIP:
                            continue
                        S.op("pe", lambda e, h=h: e.matmul(pv[sl, h, :], lhsT=Y[sl, h, cp:cp + 64], rhs=r[sl, h, :], start=True, stop=True),
                             reads=[BY, Br], writes=[Bpv])
                    if "n" not in CSKIP:
                        S.op("dve", lambda e: e.tensor_copy(out=vn[sl, :, :], in_=pv[sl, :, :]), reads=[Bpv], writes=[Bvn])
                    for h in range(4):
                        if "d" in CSKIP:
                            continue
                        S.op("dve", lambda e, h=h: e.tensor_scalar(out=vd[sl, h, :], in0=pv[sl, h, :], scalar1=sc[sl, EDK, h:h + 1],
                                                                   scalar2=None, op0=ALU.mult), reads=[Bpv, Bsc], writes=[Bvd])
                    if CSTOP < 8:
                        return
                    po, Bpo = pvo_r.next()
                    for h in range(4):
                        S.op("pe", lambda e, h=h: e.matmul(po[sl, h, :], lhsT=iT[sl, h, cp:cp + 64], rhs=vn[sl, h, :], start=True, stop=True),
                             reads=[BiT, Bvn], writes=[Bpo])
                    S.op("dve", lambda e: e.tensor_tensor(out=ob[sl, :, :], in0=po[sl, :, :], in1=qs[sl, :, :], op=ALU.add),
                         reads=[Bpo, Bqs], writes=[Bob])
                    if CSTOP < 9:
                        return
                    pS, BpS = pS_r.next()
                    for h in range(4):
                        S.op("pe", lambda e, h=h: e.matmul(pS[:, h, :], lhsT=kks[sl, jb, h * 128:(h + 1) * 128], rhs=vd[sl, h, :],
                                                           start=True, stop=True), reads=[Bkk, Bvd], writes=[BpS])
                    for h in range(4):
                        S.op("dve", lambda e, h=h: e.scalar_tensor_tensor(
                            out=Sf[:, h, :], in0=Sf[:, h, :], scalar=eglb[:, h, ck:ck + 1], in1=pS[:, h, :],
                            op0=ALU.mult, op1=ALU.add), reads=[BSf, Beglb, BpS], writes=[BSf])
                    S.op("act", lambda e: e.copy(out=Sb_[:], in_=Sf[:]), reads=[BSf], writes=[BSb])

                Bosc = Buf("OSC")
                CDIR = int(os.environ.get("CDIR", "2"))
                CBLK = int(os.environ.get("CBLK", "99"))
                for d in range(CDIR):
                    def do_dir(d):
                        S.dma("sp", lambda e: e.dma_start(out=dg[:, 0:2], in_=dgate_d[d * 4:d * 4 + 4, :]), writes=[Bdg])
                        S.op("act", lambda e: e.activation(out=dg[:, 2:3], in_=dg[:, 0:1], func=AF.Exp), reads=[Bdg], writes=[Bdg])
                        S.op("dve", lambda e: e.tensor_scalar(out=dg[:, 2:3], in0=dg[:, 2:3], scalar1=-1.0, scalar2=None, op0=ALU.mult),
                             reads=[Bdg], writes=[Bdg])
                        S.op("pool", lambda e: e.memset(Sf[:], 0.0), writes=[BSf])
                        S.op("pool", lambda e: e.memset(Sb_[:], 0.0), writes=[BSb])
                        segs = range(NSEG) if d == 0 else range(NSEG - 1, -1, -1)
                        for sg in segs:
                            segdata = seg_prep(d, sg)
                            blks = range(SEG // 128) if d == 0 else range(SEG // 128 - 1, -1, -1)
                            for jb in list(blks)[:CBLK]:
                                block(d, sg, jb, segdata)
                    do_dir(d)


        def stage_D(b, wo, Bwo, don, BcD):
            T = seqs[b]
            o0 = offs[b]
            NTL = T // 128
            with ExitStack() as es:
                bc = sb("bcD", [128, D], F32, es)
                Bbc = Buf("bcD")
                S.dma("sp", lambda e: e.dma_start(out=bc[:], in_=MOD[b:b + 1, 2 * D:3 * D].partition_broadcast(128)), writes=[Bbc])
                at_r = Rot(es, nc, "atD", 2, [128, 512], BF16)
                os_r = Rot(es, nc, "osD", 2, [128, 4, 128], F32)
                zt_r = Rot(es, nc, "ztD", 2, [128, 4, 128], BF16)
                x_r = Rot(es, nc, "xD", 2, [128, D], F32)
                junk = sb("junkD", [128, 128], F32, es)
                ss_r = Rot(es, nc, "ssD", 2, [128, 4, 1], F32)
                on_r = Rot(es, nc, "onD", 2, [128, 4, 128], BF16)
                ct_r = Rot(es, nc, "ctD", 2, [128, 8, 128], BF16)
                ptr_r = Rot(es, nc, "ptrD", 2, [128, 8, 128], BF16, psum=True)
                po_r = Rot(es, nc, "poD", 4, [128, 512], F32, psum=True)
                t_r = Rot(es, nc, "tD", 2, [128, D], F32)
                xn_r = Rot(es, nc, "xnD", 2, [128, D], F32)

                def loads(i):
                    g = o0 + i * 128
                    at, Bat = at_r.next(); osc, Bos = os_r.next(); zt, Bzt = zt_r.next(); xt, Bxt = x_r.next()
                    S.dma("sp", lambda e: e.dma_start(out=at[:], in_=ATT[g:g + 128, :]), writes=[Bat])
                    S.dma("sp", lambda e: e.dma_start(out=osc[:], in_=OSC[g:g + 128, :].rearrange("p (h d) -> p h d", d=128)), writes=[Bos])
                    S.dma("sp", lambda e: e.dma_start(out=zt[:], in_=ZT[:, g:g + 128].rearrange("(h p) t -> p h t", p=128)), writes=[Bzt])
                    S.dma("sp", lambda e: e.dma_start(out=xt[:], in_=x_d[g + 1:g + 129, :]), writes=[Bxt])
                    return (at, Bat, osc, Bos, zt, Bzt, xt, Bxt)

                def tile(i, ld):
                    at, Bat, osc, Bos, zt, Bzt, xt, Bxt = ld
                    g = o0 + i * 128
                    ss, Bss = ss_r.next(); on, Bon = on_r.next(); ct, Bct = ct_r.next()
                    for h in range(4):
                        S.op("act", lambda e, h=h: e.activation(out=junk[:], in_=osc[:, h, :], func=AF.Square, accum_out=ss[:, h, :]),
                             reads=[Bos], writes=[Bss])
                    S.op("act", lambda e: e.activation(out=ss[:], in_=ss[:], func=AF.Sqrt, scale=1.0 / HD_D, bias=EPS), reads=[Bss], writes=[Bss])
                    S.op("dve", lambda e: e.reciprocal(out=ss[:], in_=ss[:]), reads=[Bss], writes=[Bss])
                    S.op("dve", lambda e: e.tensor_tensor(out=on[:], in0=osc[:], in1=ss[:].to_broadcast([128, 4, 128]), op=ALU.mult),
                         reads=[Bos, Bss], writes=[Bon])
                    ptr, Bptr = ptr_r.next()
                    for j in range(4):
                        S.op("pe", lambda e, j=j: e.transpose(out=ptr[:, j, :], in_=at[:, j * 128:(j + 1) * 128], identity=idb[:]),
                             reads=[Bat, Bconst], writes=[Bptr])
                    for h in range(4):
                        S.op("pe", lambda e, h=h: e.transpose(out=ptr[:, 4 + h, :], in_=on[:, h, :], identity=idb[:]),
                             reads=[Bon, Bconst], writes=[Bptr])
                    S.op("act", lambda e: e.copy(out=ct[:, 0:4, :], in_=ptr[:, 0:4, :]), reads=[Bptr], writes=[Bct])
                    S.op("dve", lambda e: e.scalar_tensor_tensor(out=ct[:, 4:8, :], in0=ptr[:, 4:8, :], scalar=don[:, 0:1], in1=zt[:],
                                                                 op0=ALU.mult, op1=ALU.mult), reads=[Bptr, Bzt, BcD], writes=[Bct])
                    xn, Bxn = xn_r.next()
                    tt, Btt = t_r.next()
                    for hf in range(2):
                        po, Bpo = po_r.next()
                        for kc in range(8):
                            S.op("pe", lambda e, kc=kc, po=po, hf=hf: e.matmul(
                                po[:], lhsT=ct[:, kc, :], rhs=wo[:, kc, hf * 512:(hf + 1) * 512], start=(kc == 0), stop=(kc == 7)),
                                reads=[Bct, Bwo], writes=[Bpo])
                        S.op("dve", lambda e, po=po, hf=hf: e.tensor_tensor(
                            out=tt[:, hf * 512:(hf + 1) * 512], in0=po[:], in1=bc[:, hf * 512:(hf + 1) * 512], op=ALU.mult),
                            reads=[Bpo, Bbc], writes=[Btt])
                    S.op("pool", lambda e: e.tensor_tensor(out=xn[:], in0=tt[:], in1=xt[:], op=ALU.add), reads=[Btt, Bxt], writes=[Bxn])
                    S.dma("pool", lambda e: e.dma_start(out=XN[g + 1:g + 129, :], in_=xn[:]), reads=[Bxn])

                nx = loads(0)
                for i in range(NTL):
                    cur = nx
                    if i + 1 < NTL:
                        nx = loads(i + 1)
                    tile(i, cur)

        def stage_E():
            with ExitStack() as es:
                wdn = sb("wdn", [128, 22, D], BF16, es)
                Bwdn = Buf("wdn")
                for c in range(22):
                    S.dma("pool", lambda e, c=c: e.dma_start(out=wdn[:, c, :], in_=wdn_d[c * 128:(c + 1) * 128, :]), writes=[Bwdn])
                fcw = sb("fcw", [128, 44, 3], F32, es)
                fcb = sb("fcb", [128, 44], F32, es)
                BcE = Buf("constE")
                S.dma("sp", lambda e: e.dma_start(out=fcw[:], in_=fcw_d), writes=[BcE])
                S.dma("sp", lambda e: e.dma_start(out=fcb[:], in_=fcb_d), writes=[BcE])
                wu_r = Rot(es, nc, "wuE", 3, [128, 2, 8, 128], BF16)
                Bwub = Buf("WUB")
                for c in range(22):
                    def cast_pair(c):
                        wu, Bwu = wu_r.next()
                        for ab in range(2):
                            col = ab * DFF + c * 128
                            S.dma("pool", lambda e, ab=ab, col=col: e.dma_start(
                                out=wu[:, ab, :, :], in_=wup_d[:, col:col + 128].rearrange("(kc p) n -> p kc n", p=128)), writes=[Bwu])
                        S.dma("sp", lambda e: e.dma_start(out=WUB[c], in_=wu[:]), reads=[Bwu], writes=[Bwub])
                    cast_pair(c)
                bcr = Rot(es, nc, "bcE", 1, [128, 3, D], F32)
                xtr = Rot(es, nc, "xtE", 2, [128, 4, D], F32)
                h2r = Rot(es, nc, "h2T", 2, [128, 8, 512], BF16)
                junk = sb("junkE", [128, D], F32, es)
                st_r = Rot(es, nc, "stE", 4, [128, 4], F32)
                hf_r = Rot(es, nc, "hfE", 1, [128, D], F32)
                hb_r = Rot(es, nc, "hbE", 2, [128, D], BF16)
                gt_r = Rot(es, nc, "gtE", 1, [128, 22, 512], BF16)
                ca_r = Rot(es, nc, "caE", 2, [128, 512], F32)
                cb_r = Rot(es, nc, "cbE", 2, [128, 512], F32)
                sa_r = Rot(es, nc, "saE", 2, [128, 512], F32)
                yt_r = Rot(es, nc, "ytE", 2, [128, D], F32)
                ptr_r = Rot(es, nc, "ptrE", 2, [128, 8, 128], BF16, psum=True)
                pab_r = Rot(es, nc, "pabE", 4, [128, 512], F32, psum=True)
                po_r = Rot(es, nc, "poE", 2, [128, 512], F32, psum=True)
                for t, B_ in xtr.t + gt_r.t:
                    S.op("pool", lambda e, t=t: e.memset(t[:], 0.0), writes=[B_])

                def load_x(b, t0, NT):
                    xt, Bxt = xtr.next()
                    NC = NT + 2
                    g0 = offs[b] + t0
                    for s in range(-(-NC // 128)):
                        rs = min(128, NC - s * 128)
                        S.dma("sp", lambda e, s=s, rs=rs: e.dma_start(
                            out=xt[:rs, s, :], in_=XN[g0 + s * 128: g0 + s * 128 + rs, :]), writes=[Bxt])
                    return xt, Bxt

                def do_tile(b, t0, NT, xt, Bxt, bc, Bbc):
                    NC = NT + 2
                    nsubc = -(-NC // 128)
                    h2T, Bh2 = h2r.next()
                    gt = offs[b] + t0

                    def norm_sub(s):
                        rs = min(128, NC - s * 128)
                        st, Bst = st_r.next(); hf, Bhf = hf_r.next(); hb, Bhb = hb_r.next(); ptr, Bptr = ptr_r.next()
                        S.op("act", lambda e: e.activation(out=junk[:rs, :], in_=xt[:rs, s, :], func=AF.Square, accum_out=st[:rs, 0:1]),
                             reads=[Bxt], writes=[Bst])
                        S.op("act", lambda e: e.activation(out=st[:rs, 1:2], in_=st[:rs, 0:1], func=AF.Sqrt, scale=1.0 / D, bias=EPS),
                             reads=[Bst], writes=[Bst])
                        S.op("dve", lambda e: e.reciprocal(out=st[:rs, 2:3], in_=st[:rs, 1:2]), reads=[Bst], writes=[Bst])
                        S.op("dve", lambda e: e.scalar_tensor_tensor(out=hf[:rs, :], in0=xt[:rs, s, :], scalar=st[:rs, 2:3], in1=bc[:rs, 0, :],
                                                                     op0=ALU.mult, op1=ALU.mult), reads=[Bxt, Bst, Bbc], writes=[Bhf])
                        S.op("pool", lambda e: e.tensor_tensor(out=hb[:rs, :], in0=hf[:rs, :], in1=bc[:rs, 1, :], op=ALU.add),
                             reads=[Bhf, Bbc], writes=[Bhb])
                        for kc in range(8):
                            S.op("pe", lambda e, kc=kc: e.transpose(out=ptr[:, kc, 0:rs], in_=hb[:rs, kc * 128:(kc + 1) * 128],
                                                                    identity=idb[:rs, :rs]), reads=[Bhb, Bconst], writes=[Bptr])
                        S.op("act", lambda e: e.copy(out=h2T[:, :, s * 128:s * 128 + rs], in_=ptr[:, :, 0:rs]), reads=[Bptr], writes=[Bh2])
                    for s in range(nsubc):
                        norm_sub(s)
                    if t0 == 0:
                        S.op("pool", lambda e: e.memset(h2T[:, :, 0:1], 0.0), writes=[Bh2])
                    if t0 + NT == seqs[b]:
                        S.op("pool", lambda e: e.memset(h2T[:, :, NC - 1:NC], 0.0), writes=[Bh2])

                    gtd, Bgt = gt_r.next()

                    def pair(c):
                        wu, Bwu = wu_r.next()
                        S.dma("sp", lambda e: e.dma_start(out=wu[:], in_=WUB[c]), reads=[Bwub], writes=[Bwu])

                        def half(ab, tr):
                            pp, Bpp = pab_r.next()
                            for kc in range(8):
                                S.op("pe", lambda e, kc=kc: e.matmul(pp[:, :NC], lhsT=wu[:, ab, kc, :], rhs=h2T[:, kc, 0:NC],
                                                                     start=(kc == 0), stop=(kc == 7)), reads=[Bwu, Bh2], writes=[Bpp])
                            cv, Bcv = tr.next()
                            ch = ab * 22 + c
                            S.op("act", lambda e: e.activation(out=cv[:, :NT], in_=pp[:, 1:1 + NT], func=AF.Identity,
                                                               scale=fcw[:, ch, 1:2], bias=fcb[:, ch:ch + 1]), reads=[Bpp, BcE], writes=[Bcv])
                            S.op("dve", lambda e: e.scalar_tensor_tensor(out=cv[:, :NT], in0=pp[:, 0:NT], scalar=fcw[:, ch, 0:1], in1=cv[:, :NT],
                                                                         op0=ALU.mult, op1=ALU.add), reads=[Bpp, BcE, Bcv], writes=[Bcv])
                            S.op("dve", lambda e: e.scalar_tensor_tensor(out=cv[:, :NT], in0=pp[:, 2:2 + NT], scalar=fcw[:, ch, 2:3], in1=cv[:, :NT],
                                                                         op0=ALU.mult, op1=ALU.add), reads=[Bpp, BcE, Bcv], writes=[Bcv])
                            return cv, Bcv
                        ca, Bca = half(0, ca_r)
                        cb2, Bcb2 = half(1, cb_r)
                        sa, Bsa = sa_r.next()
                        S.op("act", lambda e: e.activation(out=sa[:, :NT], in_=ca[:, :NT], func=AF.Silu), reads=[Bca], writes=[Bsa])
                        S.op("pool", lambda e: e.tensor_tensor(out=gtd[:, c, 1:1 + NT], in0=sa[:, :NT], in1=cb2[:, :NT], op=ALU.mult),
                             reads=[Bsa, Bcb2], writes=[Bgt])
                    for c in range(22):
                        pair(c)

                    def down(s):
                        rs = min(128, NC - s * 128)
                        yt, Byt = yt_r.next()
                        for hf_ in range(2):
                            def dhalf(hf_):
                                po, Bpo = po_r.next()
                                for c in range(22):
                                    S.op("pe", lambda e, c=c: e.matmul(po[:rs, :], lhsT=gtd[:, c, s * 128:s * 128 + rs],
                                                                       rhs=wdn[:, c, hf_ * 512:(hf_ + 1) * 512], start=(c == 0), stop=(c == 21)),
                                         reads=[Bgt, Bwdn], writes=[Bpo])
                                S.op("dve", lambda e: e.tensor_tensor(out=yt[:rs, hf_ * 512:(hf_ + 1) * 512], in0=po[:rs, :],
                                                                      in1=bc[:rs, 2, hf_ * 512:(hf_ + 1) * 512], op=ALU.mult),
                                     reads=[Bpo, Bbc], writes=[Byt])
                            dhalf(hf_)
                        S.op("pool", lambda e: e.tensor_tensor(out=yt[:rs, :], in0=yt[:rs, :], in1=xt[:rs, s, :], op=ALU.add),
                             reads=[Byt, Bxt], writes=[Byt])
                        p_lo = 1 if s == 0 else 0
                        p_hi = min(rs, NT + 1 - s * 128)
                        if p_hi > p_lo:
                            tok = gt - 1 + s * 128
                            S.dma("pool", lambda e: e.dma_start(out=y_d[tok + p_lo:tok + p_hi, :], in_=yt[p_lo:p_hi, :]), reads=[Byt])
                    for s in range(nsubc):
                        down(s)

                work = [(b, t0, NT) for b in range(NSEQ) for (t0, NT) in tiles_of(seqs[b])]
                nxt = load_x(*work[0])
                cur_b = -1
                bcs = None
                for wi, (b, t0, NT) in enumerate(work):
                    xt, Bxt = nxt
                    if wi + 1 < len(work):
                        nxt = load_x(*work[wi + 1])
                    if b != cur_b:
                        cur_b = b
                        bcs = bcr.next()
                        load_bcast(bcs[0], bcs[1], b, (4, 3, 5))
                    do_tile(b, t0, NT, xt, Bxt, bcs[0], bcs[1])

        if "A" in stages:
            with ExitStack() as es:
                win = sb("win", [128, 8, INC], BF16, es)
                Bwin = Buf("win")
                for kc in range(8):
                    S.dma("pool", lambda e, kc=kc: e.dma_start(out=win[:, kc, :], in_=win_d[kc * 128:(kc + 1) * 128, :]),
                          writes=[Bwin])
                qkn = sb("qkn", [128, 2], F32, es)
                dcw = sb("dcw", [128, 12, 3], F32, es)
                BcA = Buf("constA")
                S.dma("sp", lambda e: e.dma_start(out=qkn[:], in_=qkn_d), writes=[BcA])
                S.dma("sp", lambda e: e.dma_start(out=dcw[:], in_=dcw_d), writes=[BcA])
                S.op("dve", lambda e: e.tensor_scalar(out=qkn[:, 0:1], in0=qkn[:, 0:1], scalar1=HD_A ** -0.5, scalar2=None,
                                                      op0=ALU.mult), reads=[BcA], writes=[BcA])
                bcr = Rot(es, nc, "bcA", 2, [128, 2, D], F32)
                xtr = Rot(es, nc, "xt", 2, [128, 4, D], F32)
                h1r = Rot(es, nc, "h1T", 2, [128, 8, 512], BF16)
                junk = sb("junkA", [128, D], F32, es)
                st_r = Rot(es, nc, "stA", 4, [128, 4], F32)
                hf_r = Rot(es, nc, "hfA", 2, [128, D], F32)
                hb_r = Rot(es, nc, "hbA", 2, [128, D], BF16)
                ptr_r = Rot(es, nc, "ptrA", 2, [128, 8, 128], BF16, psum=True)
                pp_r = Rot(es, nc, "ppA", 3, [128, 512], F32, psum=True)
                pss_r = Rot(es, nc, "pssA", 2, [128, 512], F32, psum=True)
                sqb_r = Rot(es, nc, "sqbA", 2, [128, 512], BF16)
                rs1_r = Rot(es, nc, "rs1A", 2, [128, 512], F32)
                rs2_r = Rot(es, nc, "rs2A", 2, [128, 512], F32)
                ob_r = Rot(es, nc, "obA", 4, [128, 512], BF16)
                cb_r = Rot(es, nc, "cbA", 2, [128, 512], F32)
                sl_r = Rot(es, nc, "slA", 2, [128, 512], F32)
                tok_r = Rot(es, nc, "tokA", 3, [128, 4, 512], BF16)
                gs_r = Rot(es, nc, "gsA", 2, [16, 512], F32)
                for t, _ in xtr.t:
                    S.op("pool", lambda e, t=t: e.memset(t[:], 0.0), writes=[_])

                def load_x(b, t0, NT):
                    xt, Bxt = xtr.next()
                    NC = NT + 2
                    g0 = offs[b] + t0
                    for s in range(-(-NC // 128)):
                        rs = min(128, NC - s * 128)
                        S.dma("sp", lambda e, xt=xt, s=s, rs=rs, g0=g0: e.dma_start(
                            out=xt[:rs, s, :], in_=x_d[g0 + s * 128: g0 + s * 128 + rs, :]), writes=[Bxt])
                    return xt, Bxt

                def do_tile(b, t0, NT, xt, Bxt, bc, Bbc):
                    NC = NT + 2
                    h1T, Bh1 = h1r.next()
                    gt = offs[b] + t0
                    nsub = -(-NT // 128)

                    def norm_sub(s):
                        rs = min(128, NC - s * 128)
                        st, Bst = st_r.next()
                        hf, Bhf = hf_r.next()
                        hb, Bhb = hb_r.next()
                        ptr, Bptr = ptr_r.next()
                        S.op("act", lambda e: e.activation(
                            out=junk[:rs, :], in_=xt[:rs, s, :], func=AF.Square, accum_out=st[:rs, 0:1]),
                            reads=[Bxt], writes=[Bst])
                        S.op("act", lambda e: e.activation(
                            out=st[:rs, 1:2], in_=st[:rs, 0:1], func=AF.Sqrt, scale=1.0 / D, bias=EPS),
                            reads=[Bst], writes=[Bst])
                        S.op("dve", lambda e: e.reciprocal(out=st[:rs, 2:3], in_=st[:rs, 1:2]),
                             reads=[Bst], writes=[Bst])
                        S.op("dve", lambda e: e.scalar_tensor_tensor(
                            out=hf[:rs, :], in0=xt[:rs, s, :], scalar=st[:rs, 2:3], in1=bc[:rs, 0, :],
                            op0=ALU.mult, op1=ALU.mult), reads=[Bxt, Bst, Bbc], writes=[Bhf])
                        S.op("pool", lambda e: e.tensor_tensor(
                            out=hb[:rs, :], in0=hf[:rs, :], in1=bc[:rs, 1, :], op=ALU.add),
                            reads=[Bhf, Bbc], writes=[Bhb])
                        for kc in range(8):
                            S.op("pe", lambda e, kc=kc: e.transpose(
                                out=ptr[:, kc, 0:rs], in_=hb[:rs, kc * 128:(kc + 1) * 128], identity=idb[:rs, :rs]),
                                reads=[Bhb, Bconst], writes=[Bptr])
                        S.op("act", lambda e: e.copy(
                            out=h1T[:, :, s * 128:s * 128 + rs], in_=ptr[:, :, 0:rs]), reads=[Bptr], writes=[Bh1])

                    for s in range(-(-NC // 128)):
                        norm_sub(s)
                    if t0 == 0:
                        S.op("pool", lambda e: e.memset(h1T[:, :, 0:1], 0.0), writes=[Bh1])
                    if t0 + NT == seqs[b]:
                        S.op("pool", lambda e: e.memset(h1T[:, :, NC - 1:NC], 0.0), writes=[Bh1])

                    def proj(c0, ncols):
                        pp, Bpp = pp_r.next()
                        for kc in range(8):
                            S.op("pe", lambda e, kc=kc: e.matmul(
                                pp[:ncols, :NC], lhsT=win[:, kc, c0:c0 + ncols], rhs=h1T[:, kc, 0:NC],
                                start=(kc == 0), stop=(kc == 7)), reads=[Bwin, Bh1], writes=[Bpp])
                        return pp, Bpp

                    def inv_norm(src, Bsrc, lhs, scale):
                        sqb, Bsq = sqb_r.next()
                        pss, Bps = pss_r.next()
                        rs1, Br1 = rs1_r.next()
                        rs2, Br2 = rs2_r.next()
                        S.op("act", lambda e: e.activation(out=sqb[:, :NT], in_=src, func=AF.Square),
                             reads=[Bsrc], writes=[Bsq])
                        S.op("pe", lambda e: e.matmul(pss[:, :NT], lhsT=lhs[:], rhs=sqb[:, :NT], start=True, stop=True),
                             reads=[Bsq, Bconst], writes=[Bps])
                        S.op("act", lambda e: e.activation(out=rs1[:, :NT], in_=pss[:, :NT], func=AF.Sqrt,
                                                           scale=scale, bias=EPS), reads=[Bps], writes=[Br1])
                        S.op("dve", lambda e: e.reciprocal(out=rs2[:, :NT], in_=rs1[:, :NT]), reads=[Br1], writes=[Br2])
                        return rs2, Br2

                    def to_tok(src, Bsrc, tk, Btk, col):
                        def one(s):
                            rs = min(128, NT - s * 128)
                            ptr, Bptr = ptr_r.next()
                            S.op("pe", lambda e: e.transpose(
                                out=ptr[:rs, 0, :], in_=src[:, s * 128:s * 128 + rs], identity=idb[:]),
                                reads=[Bsrc, Bconst], writes=[Bptr])
                            S.op("act", lambda e: e.copy(
                                out=tk[:rs, s, col:col + 128], in_=ptr[:rs, 0, :]), reads=[Bptr], writes=[Btk])
                        for s in range(nsub):
                            one(s)

                    def store_tok(dst, tk, Btk):
                        for s in range(nsub):
                            rs = min(128, NT - s * 128)
                            S.dma("pool", lambda e, s=s, rs=rs: e.dma_start(
                                out=dst[gt + s * 128: gt + s * 128 + rs, :], in_=tk[:rs, s, :]), reads=[Btk])

                    def qk_chunk(cb):
                        pp, Bpp = proj(cb * 128, 128)
                        rs2, Br2 = inv_norm(pp[:, 1:1 + NT], Bpp, blk2, 1.0 / HD_A)
                        ob, Bob = ob_r.next()
                        j = 0 if cb < 4 else 1
                        S.op("dve", lambda e: e.scalar_tensor_tensor(
                            out=ob[:, :NT], in0=pp[:, 1:1 + NT], scalar=qkn[:, j:j + 1], in1=rs2[:, :NT],
                            op0=ALU.mult, op1=ALU.mult), reads=[Bpp, Br2, BcA], writes=[Bob])
                        dst = QT if cb < 4 else KT
                        r0 = (cb % 4) * 128
                        S.dma("pool", lambda e: e.dma_start(
                            out=dst[r0:r0 + 128, gt:gt + NT], in_=ob[:, :NT]), reads=[Bob])
                    for cb in range(8):
                        qk_chunk(cb)

                    tk, Btk = tok_r.next()
                    def v_chunk(j):
                        pp, Bpp = proj(1024 + j * 128, 128)
                        ob, Bob = ob_r.next()
                        S.op("act", lambda e: e.copy(out=ob[:, :NT], in_=pp[:, 1:1 + NT]),
                             reads=[Bpp], writes=[Bob])
                        to_tok(ob, Bob, tk, Btk, j * 128)
                    for j in range(4):
                        v_chunk(j)
                    store_tok(VX, tk, Btk)

                    tkk, Btkk = tok_r.next()
                    tkv, Btkv = tok_r.next()
                    def dn_chunk(jb):
                        pp, Bpp = proj(1536 + jb * 128, 128)
                        cbf, Bcb = cb_r.next()
                        sl, Bsl = sl_r.next()
                        S.op("act", lambda e: e.activation(
                            out=cbf[:, :NT], in_=pp[:, 1:1 + NT], func=AF.Copy, scale=dcw[:, jb, 1:2]),
                            reads=[Bpp, BcA], writes=[Bcb])
                        S.op("dve", lambda e: e.scalar_tensor_tensor(
                            out=cbf[:, :NT], in0=pp[:, 0:NT], scalar=dcw[:, jb, 0:1], in1=cbf[:, :NT],
                            op0=ALU.mult, op1=ALU.add), reads=[Bpp, BcA, Bcb], writes=[Bcb])
                        S.op("dve", lambda e: e.scalar_tensor_tensor(
                            out=cbf[:, :NT], in0=pp[:, 2:2 + NT], scalar=dcw[:, jb, 2:3], in1=cbf[:, :NT],
                            op0=ALU.mult, op1=ALU.add), reads=[Bpp, BcA, Bcb], writes=[Bcb])
                        S.op("act", lambda e: e.activation(out=sl[:, :NT], in_=cbf[:, :NT], func=AF.Silu),
                             reads=[Bcb], writes=[Bsl])
                        ob, Bob = ob_r.next()
                        hh = jb % 4
                        if jb < 8:
                            rs2, Br2 = inv_norm(sl[:, :NT], Bsl, ones_b, 1.0)
                            S.op("dve", lambda e: e.scalar_tensor_tensor(
                                out=ob[:, :NT], in0=sl[:, :NT], scalar=(HD_D ** -0.5 if jb < 4 else 1.0), in1=rs2[:, :NT],
                                op0=ALU.mult, op1=ALU.mult), reads=[Bsl, Br2], writes=[Bob])
                            dst = DQT if jb < 4 else DKT
                            S.dma("pool", lambda e: e.dma_start(
                                out=dst[hh * 128:(hh + 1) * 128, gt:gt + NT], in_=ob[:, :NT]), reads=[Bob])
                            if jb >= 4:
                                to_tok(ob, Bob, tkk, Btkk, hh * 128)
                        else:
                            S.op("pool", lambda e: e.tensor_copy(out=ob[:, :NT], in_=sl[:, :NT]),
                                 reads=[Bsl], writes=[Bob])
                            to_tok(ob, Bob, tkv, Btkv, hh * 128)
                    for jb in range(12):
                        dn_chunk(jb)
                    store_tok(DKK, tkk, Btkk)
                    store_tok(DVV, tkv, Btkv)

                    def z_chunk(j):
                        pp, Bpp = proj(3072 + j * 128, 128)
                        ob, Bob = ob_r.next()
                        S.op("act", lambda e: e.activation(out=ob[:, :NT], in_=pp[:, 1:1 + NT], func=AF.Silu),
                             reads=[Bpp], writes=[Bob])
                        S.dma("pool", lambda e: e.dma_start(
                            out=ZT[j * 128:(j + 1) * 128, gt:gt + NT], in_=ob[:, :NT]), reads=[Bob])
                    for j in range(4):
                        z_chunk(j)

                    pp, Bpp = proj(3584, 16)
                    gs, Bgs = gs_r.next()
                    S.op("act", lambda e: e.copy(out=gs[:, :NT], in_=pp[:16, 1:1 + NT]),
                         reads=[Bpp], writes=[Bgs])
                    S.dma("pool", lambda e: e.dma_start(out=GR[:, gt:gt + NT], in_=gs[:, :NT]), reads=[Bgs])

                work = [(b, t0, NT) for b in range(NSEQ) for (t0, NT) in tiles_of(seqs[b])]
                nxt = load_x(*work[0])
                cur_b = -1
                bcs = None
                for wi, (b, t0, NT) in enumerate(work):
                    xt, Bxt = nxt
                    if wi + 1 < len(work):
                        nxt = load_x(*work[wi + 1])
                    if b != cur_b:
                        cur_b = b
                        bcs = bcr.next()
                        load_bcast(bcs[0], bcs[1], b, (1, 0))
                    do_tile(b, t0, NT, xt, Bxt, bcs[0], bcs[1])
            S.barrier()

        if any(c in stages for c in "BCD"):
            with ExitStack() as esM:
                rbt, BcM = None, None
                wo = sb("wo", [128, 8, D], BF16, esM)
                don = sb("don", [128, 1], F32, esM)
                Bwo, BcD = Buf("wo"), Buf("constD")
                if "D" in stages:
                    zrow = sb("zrow", [1, D], F32, esM)
                    Bz = Buf("zrow")
                    S.op("pool", lambda e: e.memset(zrow[:], 0.0), writes=[Bz])
                    S.dma("pool", lambda e: e.dma_start(out=XN[0:1, :], in_=zrow[:]), reads=[Bz])
                    S.dma("pool", lambda e: e.dma_start(out=XN[NTOK + 1:NTOK + 2, :], in_=zrow[:]), reads=[Bz])
                    for kc in range(8):
                        S.dma("pool", lambda e, kc=kc: e.dma_start(out=wo[:, kc, :], in_=wo_d[kc * 128:(kc + 1) * 128, :]), writes=[Bwo])
                    S.dma("sp", lambda e: e.dma_start(out=don[:], in_=don_d), writes=[BcD])
                for b in range(NSEQ):
                    if "B" in stages:
                        stage_B(b, rbt, BcM)
                        S.barrier()
                    if "C" in stages:
                        stage_C(b)
                        S.barrier()
                    if "D" in stages:
                        stage_D(b, wo, Bwo, don, BcD)
                        S.barrier()

        if "E" in stages:
            stage_E()

        S.emit(nc)
    return nc


def _consts(nseq):
    ident = np.eye(128, dtype=np.float32)
    blk2 = np.zeros((128, 128), np.float32)
    blk2[:64, :64] = 1.0
    blk2[64:, 64:] = 1.0
    i = np.arange(128)[:, None]
    j = np.arange(128)[None, :]
    same = (i // 64) == (j // 64)
    mk = [same & (j < i), same & (j > i), same & (j >= i), same & (j <= i)]
    masks = np.stack([np.where(m, np.float32(0.0), np.float32(NEG)) for m in mk], axis=1).astype(np.float32)
    m01 = np.ones((4, 512), np.float32)
    m01[:, ::64] = 0.0
    mk7 = [(i // 8) == (j // 8)]
    for m_ in (8, 16, 32):
        mk7.append(((i // (2 * m_)) == (j // (2 * m_))) & ((i // m_) % 2 == 1) & ((j // m_) % 2 == 0))
    for m_ in (8, 16, 32):
        mk7.append(((i // (2 * m_)) == (j // (2 * m_))) & ((i // m_) % 2 == 0) & ((j // m_) % 2 == 1))
    mk7 = np.ascontiguousarray(np.stack(mk7, axis=1).astype(np.float32))
    return {"ident": ident, "blk2": blk2, "masks": np.ascontiguousarray(masks), "m01": m01, "mk7": mk7}


def _rb_table(att_rpb):
    rpb = np.asarray(att_rpb, np.float32)
    krl = np.arange(2)[:, None, None, None]
    kc = np.arange(64)[None, :, None, None]
    e = np.arange(16)[None, None, :, None]
    qc = np.arange(64)[None, None, None, :]
    dr = krl + 7 - e
    w = np.clip(qc - 8, 0, 48)
    ok = (dr >= -7) & (dr <= 7) & (kc >= w) & (kc < w + 16)
    ri = np.clip(dr + 7, 0, 14)
    ci = np.clip(kc - qc + 15, 0, 30)
    ri, ci, ok = np.broadcast_arrays(ri, ci, ok)
    tab = rpb[:, ri, ci]
    tab = np.where(ok[None], tab, np.float32(NEG)).astype(np.float32)
    return np.ascontiguousarray(tab.transpose(1, 2, 0, 3, 4).reshape(128, 8, 16, 64))


def shared_inputs(nseq, ada_w, ada_b, norm1_g, norm2_g, w_in, att_q_norm, att_k_norm, att_rpb, dn_conv_w,
                  dn_a_log, dn_dt_bias, dn_out_norm, w_o, ffn_w_up, ffn_conv_w, ffn_conv_b, ffn_w_down):
    f = lambda a: np.ascontiguousarray(np.asarray(a, np.float32))
    m = dict(_consts(nseq))
    m["ada_w"] = f(ada_w[0])
    m["ada_b"] = f(np.broadcast_to(np.asarray(ada_b[0])[None, :], (nseq, 6 * D)))
    m["n12"] = f(np.broadcast_to(np.stack([np.asarray(norm1_g[0]), np.asarray(norm2_g[0])])[None], (nseq, 2, D)))
    m["w_in"] = f(w_in[0])
    m["qkn"] = f(np.stack([np.tile(np.asarray(att_q_norm[0]), 2), np.tile(np.asarray(att_k_norm[0]), 2)], axis=1))
    m["rb"] = _rb_table(att_rpb[0])
    m["dcw"] = f(np.asarray(dn_conv_w[0]).reshape(3, 12, 128).transpose(2, 1, 0))
    m["dgate"] = f(np.stack([np.asarray(dn_a_log[0]).reshape(8), np.asarray(dn_dt_bias[0]).reshape(8)], axis=1))
    m["don"] = f(np.asarray(dn_out_norm[0]).reshape(128, 1))
    m["w_o"] = f(w_o[0])
    m["w_up"] = f(ffn_w_up[0])
    m["fcw"] = f(np.asarray(ffn_conv_w[0]).reshape(3, 44, 128).transpose(2, 1, 0))
    m["fcb"] = f(np.asarray(ffn_conv_b[0]).reshape(44, 128).T)
    m["w_dn"] = f(ffn_w_down[0])
    return m


def core_inputs(shared, xs, cs):
    m = dict(shared)
    x = np.concatenate([np.zeros((1, D), np.float32)] + [np.asarray(a, np.float32) for a in xs]
                       + [np.zeros((1, D), np.float32)], axis=0)
    m["x"] = np.ascontiguousarray(x)
    c = np.stack([np.asarray(a, np.float32) for a in cs])
    m["cT"] = np.ascontiguousarray(c.T.reshape(8, 128, len(cs)).transpose(1, 0, 2))
    return m


N_CORES = 8


def kernel(x_prompt, x_sample, c_prompt, c_sample, ada_w, ada_b, norm1_g, norm2_g, w_in,
           att_q_norm, att_k_norm, att_rpb, dn_conv_w, dn_a_log, dn_dt_bias, dn_out_norm, w_o,
           ffn_w_up, ffn_conv_w, ffn_conv_b, ffn_w_down):
    x_prompt = np.asarray(x_prompt, np.float32)
    x_sample = np.asarray(x_sample, np.float32)
    c_prompt = np.asarray(c_prompt, np.float32)
    c_sample = np.asarray(c_sample, np.float32)
    BP, TP, _ = x_prompt.shape
    BS, TS, _ = x_sample.shape
    ppc = BP // N_CORES
    seqs = [TP] * ppc + [TS]
    shared = shared_inputs(len(seqs), ada_w, ada_b, norm1_g, norm2_g, w_in, att_q_norm, att_k_norm, att_rpb,
                           dn_conv_w, dn_a_log, dn_dt_bias, dn_out_norm, w_o, ffn_w_up, ffn_conv_w, ffn_conv_b,
                           ffn_w_down)
    in_maps = []
    for c in range(N_CORES):
        xs = [x_prompt[c * ppc + i] for i in range(ppc)] + [x_sample[c * BS // N_CORES]]
        cs = [c_prompt[c * ppc + i] for i in range(ppc)] + [c_sample[c * BS // N_CORES]]
        in_maps.append(core_inputs(shared, xs, cs))
    nc = build(seqs)
    res = run_bass_kernel_spmd(nc, in_maps, core_ids=list(range(N_CORES)))
    y_prompt = np.empty_like(x_prompt)
    y_sample = np.empty_like(x_sample)
    half = TS // 2
    for c in range(N_CORES):
        y = np.asarray(res.results[c]["y"], np.float32)
        y_prompt[c * ppc:(c + 1) * ppc] = y[:ppc * TP].reshape(ppc, TP, D)
        sidx = c * BS // N_CORES
        ys = y[ppc * TP:].reshape(TS, D)
        if c % 2 == 0:
            y_sample[sidx, :half] = ys[:half]
        else:
            y_sample[sidx, half:] = ys[half:]
    return (y_prompt, y_sample)
```

```python
import math
from contextlib import ExitStack

import numpy as np
import concourse.bass as bass
import concourse.mybir as mybir
from concourse.bass_utils import run_bass_kernel_spmd

F32 = mybir.dt.float32
BF16 = mybir.dt.bfloat16
AF = mybir.ActivationFunctionType
ALU = mybir.AluOpType

D = 1024
GW = 64
NH_A, HD_A = 8, 64
NH_D, HD_D = 4, 128
DFF = 2816
INC = 3600
EPS = 1e-6
NEG = -30000.0

ENGS = ("pe", "act", "dve", "pool", "sp")
NDMA = 8


class Buf:
    __slots__ = ("name", "lw", "rd")

    def __init__(self, name):
        self.name = name
        self.lw = None
        self.rd = []


class Sched:
    def __init__(self):
        self.q = {e: [] for e in ENGS}
        self.cnt = {e: 0 for e in ENGS}
        self.known = {e: {} for e in ENGS}
        self.dma_n = {e: 0 for e in ENGS}
        self.dma_slot_val = {}
        self.pending = {e: {} for e in ENGS}
        self.nops = 0

    def _deps(self, eng, reads, writes):
        deps = dict(self.pending[eng])
        self.pending[eng] = {}

        def add(ev):
            if ev is None:
                return
            k, v = ev
            if deps.get(k, 0) < v:
                deps[k] = v
        for b in reads:
            add(b.lw)
        for b in writes:
            add(b.lw)
            for r in b.rd:
                add(r)
        return deps

    def _finish(self, eng, fn, deps, ev, reads, writes, inc):
        kn = self.known[eng]
        waits = []
        for k, v in deps.items():
            if kn.get(k, 0) >= v:
                continue
            if eng == "pe" and k == "pe":
                continue
            kn[k] = v
            waits.append((k, v))
        self.q[eng].append((fn, waits, inc))
        self.nops += 1
        for b in reads:
            b.rd.append(ev)
        for b in writes:
            b.lw = ev
            b.rd = []

    def op(self, eng, fn, reads=(), writes=()):
        deps = self._deps(eng, reads, writes)
        self.cnt[eng] += 1
        ev = (eng, self.cnt[eng])
        self._finish(eng, fn, deps, ev, reads, writes, (eng, 1))
        return ev

    def dma(self, queue, fn, reads=(), writes=()):
        deps = self._deps(queue, reads, writes)
        n = self.dma_n[queue]
        self.dma_n[queue] = n + 1
        key = ("dma", queue, n % NDMA)
        prev = self.dma_slot_val.get(key, 0)
        if prev and deps.get(key, 0) < prev:
            deps[key] = prev
        val = prev + 16
        self.dma_slot_val[key] = val
        ev = (key, val)
        self._finish(queue, fn, deps, ev, reads, writes, (key, 16))
        return ev

    def all_events(self):
        evs = {}
        for e in ENGS:
            if self.cnt[e]:
                evs[e] = self.cnt[e]
        for k, v in self.dma_slot_val.items():
            evs[k] = v
        return evs

    def barrier(self):
        evs = self.all_events()
        for e in ENGS:
            p = self.pending[e]
            for k, v in evs.items():
                if p.get(k, 0) < v:
                    p[k] = v

    def emit(self, nc):
        final = self.all_events()
        with ExitStack() as es:
            sems = {}
            for e in ENGS:
                sems[e] = es.enter_context(nc.semaphore("s_" + e))
            for q in ("sp", "pool"):
                for s in range(NDMA):
                    sems[("dma", q, s)] = es.enter_context(nc.semaphore("d_%s%d" % (q, s)))
            block = es.enter_context(nc.Block())

            def run(name):
                def body(eng):
                    for fn, waits, inc in self.q[name]:
                        for k, v in waits:
                            eng.wait_ge(sems[k], v)
                        fn(eng).then_inc(sems[inc[0]], inc[1])
                    if name == "sp":
                        for k, v in final.items():
                            eng.wait_ge(sems[k], v)
                return body

            block.tensor(run("pe"))
            block.scalar(run("act"))
            block.vector(run("dve"))
            block.gpsimd(run("pool"))
            block.sync(run("sp"))


_UID = [0]


class Rot:
    def __init__(self, es, nc, name, n, shape, dt, psum=False):
        self.t = []
        for i in range(n):
            _UID[0] += 1
            nm = "r_%s%d_%d" % (name, i, _UID[0])
            if psum:
                t = es.enter_context(nc.psum_tensor(nm, shape, dt))
            else:
                t = es.enter_context(nc.sbuf_tensor(nm, shape, dt))
            self.t.append((t, Buf(nm)))
        self.i = 0

    def next(self):
        r = self.t[self.i % len(self.t)]
        self.i += 1
        return r


def interleave(gens, width):
    it = iter(gens)
    active = []
    while True:
        while len(active) < width:
            g = next(it, None)
            if g is None:
                break
            active.append(g)
        if not active:
            break
        for g in list(active):
            try:
                next(g)
            except StopIteration:
                active.remove(g)


def tiles_of(T, mx=510):
    n = -(-T // mx)
    base, rem = divmod(T, n)
    out, t0 = [], 0
    for i in range(n):
        s = base + (1 if i < rem else 0)
        out.append((t0, s))
        t0 += s
    return out


def build(seqs, stages="0ABCDE", debug=False):
    NSEQ = len(seqs)
    NTOK = sum(seqs)
    offs = [sum(seqs[:i]) for i in range(NSEQ)]
    nc = bass.Bass("TRN2", target_bir_lowering=False)
    S = Sched()
    kind_dbg = "ExternalOutput" if debug else "Internal"

    def din(name, shape, dt=F32):
        return nc.dram_tensor(name, list(shape), dt, kind="ExternalInput").ap()

    def dscr(name, shape, dt):
        return nc.dram_tensor(name, list(shape), dt, kind=kind_dbg).ap()

    dumped = set()

    def dump(name, ap, B, dt=F32):
        if not debug or name in dumped:
            return
        dumped.add(name)
        t = nc.dram_tensor("dbg_" + name, list(ap.shape), dt, kind="ExternalOutput").ap()
        S.dma("pool", lambda e: e.dma_start(out=t, in_=ap), reads=[B])

    x_d = din("x", [NTOK + 2, D])
    cT_d = din("cT", [128, 8, NSEQ])
    adaw_d = din("ada_w", [D, 6 * D])
    adab_d = din("ada_b", [NSEQ, 6 * D])
    n12_d = din("n12", [NSEQ, 2, D])
    win_d = din("w_in", [D, INC])
    qkn_d = din("qkn", [128, 2])
    rb_d = din("rb", [128, NH_A, 16, GW])
    dcw_d = din("dcw", [128, 12, 3])
    dgate_d = din("dgate", [8, 2])
    don_d = din("don", [128, 1])
    wo_d = din("w_o", [D, D])
    wup_d = din("w_up", [D, 2 * DFF])
    fcw_d = din("fcw", [128, 44, 3])
    fcb_d = din("fcb", [128, 44])
    wdn_d = din("w_dn", [DFF, D])
    ident_d = din("ident", [128, 128])
    blk2_d = din("blk2", [128, 128])
    masks_d = din("masks", [128, 4, 128])
    m01_d = din("m01", [4, 512])
    mk7_d = din("mk7", [128, 7, 128])
    y_d = nc.dram_tensor("y", [NTOK, D], F32, kind="ExternalOutput").ap()

    QT = dscr("QT", [512, NTOK], BF16)
    KT = dscr("KT", [512, NTOK], BF16)
    VX = dscr("VX", [NTOK, 512], BF16)
    DQT = dscr("DQT", [512, NTOK], BF16)
    DKT = dscr("DKT", [512, NTOK], BF16)
    DKK = dscr("DKK", [NTOK, 512], BF16)
    DVV = dscr("DVV", [NTOK, 512], BF16)
    ZT = dscr("ZT", [512, NTOK], BF16)
    GR = dscr("GR", [16, NTOK], F32)
    MOD = dscr("MOD", [NSEQ, 6 * D], F32)
    ATT = dscr("ATT", [NTOK, 512], BF16)
    OSC = dscr("OSC", [NTOK, 512], F32)
    BCS = dscr("BCS", [24, 512], F32)
    OSB = dscr("OSB", [NTOK, 512], F32)
    XN = dscr("XN", [NTOK + 2, D], F32)
    WUB = dscr("WUB", [22, 128, 2, 8, 128], BF16)

    with ExitStack() as es0:
        def sb(name, shape, dt, es=es0):
            _UID[0] += 1
            return es.enter_context(nc.sbuf_tensor("s_%s_%d" % (name, _UID[0]), list(shape), dt))

        idf = sb("idf", [128, 128], F32)
        idb = sb("idb", [128, 128], BF16)
        blk2 = sb("blk2", [128, 128], BF16)
        ones_b = sb("ones_b", [128, 128], BF16)
        Bconst = Buf("const")
        S.dma("sp", lambda e: e.dma_start(out=idf[:], in_=ident_d), writes=[Bconst])
        S.dma("pool", lambda e: e.dma_start(out=blk2[:], in_=blk2_d), writes=[Bconst])
        S.op("dve", lambda e: e.tensor_copy(out=idb[:], in_=idf[:]), reads=[Bconst], writes=[Bconst])
        S.op("pool", lambda e: e.memset(ones_b[:], 1.0), writes=[Bconst])

        if "0" in stages:
            with ExitStack() as es:
                cT = sb("cT", [128, 8, NSEQ], F32, es)
                BcT = Buf("cT")
                modr = sb("modr", [NSEQ, 6 * D], F32, es)
                Bmod = Buf("modr")
                adab = sb("adab", [NSEQ, 6 * D], F32, es)
                Badab = Buf("adab")
                n12 = sb("n12", [NSEQ, 2, D], F32, es)
                Bn12 = Buf("n12")
                awr = Rot(es, nc, "aw", 2, [128, 8, 512], F32)
                pmod = Rot(es, nc, "pmod", 2, [NSEQ, 512], F32, psum=True)
                S.dma("sp", lambda e: e.dma_start(out=cT[:], in_=cT_d), writes=[BcT])
                S.dma("sp", lambda e: e.dma_start(out=adab[:], in_=adab_d), writes=[Badab])
                S.dma("sp", lambda e: e.dma_start(out=n12[:], in_=n12_d), writes=[Bn12])
                S.op("act", lambda e: e.activation(out=cT[:], in_=cT[:], func=AF.Silu), reads=[BcT], writes=[BcT])
                for nb in range(12):
                    aw, Baw = awr.next()
                    S.dma("sp", lambda e, aw=aw, nb=nb: e.dma_start(
                        out=aw[:], in_=adaw_d[:, nb * 512:(nb + 1) * 512].rearrange("(kc p) n -> p kc n", p=128)),
                        writes=[Baw])
                    pm, Bpm = pmod.next()
                    for kc in range(8):
                        S.op("pe", lambda e, pm=pm, aw=aw, kc=kc: e.matmul(
                            pm[:], lhsT=cT[:, kc, :], rhs=aw[:, kc, :], start=(kc == 0), stop=(kc == 7)),
                            reads=[BcT, Baw], writes=[Bpm])
                    S.op("dve", lambda e, pm=pm, nb=nb: e.tensor_tensor(
                        out=modr[:, nb * 512:(nb + 1) * 512], in0=pm[:], in1=adab[:, nb * 512:(nb + 1) * 512], op=ALU.add),
                        reads=[Bpm, Badab], writes=[Bmod])
                for j, c0 in ((0, D), (1, 4 * D)):
                    S.op("dve", lambda e, j=j, c0=c0: e.scalar_tensor_tensor(
                        out=modr[:, c0:c0 + D], in0=modr[:, c0:c0 + D], scalar=1.0, in1=n12[:, j, :],
                        op0=ALU.add, op1=ALU.mult), reads=[Bmod, Bn12], writes=[Bmod])
                S.dma("sp", lambda e: e.dma_start(out=MOD, in_=modr[:]), reads=[Bmod])
            S.barrier()

        def load_bcast(bc, Bbc, b, slots):
            for i, sl in enumerate(slots):
                S.dma("sp", lambda e, i=i, sl=sl: e.dma_start(
                    out=bc[:, i, :], in_=MOD[b:b + 1, sl * D:(sl + 1) * D].partition_broadcast(128)),
                    writes=[Bbc])


        def stage_B(b, rbt, BcM):
            T = seqs[b]
            Rr = T // GW
            NCH = Rr // 2
            o0 = offs[b]

            def r0_of(r):
                return min(max(r - 4, 0), Rr - 8)

            with ExitStack() as es:
                rbt = sb("rbt", [128, NH_A, 16, GW], F32, es)
                BcM = Buf("constM")
                S.dma("sp", lambda e: e.dma_start(out=rbt[:], in_=rb_d), writes=[BcM])
                qT = sb("qT", [128, 4, T], BF16, es)
                kT = sb("kT", [128, 4, T], BF16, es)
                vx = sb("vx", [128, NCH, NH_A, HD_A + 1], BF16, es)
                Bq, Bk, Bv = Buf("qT"), Buf("kT"), Buf("vx")
                for pr in range(4):
                    S.dma("sp", lambda e, pr=pr: e.dma_start(out=qT[:, pr, :], in_=QT[pr * 128:(pr + 1) * 128, o0:o0 + T]),
                          writes=[Bq])
                    S.dma("sp", lambda e, pr=pr: e.dma_start(out=kT[:, pr, :], in_=KT[pr * 128:(pr + 1) * 128, o0:o0 + T]),
                          writes=[Bk])
                S.op("pool", lambda e: e.memset(vx[:, :, :, HD_A:HD_A + 1], 1.0), writes=[Bv])
                for j in range(NCH):
                    S.dma("sp", lambda e, j=j: e.dma_start(
                        out=vx[:, j, :, 0:HD_A],
                        in_=VX[o0 + j * 128:o0 + (j + 1) * 128, :].rearrange("p (h d) -> p h d", d=HD_A)), writes=[Bv])
                ps_r = Rot(es, nc, "psB", 2, [128, 1024], F32, psum=True)
                po_r = Rot(es, nc, "poB", 2, [128, 4, HD_A + 1], F32, psum=True)
                ss_r = Rot(es, nc, "ssB", 2, [128, 768], F32)
                pT_r = Rot(es, nc, "pTB", 24, [128, 768], BF16)
                rd_r = Rot(es, nc, "rdB", 2, [128, 4, 1], F32)
                ab_r = Rot(es, nc, "abB", 3, [128, 4, HD_A], BF16)

                def do_hg(hg):
                    pts = {}

                    def do_chunk(j, hl):
                        h = hg * 4 + hl
                        pr, hb = h // 2, (h % 2) * 64
                        rows = [r for r in range(Rr) if r0_of(r) <= 2 * j + 1 and r0_of(r) + 7 >= 2 * j]
                        lo, nq = rows[0], len(rows)
                        e_lo = lo - 2 * j + 7
                        ps, Bps = ps_r.next()
                        n1 = min(512, nq * 64)
                        S.op("pe", lambda e: e.matmul(
                            ps[:, 0:n1], lhsT=kT[hb:hb + 64, pr, j * 128:(j + 1) * 128],
                            rhs=qT[hb:hb + 64, pr, lo * 64:lo * 64 + n1], start=True, stop=True),
                            reads=[Bk, Bq], writes=[Bps])
                        if nq * 64 > 512:
                            S.op("pe", lambda e: e.matmul(
                                ps[:, 512:nq * 64], lhsT=kT[hb:hb + 64, pr, j * 128:(j + 1) * 128],
                                rhs=qT[hb:hb + 64, pr, lo * 64 + 512:(lo + nq) * 64], start=True, stop=True),
                                reads=[Bk, Bq], writes=[Bps])
                        ss, Bss = ss_r.next()
                        pt, Bpt = pT_r.next()
                        for (a0, a1) in ((0, min(nq, 8)), (8, nq)):
                            if a1 <= a0:
                                continue
                            S.op("dve", lambda e, a0=a0, a1=a1: e.tensor_tensor(
                                out=ss[:, a0 * 64:a1 * 64].rearrange("p (a c) -> p a c", c=GW),
                                in0=ps[:, a0 * 64:a1 * 64].rearrange("p (a c) -> p a c", c=GW),
                                in1=rbt[:, h, e_lo + a0:e_lo + a1, :], op=ALU.add), reads=[Bps, BcM], writes=[Bss])
                        S.op("act", lambda e: e.activation(out=pt[:, 0:nq * 64], in_=ss[:, 0:nq * 64], func=AF.Exp),
                             reads=[Bss], writes=[Bpt])
                        for r in rows:
                            r0 = r0_of(r)
                            v0 = r0 <= 2 * j <= r0 + 7
                            v1 = r0 <= 2 * j + 1 <= r0 + 7
                            cc = (r - lo) * 64
                            if not v0:
                                S.op("pool", lambda e, cc=cc: e.memset(pt[0:64, cc:cc + 64], 0.0), reads=[], writes=[Bpt])
                            if not v1:
                                S.op("pool", lambda e, cc=cc: e.memset(pt[64:128, cc:cc + 64], 0.0), reads=[], writes=[Bpt])
                        pts[(j, hl)] = (pt, Bpt, lo)

                    def do_rowpair(rp):
                        po, Bpo = po_r.next()

                        def pv(r, hl):
                            h = hg * 4 + hl
                            r0 = r0_of(r)
                            chunks = list(range(r0 // 2, (r0 + 7) // 2 + 1))
                            ph = (r % 2) * 64
                            for ci, j in enumerate(chunks):
                                pt, Bpt, lo = pts[(j, hl)]
                                k0, k1 = 0, 128
                                c0 = (r - lo) * 64
                                S.op("pe", lambda e, pt=pt, j=j, k0=k0, k1=k1, c0=c0, ci=ci: e.matmul(
                                    po[ph:ph + 64, hl, :], lhsT=pt[k0:k1, c0:c0 + 64], rhs=vx[k0:k1, j, h, :],
                                    start=(ci == 0), stop=(ci == len(chunks) - 1)),
                                    reads=[Bpt, Bv], writes=[Bpo])
                        for r in (2 * rp, 2 * rp + 1):
                            for hl in range(4):
                                pv(r, hl)
                        rd, Brd = rd_r.next()
                        ab, Bab = ab_r.next()
                        S.op("dve", lambda e: e.reciprocal(out=rd[:], in_=po[:, :, HD_A:HD_A + 1]), reads=[Bpo], writes=[Brd])
                        S.op("dve", lambda e: e.tensor_tensor(
                            out=ab[:], in0=po[:, :, 0:HD_A], in1=rd[:].to_broadcast([128, 4, HD_A]),
                            op=ALU.mult), reads=[Bpo, Brd], writes=[Bab])
                        S.dma("pool", lambda e: e.dma_start(
                            out=ATT[o0 + rp * 128:o0 + (rp + 1) * 128, hg * 256:(hg + 1) * 256].rearrange(
                                "p (h d) -> p h d", d=HD_A), in_=ab[:]), reads=[Bab])

                    done_rp = 0
                    for j in range(NCH):
                        for hl in range(4):
                            do_chunk(j, hl)
                        while done_rp < Rr // 2:
                            need = max((r0_of(r) + 7) // 2 for r in (2 * done_rp, 2 * done_rp + 1))
                            if need > j:
                                break
                            do_rowpair(done_rp)
                            done_rp += 1

                for hg_ in range(2):
                    do_hg(hg_)


        def stage_C(b):
            T = seqs[b]
            o0 = offs[b]
            SEG = 512
            NSEG = T // SEG
            NCK = SEG // 64
            NB = T // 128
            BETA, EGC, EDK, NBEG, GB, NGC = range(6)
            with ExitStack() as es:
                masks = sb("masksC", [128, 4, 128], F32, es)
                id4 = sb("id4C", [128, 4, 128], BF16, es)
                m01 = sb("m01C", [4, SEG], F32, es)
                mk7f = sb("mk7fC", [128, 7, 128], F32, es)
                mk7 = sb("mk7C", [128, 7, 4, 128], BF16, es)
                BcC = Buf("constC")
                S.dma("sp", lambda e: e.dma_start(out=mk7f[:], in_=mk7_d), writes=[BcC])
                for h in range(4):
                    S.op("pool", lambda e, h=h: e.tensor_copy(out=mk7[:, :, h, :], in_=mk7f[:]), reads=[BcC], writes=[BcC])
                S.dma("sp", lambda e: e.dma_start(out=masks[:], in_=masks_d), writes=[BcC])
                S.dma("sp", lambda e: e.dma_start(out=m01[:], in_=m01_d), writes=[BcC])
                for h in range(4):
                    S.op("pool", lambda e, h=h: e.tensor_copy(out=id4[:, h, :], in_=idb[:]), reads=[Bconst], writes=[BcC])
                Sf = [sb("SfC%d" % d, [128, 4, 128], F32, es) for d in range(2)]
                Sb_ = [sb("SbC%d" % d, [128, 4, 128], BF16, es) for d in range(2)]
                BSf = [Buf("Sf0"), Buf("Sf1")]
                BSb = [Buf("Sb0"), Buf("Sb1")]
                dg = [sb("dgC%d" % d, [4, 4], F32, es) for d in range(2)]
                Bdg = [Buf("dg0"), Buf("dg1")]
                Bbcs = [Buf("BCS0"), Buf("BCS1")]
                Bosc = Buf("OSC")
                kT_r = [Rot(es, nc, "kTC%d" % d, 3, [128, 4, 128], BF16) for d in range(2)]
                qT_r = [Rot(es, nc, "qTC%d" % d, 3, [128, 4, 128], BF16) for d in range(2)]
                kk_r = [Rot(es, nc, "kkC%d" % d, 3, [128, 512], BF16) for d in range(2)]
                vv_r = Rot(es, nc, "vvC", 3, [128, 512], BF16)
                row_r = Rot(es, nc, "rowC", 16, [4, SEG], F32)
                bc_r = [Rot(es, nc, "bcC%d" % d, 1, [128, 8, 128], F32) for d in range(2)]
                egl_r = [Rot(es, nc, "eglC%d" % d, 2, [128, 4, NCK], F32) for d in range(2)]
                scs_r = [Rot(es, nc, "scsC%d" % d, 2, [128, SEG // 128, 24], F32) for d in range(2)]
                pp_r = Rot(es, nc, "ppC", 3, [128, 4, 128], F32, psum=True)
                pkq_r = Rot(es, nc, "pkqC", 2, [128, 4, 128], F32, psum=True)
                pvo_r = Rot(es, nc, "pvoC", 2, [128, 4, 128], F32, psum=True)
                pS_r = Rot(es, nc, "pSC", 1, [128, 4, 128], F32, psum=True)
                dm_r = Rot(es, nc, "dmC", 4, [128, 128], F32)
                E_r = [Rot(es, nc, "EC%d" % d, 4, [128, 4, 128], BF16) for d in range(2)]
                mb_r = [Rot(es, nc, "mbC%d" % d, 10, [128, 4, 128], BF16) for d in range(2)]
                m0_r = [Rot(es, nc, "m0C%d" % d, 2, [128, 4, 128], BF16) for d in range(2)]
                fin_r = [Rot(es, nc, "finC%d" % d, 6, [128, 4, 128], BF16) for d in range(2)]
                it_r = [Rot(es, nc, "itC%d" % d, 3, [128, 4, 128], BF16) for d in range(2)]
                bv_r = [Rot(es, nc, "bvC%d" % d, 3, [128, 4, 128], F32) for d in range(2)]
                r_r = [Rot(es, nc, "rC%d" % d, 2, [128, 4, 128], BF16) for d in range(2)]
                qs_r = [Rot(es, nc, "qsC%d" % d, 2, [128, 4, 128], F32) for d in range(2)]
                vn_r = [Rot(es, nc, "vnC%d" % d, 2, [128, 4, 128], BF16) for d in range(2)]
                vd_r = [Rot(es, nc, "vdC%d" % d, 2, [128, 4, 128], BF16) for d in range(2)]
                o_r = [Rot(es, nc, "oC%d" % d, 2, [128, 4, 128], F32) for d in range(2)]

                def seg_prep(d, sg):
                    t0 = o0 + sg * SEG
                    def row():
                        return row_r.next()
                    bl, Bbl = row(); al, Bal = row()
                    S.dma("sp", lambda e: e.dma_start(out=bl[:], in_=GR[d * 4:d * 4 + 4, t0:t0 + SEG]), writes=[Bbl])
                    S.dma("sp", lambda e: e.dma_start(out=al[:], in_=GR[8 + d * 4:12 + d * 4, t0:t0 + SEG]), writes=[Bal])
                    l1, Bl1 = row(); sp_, Bsp = row(); g, Bg = row(); pre, Bpre = row()
                    S.op("act", lambda e: e.activation(out=l1[:], in_=bl[:], func=AF.Exp, scale=-1.0), reads=[Bbl], writes=[Bl1])
                    S.op("act", lambda e: e.activation(out=l1[:], in_=l1[:], func=AF.Ln, bias=1.0), reads=[Bl1], writes=[Bl1])
                    S.op("act", lambda e: e.activation(out=sp_[:], in_=al[:], func=AF.Exp, bias=dg[d][:, 1:2]), reads=[Bal, Bdg[d]], writes=[Bsp])
                    S.op("act", lambda e: e.activation(out=sp_[:], in_=sp_[:], func=AF.Ln, bias=1.0), reads=[Bsp], writes=[Bsp])
                    S.op("dve", lambda e: e.tensor_scalar(out=g[:], in0=sp_[:], scalar1=dg[d][:, 2:3], scalar2=None, op0=ALU.mult),
                         reads=[Bsp, Bdg[d]], writes=[Bg])
                    S.op("dve", lambda e: e.tensor_tensor_scan(out=pre[:], data0=m01[:], data1=g[:], initial=0.0,
                                                               op0=ALU.mult, op1=ALU.add), reads=[Bg, BcC], writes=[Bpre])
                    tot = pre[:].rearrange("p (n c) -> p n c", c=64)[:, :, 63:64]
                    def v3(t):
                        return t[:].rearrange("p (n c) -> p n c", c=64)
                    if d == 0:
                        gc, Bgc = pre, Bpre
                    else:
                        gc, Bgc = row()
                        S.op("dve", lambda e: e.tensor_tensor(out=v3(gc), in0=tot.to_broadcast([4, NCK, 64]), in1=v3(pre),
                                                              op=ALU.subtract), reads=[Bpre], writes=[Bgc])
                        S.op("dve", lambda e: e.tensor_tensor(out=gc[:], in0=gc[:], in1=g[:], op=ALU.add),
                             reads=[Bgc, Bg], writes=[Bgc])
                    ngc, Bngc = row(); gb, Bgb = row(); egc, Begc = row(); beta, Bbeta = row()
                    nbeg, Bnbeg = row(); edk, Bedk = row(); egl, Begl = row()
                    S.op("dve", lambda e: e.tensor_scalar(out=ngc[:], in0=gc[:], scalar1=-1.0, scalar2=None, op0=ALU.mult),
                         reads=[Bgc], writes=[Bngc])
                    S.op("dve", lambda e: e.tensor_tensor(out=gb[:], in0=gc[:], in1=l1[:], op=ALU.subtract),
                         reads=[Bgc, Bl1], writes=[Bgb])
                    S.op("act", lambda e: e.activation(out=egc[:], in_=gc[:], func=AF.Exp), reads=[Bgc], writes=[Begc])
                    S.op("act", lambda e: e.activation(out=beta[:], in_=l1[:], func=AF.Exp, scale=-1.0), reads=[Bl1], writes=[Bbeta])
                    S.op("dve", lambda e: e.scalar_tensor_tensor(out=nbeg[:], in0=beta[:], scalar=-1.0, in1=egc[:],
                                                                 op0=ALU.mult, op1=ALU.mult), reads=[Bbeta, Begc], writes=[Bnbeg])
                    S.op("dve", lambda e: e.tensor_tensor(out=v3(edk), in0=tot.to_broadcast([4, NCK, 64]), in1=v3(gc),
                                                          op=ALU.subtract), reads=[Bpre, Bgc], writes=[Bedk])
                    S.op("act", lambda e: e.activation(out=edk[:], in_=edk[:], func=AF.Exp), reads=[Bedk], writes=[Bedk])
                    S.op("act", lambda e: e.activation(out=egl[:, 0:NCK].rearrange("p (n o) -> p n o", o=1), in_=tot,
                                                       func=AF.Exp), reads=[Bpre], writes=[Begl])
                    r0 = d * 12
                    S.dma("pool", lambda e: e.dma_start(out=BCS[r0:r0 + 4, 0:SEG], in_=ngc[:]), reads=[Bngc], writes=[Bbcs[d]])
                    S.dma("pool", lambda e: e.dma_start(out=BCS[r0 + 4:r0 + 8, 0:SEG], in_=gb[:]), reads=[Bgb], writes=[Bbcs[d]])
                    S.dma("pool", lambda e: e.dma_start(out=BCS[r0 + 8:r0 + 12, 0:NCK], in_=egl[:, 0:NCK]), reads=[Begl], writes=[Bbcs[d]])
                    eglb, Beglb = egl_r[d].next()
                    for h in range(4):
                        S.dma("sp", lambda e, h=h: e.dma_start(out=eglb[:, h, :], in_=BCS[r0 + 8 + h:r0 + 9 + h, 0:NCK].partition_broadcast(128)),
                              reads=[Bbcs[d]], writes=[Beglb])
                    scs, Bscs = scs_r[d].next()
                    rows = ((beta, Bbeta), (egc, Begc), (edk, Bedk), (nbeg, Bnbeg), (gb, Bgb), (ngc, Bngc))
                    for jb in range(SEG // 128):
                        psc, Bpsc = pp_r.next()
                        for qi, (rt, Brt) in enumerate(rows):
                            S.op("pe", lambda e, qi=qi, rt=rt, jb=jb, psc=psc: e.transpose(
                                out=psc[:, 0, qi * 4:qi * 4 + 4], in_=rt[:, jb * 128:(jb + 1) * 128], identity=idf[0:4, 0:4]),
                                reads=[Brt, Bconst], writes=[Bpsc])
                        S.op("dve", lambda e, jb=jb, psc=psc: e.tensor_copy(out=scs[:, jb, :], in_=psc[:, 0, 0:24]),
                             reads=[Bpsc], writes=[Bscs])
                    return dict(scs=scs, Bscs=Bscs, eglb=eglb, Beglb=Beglb)

                def mm4(lhs, Blhs, rhs, Brhs, acc=None):
                    pt, Bpt = pp_r.next()
                    for h in range(4):
                        if acc is not None:
                            S.op("pe", lambda e, h=h: e.matmul(pt[:, h, :], lhsT=idb[:], rhs=acc[0][:, h, :], start=True, stop=False),
                                 reads=[Bconst, acc[1]], writes=[Bpt])
                        S.op("pe", lambda e, h=h: e.matmul(pt[:, h, :], lhsT=lhs[:, h, :], rhs=rhs[:, h, :],
                                                           start=(acc is None), stop=True), reads=[Blhs, Brhs], writes=[Bpt])
                    return pt, Bpt

                evn = [0]

                def evac(d, pt, Bpt, pool=None):
                    t, Bt = (pool or mb_r[d]).next()
                    evn[0] += 1
                    if evn[0] % 2:
                        S.op("act", lambda e: e.copy(out=t[:], in_=pt[:]), reads=[Bpt], writes=[Bt])
                    else:
                        S.op("dve", lambda e: e.tensor_copy(out=t[:], in_=pt[:]), reads=[Bpt], writes=[Bt])
                    return t, Bt

                def masked(d, src, Bsrc, mi_, eng):
                    t, Bt = mb_r[d].next()
                    S.op(eng, lambda e: e.tensor_tensor(out=t[:], in0=src[:], in1=mk7[:, mi_, :, :], op=ALU.mult),
                         reads=[Bsrc, BcC], writes=[Bt])
                    return t, Bt

                def plus_id(d, src, Bsrc):
                    t, Bt = mb_r[d].next()
                    S.op("pool", lambda e: e.tensor_tensor(out=t[:], in0=src[:], in1=id4[:], op=ALU.add), reads=[Bsrc, BcC], writes=[Bt])
                    return t, Bt

                def inverse(d, Ml, BMl, Nu, BNu):
                    Q, BQ = masked(d, Ml, BMl, 0, "dve")
                    P, BP = masked(d, Nu, BNu, 0, "pool")
                    Z, BZ = plus_id(d, Q, BQ)
                    Y, BY = plus_id(d, P, BP)
                    yield
                    for lvl in range(2):
                        pQ, BpQ = mm4(P, BP, Q, BQ)
                        Q2, BQ2 = evac(d, pQ, BpQ)
                        pP, BpP = mm4(Q, BQ, P, BP)
                        P2, BP2 = evac(d, pP, BpP)
                        Q, BQ, P, BP = Q2, BQ2, P2, BP2
                        yield
                        pY, BpY = mm4(Q, BQ, Y, BY, acc=(Y, BY))
                        Y2, BY2 = evac(d, pY, BpY)
                        pZ, BpZ = mm4(P, BP, Z, BZ, acc=(Z, BZ))
                        Z2, BZ2 = evac(d, pZ, BpZ)
                        Y, BY, Z, BZ = Y2, BY2, Z2, BZ2
                        yield
                    X, BX, XT, BXT = Z, BZ, Y, BY
                    for li in range(3):
                        Mo, BMo = masked(d, Ml, BMl, 1 + li, "dve")
                        No, BNo = masked(d, Nu, BNu, 4 + li, "pool")
                        pB, BpB = mm4(No, BNo, X, BX)
                        Bm, BBm = evac(d, pB, BpB)
                        pB2, BpB2 = mm4(Mo, BMo, XT, BXT)
                        Bm2, BBm2 = evac(d, pB2, BpB2)
                        yield
                        fin = fin_r[d] if li == 2 else None
                        pC, BpC = mm4(XT, BXT, Bm, BBm, acc=(X, BX))
                        Xn, BXn = evac(d, pC, BpC, fin)
                        pC2, BpC2 = mm4(X, BX, Bm2, BBm2, acc=(XT, BXT))
                        XTn, BXTn = evac(d, pC2, BpC2, fin)
                        X, BX, XT, BXT = Xn, BXn, XTn, BXTn
                        yield
                    return X, BX, XT, BXT

                ready = [[], []]
                segcache = [{}, {}]

                def prep_thread(d):
                    order = range(NB) if d == 0 else range(NB - 1, -1, -1)
                    for bi in order:
                        while len(ready[d]) >= 1:
                            yield
                        sg, jb = divmod(bi, SEG // 128)
                        if sg not in segcache[d]:
                            segcache[d] = {sg: seg_prep(d, sg)}
                            yield
                        yield from prep_block(d, bi, sg, jb)

                def prep_block(d, bi, sg, jb):
                    sd = segcache[d][sg]
                    scs, Bsc = sd["scs"], sd["Bscs"]
                    gtok = o0 + bi * 128
                    kTb, BkT = kT_r[d].next(); qTb, BqT = qT_r[d].next(); kkb, Bkk = kk_r[d].next(); vvb, Bvv = vv_r.next()
                    bc, Bbc = bc_r[d].next()
                    S.dma("sp", lambda e: e.dma_start(out=kTb[:], in_=DKT[:, gtok:gtok + 128].rearrange("(h p) t -> p h t", p=128)), writes=[BkT])
                    S.dma("sp", lambda e: e.dma_start(out=qTb[:], in_=DQT[:, gtok:gtok + 128].rearrange("(h p) t -> p h t", p=128)), writes=[BqT])
                    S.dma("sp", lambda e: e.dma_start(out=kkb[:], in_=DKK[gtok:gtok + 128, :]), writes=[Bkk])
                    S.dma("sp", lambda e: e.dma_start(out=vvb[:], in_=DVV[gtok:gtok + 128, :]), writes=[Bvv])
                    r0 = d * 12
                    c0 = jb * 128
                    for i in range(8):
                        S.dma("sp", lambda e, i=i: e.dma_start(out=bc[:, i, :], in_=BCS[r0 + i:r0 + i + 1, c0:c0 + 128].partition_broadcast(128)),
                              reads=[Bbcs[d]], writes=[Bbc])
                    yield
                    E1, BE1 = E_r[d].next(); E1T, BE1T = E_r[d].next(); E2T, BE2T = E_r[d].next()
                    mi = (0, 1, 2) if d == 0 else (1, 0, 3)

                    def emat(E, BE, h, src_i, sgn, mk, bias_q):
                        dm, Bdm = dm_r.next()
                        S.op("pool", lambda e: e.tensor_tensor(
                            out=dm[:], in0=masks[:, mk, :], in1=bc[:, src_i * 4 + h, :], op=(ALU.add if sgn > 0 else ALU.subtract)),
                            reads=[Bbc, BcC], writes=[Bdm])
                        S.op("act", lambda e: e.activation(out=E[:, h, :], in_=dm[:], func=AF.Exp,
                                                           bias=scs[:, jb, bias_q * 4 + h:bias_q * 4 + h + 1]),
                             reads=[Bdm, Bsc], writes=[BE])
                    for h in range(4):
                        emat(E1, BE1, h, 0, 1.0, mi[0], GB)
                        emat(E1T, BE1T, h, 1, 1.0, mi[1], NGC)
                    yield
                    for h in range(4):
                        emat(E2T, BE2T, h, 0, -1.0, mi[2], NGC)
                    Q0, BQ0 = m0_r[d].next(); P0, BP0 = m0_r[d].next()
                    pG, BpG = pp_r.next()
                    for h in range(4):
                        S.op("pe", lambda e, h=h: e.matmul(pG[:, h, :], lhsT=kTb[:, h, :], rhs=kTb[:, h, :], start=True, stop=True),
                             reads=[BkT], writes=[BpG])
                    S.op("dve", lambda e: e.scalar_tensor_tensor(out=Q0[:], in0=pG[:], scalar=-1.0, in1=E1[:], op0=ALU.mult, op1=ALU.mult),
                         reads=[BpG, BE1], writes=[BQ0])
                    S.op("dve", lambda e: e.scalar_tensor_tensor(out=P0[:], in0=pG[:], scalar=-1.0, in1=E1T[:], op0=ALU.mult, op1=ALU.mult),
                         reads=[BpG, BE1T], writes=[BP0])
                    yield
                    pKQ, BpKQ = pp_r.next()
                    for h in range(4):
                        S.op("pe", lambda e, h=h: e.matmul(pKQ[:, h, :], lhsT=kTb[:, h, :], rhs=qTb[:, h, :], start=True, stop=True),
                             reads=[BkT, BqT], writes=[BpKQ])
                    iT, BiT = it_r[d].next()
                    S.op("dve", lambda e: e.tensor_tensor(out=iT[:], in0=pKQ[:], in1=E2T[:], op=ALU.mult),
                         reads=[BpKQ, BE2T], writes=[BiT])
                    bv, Bbv = bv_r[d].next()
                    S.op("pool", lambda e: e.tensor_tensor(
                        out=bv[:], in0=vvb[:].rearrange("p (h d) -> p h d", d=128),
                        in1=scs[:, jb, BETA * 4:BETA * 4 + 4].unsqueeze(2).to_broadcast([128, 4, 128]), op=ALU.mult),
                        reads=[Bvv, Bsc], writes=[Bbv])
                    yield
                    if d == 0:
                        X_, BX_, XT_, BXT_ = yield from inverse(d, Q0, BQ0, P0, BP0)
                        Y, BY = XT_, BXT_
                    else:
                        X_, BX_, XT_, BXT_ = yield from inverse(d, P0, BP0, Q0, BQ0)
                        Y, BY = X_, BX_
                    ready[d].append(dict(bi=bi, jb=jb, sd=sd, kTb=kTb, BkT=BkT, qTb=qTb, BqT=BqT, kkb=kkb, Bkk=Bkk,
                                         Y=Y, BY=BY, iT=iT, BiT=BiT, bv=bv, Bbv=Bbv))
                    yield

                def scan_thread(d):
                    for _ in range(NB):
                        while not ready[d]:
                            yield
                        yield from scan_block(d, ready[d].pop(0))

                def scan_block(d, pr):
                    ob, Bob = o_r[d].next()
                    for cp in ((0, 64) if d == 0 else (64, 0)):
                        yield from chunk(d, cp, pr, ob, Bob)
                    gtok = o0 + pr["bi"] * 128
                    dst = (OSC if d == 0 else OSB)[gtok:gtok + 128, :].rearrange("p (h d) -> p h d", d=128)
                    S.dma("pool", lambda e: e.dma_start(out=dst, in_=ob[:]), reads=[Bob])
                    yield

                def chunk(d, cp, pr, ob, Bob):
                    sd = pr["sd"]; jb = pr["jb"]
                    scs, Bsc, eglb, Beglb = sd["scs"], sd["Bscs"], sd["eglb"], sd["Beglb"]
                    kTb, BkT, qTb, BqT, kkb, Bkk = pr["kTb"], pr["BkT"], pr["qTb"], pr["BqT"], pr["kkb"], pr["Bkk"]
                    Y, BY, iT, BiT, bv, Bbv = pr["Y"], pr["BY"], pr["iT"], pr["BiT"], pr["bv"], pr["Bbv"]
                    ck = (jb * 128 + cp) // 64
                    sl = slice(cp, cp + 64)
                    def scol(q, h):
                        return scs[sl, jb, q * 4 + h:q * 4 + h + 1]
                    pk, Bpk = pkq_r.next(); pq, Bpq = pkq_r.next()
                    for h in range(4):
                        S.op("pe", lambda e, h=h: e.matmul(pk[sl, h, :], lhsT=kTb[:, h, cp:cp + 64], rhs=Sb_[d][:, h, :], start=True, stop=True),
                             reads=[BkT, BSb[d]], writes=[Bpk])
                    for h in range(4):
                        S.op("pe", lambda e, h=h: e.matmul(pq[sl, h, :], lhsT=qTb[:, h, cp:cp + 64], rhs=Sb_[d][:, h, :], start=True, stop=True),
                             reads=[BqT, BSb[d]], writes=[Bpq])
                    r, Br = r_r[d].next(); qs, Bqs = qs_r[d].next(); vn, Bvn = vn_r[d].next(); vd, Bvd = vd_r[d].next()
                    for h in range(4):
                        S.op("dve", lambda e, h=h: e.scalar_tensor_tensor(
                            out=r[sl, h, :], in0=pk[sl, h, :], scalar=scol(NBEG, h), in1=bv[sl, h, :],
                            op0=ALU.mult, op1=ALU.add), reads=[Bpk, Bsc, Bbv], writes=[Br])
                    for h in range(4):
                        S.op("act", lambda e, h=h: e.activation(out=qs[sl, h, :], in_=pq[sl, h, :], func=AF.Copy, scale=scol(EGC, h)),
                             reads=[Bpq, Bsc], writes=[Bqs])
                    yield
                    pv, Bpv = pvo_r.next()
                    for h in range(4):
                        S.op("pe", lambda e, h=h: e.matmul(pv[sl, h, :], lhsT=Y[sl, h, cp:cp + 64], rhs=r[sl, h, :], start=True, stop=True),
                             reads=[BY, Br], writes=[Bpv])
                    S.op("dve", lambda e: e.tensor_copy(out=vn[sl, :, :], in_=pv[sl, :, :]), reads=[Bpv], writes=[Bvn])
                    for h in range(4):
                        S.op("dve", lambda e, h=h: e.tensor_scalar(out=vd[sl, h, :], in0=pv[sl, h, :], scalar1=scol(EDK, h),
                                                                   scalar2=None, op0=ALU.mult), reads=[Bpv, Bsc], writes=[Bvd])
                    yield
                    po, Bpo = pvo_r.next()
                    for h in range(4):
                        S.op("pe", lambda e, h=h: e.matmul(po[sl, h, :], lhsT=iT[sl, h, cp:cp + 64], rhs=vn[sl, h, :], start=True, stop=True),
                             reads=[BiT, Bvn], writes=[Bpo])
                    S.op("dve", lambda e: e.tensor_tensor(out=ob[sl, :, :], in0=po[sl, :, :], in1=qs[sl, :, :], op=ALU.add),
                         reads=[Bpo, Bqs], writes=[Bob])
                    pS, BpS = pS_r.next()
                    for h in range(4):
                        S.op("pe", lambda e, h=h: e.matmul(pS[:, h, :], lhsT=kkb[sl, h * 128:(h + 1) * 128], rhs=vd[sl, h, :],
                                                           start=True, stop=True), reads=[Bkk, Bvd], writes=[BpS])
                    for h in range(4):
                        S.op("dve", lambda e, h=h: e.scalar_tensor_tensor(
                            out=Sf[d][:, h, :], in0=Sf[d][:, h, :], scalar=eglb[:, h, ck:ck + 1], in1=pS[:, h, :],
                            op0=ALU.mult, op1=ALU.add), reads=[BSf[d], Beglb, BpS], writes=[BSf[d]])
                    S.op("act", lambda e: e.copy(out=Sb_[d][:], in_=Sf[d][:]), reads=[BSf[d]], writes=[BSb[d]])
                    yield

                for d in range(2):
                    def init_dir(d):
                        S.dma("sp", lambda e: e.dma_start(out=dg[d][:, 0:2], in_=dgate_d[d * 4:d * 4 + 4, :]), writes=[Bdg[d]])
                        S.op("act", lambda e: e.activation(out=dg[d][:, 2:3], in_=dg[d][:, 0:1], func=AF.Exp), reads=[Bdg[d]], writes=[Bdg[d]])
                        S.op("dve", lambda e: e.tensor_scalar(out=dg[d][:, 2:3], in0=dg[d][:, 2:3], scalar1=-1.0, scalar2=None, op0=ALU.mult),
                             reads=[Bdg[d]], writes=[Bdg[d]])
                        S.op("pool", lambda e: e.memset(Sf[d][:], 0.0), writes=[BSf[d]])
                        S.op("pool", lambda e: e.memset(Sb_[d][:], 0.0), writes=[BSb[d]])
                    init_dir(d)
                interleave([prep_thread(0), scan_thread(0), prep_thread(1), scan_thread(1)], 4)

        def stage_D(b, wo, Bwo, don, BcD):
            T = seqs[b]
            o0 = offs[b]
            NTL = T // 128
            with ExitStack() as es:
                bc = sb("bcD", [128, D], F32, es)
                Bbc = Buf("bcD")
                S.dma("sp", lambda e: e.dma_start(out=bc[:], in_=MOD[b:b + 1, 2 * D:3 * D].partition_broadcast(128)), writes=[Bbc])
                at_r = Rot(es, nc, "atD", 2, [128, 512], BF16)
                os_r = Rot(es, nc, "osD", 2, [128, 4, 128], F32)
                osb_r = Rot(es, nc, "osbD", 2, [128, 4, 128], F32)
                zt_r = Rot(es, nc, "ztD", 2, [128, 4, 128], BF16)
                x_r = Rot(es, nc, "xD", 2, [128, D], F32)
                junk = sb("junkD", [128, 128], F32, es)
                ss_r = Rot(es, nc, "ssD", 2, [128, 4, 1], F32)
                on_r = Rot(es, nc, "onD", 2, [128, 4, 128], BF16)
                ct_r = Rot(es, nc, "ctD", 2, [128, 8, 128], BF16)
                ptr_r = Rot(es, nc, "ptrD", 2, [128, 8, 128], BF16, psum=True)
                po_r = Rot(es, nc, "poD", 4, [128, 512], F32, psum=True)
                t_r = Rot(es, nc, "tD", 2, [128, D], F32)
                xn_r = Rot(es, nc, "xnD", 2, [128, D], F32)

                def loads(i):
                    g = o0 + i * 128
                    at, Bat = at_r.next(); osc, Bos = os_r.next(); zt, Bzt = zt_r.next(); xt, Bxt = x_r.next()
                    S.dma("sp", lambda e: e.dma_start(out=at[:], in_=ATT[g:g + 128, :]), writes=[Bat])
                    S.dma("sp", lambda e: e.dma_start(out=osc[:], in_=OSC[g:g + 128, :].rearrange("p (h d) -> p h d", d=128)), writes=[Bos])
                    osb, Bosb = osb_r.next()
                    S.dma("sp", lambda e: e.dma_start(out=osb[:], in_=OSB[g:g + 128, :].rearrange("p (h d) -> p h d", d=128)), writes=[Bosb])
                    S.op("pool", lambda e: e.tensor_tensor(out=osc[:], in0=osc[:], in1=osb[:], op=ALU.add), reads=[Bos, Bosb], writes=[Bos])
                    S.dma("sp", lambda e: e.dma_start(out=zt[:], in_=ZT[:, g:g + 128].rearrange("(h p) t -> p h t", p=128)), writes=[Bzt])
                    S.dma("sp", lambda e: e.dma_start(out=xt[:], in_=x_d[g + 1:g + 129, :]), writes=[Bxt])
                    return (at, Bat, osc, Bos, zt, Bzt, xt, Bxt)

                def tile(i, ld):
                    at, Bat, osc, Bos, zt, Bzt, xt, Bxt = ld
                    g = o0 + i * 128
                    ss, Bss = ss_r.next(); on, Bon = on_r.next(); ct, Bct = ct_r.next()
                    for h in range(4):
                        S.op("act", lambda e, h=h: e.activation(out=junk[:], in_=osc[:, h, :], func=AF.Square, accum_out=ss[:, h, :]),
                             reads=[Bos], writes=[Bss])
                    S.op("act", lambda e: e.activation(out=ss[:], in_=ss[:], func=AF.Ln, scale=1.0 / HD_D, bias=EPS), reads=[Bss], writes=[Bss])
                    S.op("act", lambda e: e.activation(out=ss[:], in_=ss[:], func=AF.Exp, scale=-0.5), reads=[Bss], writes=[Bss])
                    S.op("dve", lambda e: e.tensor_tensor(out=on[:], in0=osc[:], in1=ss[:].to_broadcast([128, 4, 128]), op=ALU.mult),
                         reads=[Bos, Bss], writes=[Bon])
                    ptr, Bptr = ptr_r.next()
                    for j in range(4):
                        S.op("pe", lambda e, j=j: e.transpose(out=ptr[:, j, :], in_=at[:, j * 128:(j + 1) * 128], identity=idb[:]),
                             reads=[Bat, Bconst], writes=[Bptr])
                    for h in range(4):
                        S.op("pe", lambda e, h=h: e.transpose(out=ptr[:, 4 + h, :], in_=on[:, h, :], identity=idb[:]),
                             reads=[Bon, Bconst], writes=[Bptr])
                    S.op("act", lambda e: e.copy(out=ct[:, 0:4, :], in_=ptr[:, 0:4, :]), reads=[Bptr], writes=[Bct])
                    S.op("dve", lambda e: e.scalar_tensor_tensor(out=ct[:, 4:8, :], in0=ptr[:, 4:8, :], scalar=don[:, 0:1], in1=zt[:],
                                                                 op0=ALU.mult, op1=ALU.mult), reads=[Bptr, Bzt, BcD], writes=[Bct])
                    xn, Bxn = xn_r.next()
                    tt, Btt = t_r.next()
                    for hf in range(2):
                        po, Bpo = po_r.next()
                        for kc in range(8):
                            S.op("pe", lambda e, kc=kc, po=po, hf=hf: e.matmul(
                                po[:], lhsT=ct[:, kc, :], rhs=wo[:, kc, hf * 512:(hf + 1) * 512], start=(kc == 0), stop=(kc == 7)),
                                reads=[Bct, Bwo], writes=[Bpo])
                        S.op("dve", lambda e, po=po, hf=hf: e.tensor_tensor(
                            out=tt[:, hf * 512:(hf + 1) * 512], in0=po[:], in1=bc[:, hf * 512:(hf + 1) * 512], op=ALU.mult),
                            reads=[Bpo, Bbc], writes=[Btt])
                    S.op("pool", lambda e: e.tensor_tensor(out=xn[:], in0=tt[:], in1=xt[:], op=ALU.add), reads=[Btt, Bxt], writes=[Bxn])
                    S.dma("pool", lambda e: e.dma_start(out=XN[g + 1:g + 129, :], in_=xn[:]), reads=[Bxn])

                nx = loads(0)
                for i in range(NTL):
                    cur = nx
                    if i + 1 < NTL:
                        nx = loads(i + 1)
                    tile(i, cur)

        def stage_E():
            with ExitStack() as es:
                wdn = sb("wdn", [128, 22, D], BF16, es)
                Bwdn = Buf("wdn")
                for c in range(22):
                    S.dma("pool", lambda e, c=c: e.dma_start(out=wdn[:, c, :], in_=wdn_d[c * 128:(c + 1) * 128, :]), writes=[Bwdn])
                fcw = sb("fcw", [128, 44, 3], F32, es)
                fcb = sb("fcb", [128, 44], F32, es)
                BcE = Buf("constE")
                S.dma("sp", lambda e: e.dma_start(out=fcw[:], in_=fcw_d), writes=[BcE])
                S.dma("sp", lambda e: e.dma_start(out=fcb[:], in_=fcb_d), writes=[BcE])
                wu_r = Rot(es, nc, "wuE", 3, [128, 2, 8, 128], BF16)
                Bwub = Buf("WUB")
                for c in range(22):
                    def cast_pair(c):
                        wu, Bwu = wu_r.next()
                        for ab in range(2):
                            col = ab * DFF + c * 128
                            S.dma("pool", lambda e, ab=ab, col=col: e.dma_start(
                                out=wu[:, ab, :, :], in_=wup_d[:, col:col + 128].rearrange("(kc p) n -> p kc n", p=128)), writes=[Bwu])
                        S.dma("sp", lambda e: e.dma_start(out=WUB[c], in_=wu[:]), reads=[Bwu], writes=[Bwub])
                    cast_pair(c)
                bcr = Rot(es, nc, "bcE", 1, [128, 3, D], F32)
                xtr = Rot(es, nc, "xtE", 2, [128, 4, D], F32)
                h2r = Rot(es, nc, "h2T", 2, [128, 8, 512], BF16)
                junk = sb("junkE", [128, D], F32, es)
                st_r = Rot(es, nc, "stE", 4, [128, 4], F32)
                hf_r = Rot(es, nc, "hfE", 1, [128, D], F32)
                hb_r = Rot(es, nc, "hbE", 2, [128, D], BF16)
                gt_r = Rot(es, nc, "gtE", 1, [128, 22, 512], BF16)
                ca_r = Rot(es, nc, "caE", 2, [128, 512], F32)
                cb_r = Rot(es, nc, "cbE", 2, [128, 512], F32)
                sa_r = Rot(es, nc, "saE", 2, [128, 512], F32)
                yt_r = Rot(es, nc, "ytE", 2, [128, D], F32)
                ptr_r = Rot(es, nc, "ptrE", 2, [128, 8, 128], BF16, psum=True)
                pab_r = Rot(es, nc, "pabE", 4, [128, 512], F32, psum=True)
                po_r = Rot(es, nc, "poE", 2, [128, 512], F32, psum=True)
                for t, B_ in xtr.t + gt_r.t:
                    S.op("pool", lambda e, t=t: e.memset(t[:], 0.0), writes=[B_])

                def load_x(b, t0, NT):
                    xt, Bxt = xtr.next()
                    NC = NT + 2
                    g0 = offs[b] + t0
                    for s in range(-(-NC // 128)):
                        rs = min(128, NC - s * 128)
                        S.dma("sp", lambda e, s=s, rs=rs: e.dma_start(
                            out=xt[:rs, s, :], in_=XN[g0 + s * 128: g0 + s * 128 + rs, :]), writes=[Bxt])
                    return xt, Bxt

                def do_tile(b, t0, NT, xt, Bxt, bc, Bbc):
                    NC = NT + 2
                    nsubc = -(-NC // 128)
                    h2T, Bh2 = h2r.next()
                    gt = offs[b] + t0

                    def norm_sub(s):
                        rs = min(128, NC - s * 128)
                        st, Bst = st_r.next(); hf, Bhf = hf_r.next(); hb, Bhb = hb_r.next(); ptr, Bptr = ptr_r.next()
                        S.op("act", lambda e: e.activation(out=junk[:rs, :], in_=xt[:rs, s, :], func=AF.Square, accum_out=st[:rs, 0:1]),
                             reads=[Bxt], writes=[Bst])
                        S.op("act", lambda e: e.activation(out=st[:rs, 1:2], in_=st[:rs, 0:1], func=AF.Ln, scale=1.0 / D, bias=EPS),
                             reads=[Bst], writes=[Bst])
                        S.op("act", lambda e: e.activation(out=st[:rs, 2:3], in_=st[:rs, 1:2], func=AF.Exp, scale=-0.5), reads=[Bst], writes=[Bst])
                        S.op("dve", lambda e: e.scalar_tensor_tensor(out=hf[:rs, :], in0=xt[:rs, s, :], scalar=st[:rs, 2:3], in1=bc[:rs, 0, :],
                                                                     op0=ALU.mult, op1=ALU.mult), reads=[Bxt, Bst, Bbc], writes=[Bhf])
                        S.op("pool", lambda e: e.tensor_tensor(out=hb[:rs, :], in0=hf[:rs, :], in1=bc[:rs, 1, :], op=ALU.add),
                             reads=[Bhf, Bbc], writes=[Bhb])
                        for kc in range(8):
                            S.op("pe", lambda e, kc=kc: e.transpose(out=ptr[:, kc, 0:rs], in_=hb[:rs, kc * 128:(kc + 1) * 128],
                                                                    identity=idb[:rs, :rs]), reads=[Bhb, Bconst], writes=[Bptr])
                        S.op("act", lambda e: e.copy(out=h2T[:, :, s * 128:s * 128 + rs], in_=ptr[:, :, 0:rs]), reads=[Bptr], writes=[Bh2])
                    for s in range(nsubc):
                        norm_sub(s)
                    if t0 == 0:
                        S.op("pool", lambda e: e.memset(h2T[:, :, 0:1], 0.0), writes=[Bh2])
                    if t0 + NT == seqs[b]:
                        S.op("pool", lambda e: e.memset(h2T[:, :, NC - 1:NC], 0.0), writes=[Bh2])

                    gtd, Bgt = gt_r.next()

                    def pair(c):
                        wu, Bwu = wu_r.next()
                        S.dma("sp", lambda e: e.dma_start(out=wu[:], in_=WUB[c]), reads=[Bwub], writes=[Bwu])

                        def half(ab, tr):
                            pp, Bpp = pab_r.next()
                            for kc in range(8):
                                S.op("pe", lambda e, kc=kc: e.matmul(pp[:, :NC], lhsT=wu[:, ab, kc, :], rhs=h2T[:, kc, 0:NC],
                                                                     start=(kc == 0), stop=(kc == 7)), reads=[Bwu, Bh2], writes=[Bpp])
                            cv, Bcv = tr.next()
                            ch = ab * 22 + c
                            S.op("act", lambda e: e.activation(out=cv[:, :NT], in_=pp[:, 1:1 + NT], func=AF.Identity,
                                                               scale=fcw[:, ch, 1:2], bias=fcb[:, ch:ch + 1]), reads=[Bpp, BcE], writes=[Bcv])
                            S.op("dve", lambda e: e.scalar_tensor_tensor(out=cv[:, :NT], in0=pp[:, 0:NT], scalar=fcw[:, ch, 0:1], in1=cv[:, :NT],
                                                                         op0=ALU.mult, op1=ALU.add), reads=[Bpp, BcE, Bcv], writes=[Bcv])
                            S.op("dve", lambda e: e.scalar_tensor_tensor(out=cv[:, :NT], in0=pp[:, 2:2 + NT], scalar=fcw[:, ch, 2:3], in1=cv[:, :NT],
                                                                         op0=ALU.mult, op1=ALU.add), reads=[Bpp, BcE, Bcv], writes=[Bcv])
                            return cv, Bcv
                        ca, Bca = half(0, ca_r)
                        cb2, Bcb2 = half(1, cb_r)
                        sa, Bsa = sa_r.next()
                        S.op("act", lambda e: e.activation(out=sa[:, :NT], in_=ca[:, :NT], func=AF.Silu), reads=[Bca], writes=[Bsa])
                        S.op("pool", lambda e: e.tensor_tensor(out=gtd[:, c, 1:1 + NT], in0=sa[:, :NT], in1=cb2[:, :NT], op=ALU.mult),
                             reads=[Bsa, Bcb2], writes=[Bgt])
                    for c in range(22):
                        pair(c)

                    def down(s):
                        rs = min(128, NC - s * 128)
                        yt, Byt = yt_r.next()
                        for hf_ in range(2):
                            def dhalf(hf_):
                                po, Bpo = po_r.next()
                                for c in range(22):
                                    S.op("pe", lambda e, c=c: e.matmul(po[:rs, :], lhsT=gtd[:, c, s * 128:s * 128 + rs],
                                                                       rhs=wdn[:, c, hf_ * 512:(hf_ + 1) * 512], start=(c == 0), stop=(c == 21)),
                                         reads=[Bgt, Bwdn], writes=[Bpo])
                                S.op("dve", lambda e: e.tensor_tensor(out=yt[:rs, hf_ * 512:(hf_ + 1) * 512], in0=po[:rs, :],
                                                                      in1=bc[:rs, 2, hf_ * 512:(hf_ + 1) * 512], op=ALU.mult),
                                     reads=[Bpo, Bbc], writes=[Byt])
                            dhalf(hf_)
                        S.op("pool", lambda e: e.tensor_tensor(out=yt[:rs, :], in0=yt[:rs, :], in1=xt[:rs, s, :], op=ALU.add),
                             reads=[Byt, Bxt], writes=[Byt])
                        p_lo = 1 if s == 0 else 0
                        p_hi = min(rs, NT + 1 - s * 128)
                        if p_hi > p_lo:
                            tok = gt - 1 + s * 128
                            S.dma("pool", lambda e: e.dma_start(out=y_d[tok + p_lo:tok + p_hi, :], in_=yt[p_lo:p_hi, :]), reads=[Byt])
                    for s in range(nsubc):
                        down(s)

                work = [(b, t0, NT) for b in range(NSEQ) for (t0, NT) in tiles_of(seqs[b])]
                nxt = load_x(*work[0])
                cur_b = -1
                bcs = None
                for wi, (b, t0, NT) in enumerate(work):
                    xt, Bxt = nxt
                    if wi + 1 < len(work):
                        nxt = load_x(*work[wi + 1])
                    if b != cur_b:
                        cur_b = b
                        bcs = bcr.next()
                        load_bcast(bcs[0], bcs[1], b, (4, 3, 5))
                    do_tile(b, t0, NT, xt, Bxt, bcs[0], bcs[1])

        if "A" in stages:
            with ExitStack() as es:
                win = sb("win", [128, 8, INC], BF16, es)
                Bwin = Buf("win")
                for kc in range(8):
                    S.dma("pool", lambda e, kc=kc: e.dma_start(out=win[:, kc, :], in_=win_d[kc * 128:(kc + 1) * 128, :]),
                          writes=[Bwin])
                qkn = sb("qkn", [128, 2], F32, es)
                dcw = sb("dcw", [128, 12, 3], F32, es)
                BcA = Buf("constA")
                S.dma("sp", lambda e: e.dma_start(out=qkn[:], in_=qkn_d), writes=[BcA])
                S.dma("sp", lambda e: e.dma_start(out=dcw[:], in_=dcw_d), writes=[BcA])
                S.op("dve", lambda e: e.tensor_scalar(out=qkn[:, 0:1], in0=qkn[:, 0:1], scalar1=HD_A ** -0.5, scalar2=None,
                                                      op0=ALU.mult), reads=[BcA], writes=[BcA])
                bcr = Rot(es, nc, "bcA", 2, [128, 2, D], F32)
                xtr = Rot(es, nc, "xt", 2, [128, 4, D], F32)
                h1r = Rot(es, nc, "h1T", 2, [128, 8, 512], BF16)
                junk = sb("junkA", [128, D], F32, es)
                st_r = Rot(es, nc, "stA", 4, [128, 4], F32)
                hf_r = Rot(es, nc, "hfA", 2, [128, D], F32)
                hb_r = Rot(es, nc, "hbA", 2, [128, D], BF16)
                ptr_r = Rot(es, nc, "ptrA", 2, [128, 8, 128], BF16, psum=True)
                pp_r = Rot(es, nc, "ppA", 4, [128, 512], F32, psum=True)
                pss_r = Rot(es, nc, "pssA", 2, [128, 512], F32, psum=True)
                sqb_r = Rot(es, nc, "sqbA", 4, [128, 512], BF16)
                rs1_r = Rot(es, nc, "rs1A", 4, [128, 512], F32)
                rs2_r = Rot(es, nc, "rs2A", 4, [128, 512], F32)
                ob_r = Rot(es, nc, "obA", 6, [128, 512], BF16)
                cb_r = Rot(es, nc, "cbA", 4, [128, 512], F32)
                sl_r = Rot(es, nc, "slA", 4, [128, 512], F32)
                ee_r = Rot(es, nc, "eeA", 4, [128, 512], F32)
                tok_r = Rot(es, nc, "tokA", 3, [128, 4, 512], BF16)
                gs_r = Rot(es, nc, "gsA", 2, [16, 512], F32)
                for t, _ in xtr.t:
                    S.op("pool", lambda e, t=t: e.memset(t[:], 0.0), writes=[_])

                def load_x(b, t0, NT):
                    xt, Bxt = xtr.next()
                    NC = NT + 2
                    g0 = offs[b] + t0
                    for s in range(-(-NC // 128)):
                        rs = min(128, NC - s * 128)
                        S.dma("sp", lambda e, xt=xt, s=s, rs=rs, g0=g0: e.dma_start(
                            out=xt[:rs, s, :], in_=x_d[g0 + s * 128: g0 + s * 128 + rs, :]), writes=[Bxt])
                    return xt, Bxt

                def do_tile(b, t0, NT, xt, Bxt, bc, Bbc):
                    NC = NT + 2
                    h1T, Bh1 = h1r.next()
                    gt = offs[b] + t0
                    nsub = -(-NT // 128)

                    def norm_sub(s):
                        rs = min(128, NC - s * 128)
                        st, Bst = st_r.next()
                        hf, Bhf = hf_r.next()
                        hb, Bhb = hb_r.next()
                        ptr, Bptr = ptr_r.next()
                        S.op("act", lambda e: e.activation(
                            out=junk[:rs, :], in_=xt[:rs, s, :], func=AF.Square, accum_out=st[:rs, 0:1]),
                            reads=[Bxt], writes=[Bst])
                        S.op("act", lambda e: e.activation(
                            out=st[:rs, 1:2], in_=st[:rs, 0:1], func=AF.Ln, scale=1.0 / D, bias=EPS),
                            reads=[Bst], writes=[Bst])
                        S.op("act", lambda e: e.activation(out=st[:rs, 2:3], in_=st[:rs, 1:2], func=AF.Exp, scale=-0.5),
                             reads=[Bst], writes=[Bst])
                        S.op("dve", lambda e: e.scalar_tensor_tensor(
                            out=hf[:rs, :], in0=xt[:rs, s, :], scalar=st[:rs, 2:3], in1=bc[:rs, 0, :],
                            op0=ALU.mult, op1=ALU.mult), reads=[Bxt, Bst, Bbc], writes=[Bhf])
                        S.op("pool", lambda e: e.tensor_tensor(
                            out=hb[:rs, :], in0=hf[:rs, :], in1=bc[:rs, 1, :], op=ALU.add),
                            reads=[Bhf, Bbc], writes=[Bhb])
                        for kc in range(8):
                            S.op("pe", lambda e, kc=kc: e.transpose(
                                out=ptr[:, kc, 0:rs], in_=hb[:rs, kc * 128:(kc + 1) * 128], identity=idb[:rs, :rs]),
                                reads=[Bhb, Bconst], writes=[Bptr])
                        S.op("act", lambda e: e.copy(
                            out=h1T[:, :, s * 128:s * 128 + rs], in_=ptr[:, :, 0:rs]), reads=[Bptr], writes=[Bh1])

                    for s in range(-(-NC // 128)):
                        norm_sub(s)
                    if t0 == 0:
                        S.op("pool", lambda e: e.memset(h1T[:, :, 0:1], 0.0), writes=[Bh1])
                    if t0 + NT == seqs[b]:
                        S.op("pool", lambda e: e.memset(h1T[:, :, NC - 1:NC], 0.0), writes=[Bh1])

                    def proj(c0, ncols):
                        pp, Bpp = pp_r.next()
                        for kc in range(8):
                            S.op("pe", lambda e, kc=kc: e.matmul(
                                pp[:ncols, :NC], lhsT=win[:, kc, c0:c0 + ncols], rhs=h1T[:, kc, 0:NC],
                                start=(kc == 0), stop=(kc == 7)), reads=[Bwin, Bh1], writes=[Bpp])
                        return pp, Bpp

                    def inv_norm(src, Bsrc, lhs, scale):
                        sqb, Bsq = sqb_r.next()
                        pss, Bps = pss_r.next()
                        rs1, Br1 = rs1_r.next()
                        rs2, Br2 = rs2_r.next()
                        S.op("act", lambda e: e.activation(out=sqb[:, :NT], in_=src, func=AF.Square),
                             reads=[Bsrc], writes=[Bsq])
                        yield
                        S.op("pe", lambda e: e.matmul(pss[:, :NT], lhsT=lhs[:], rhs=sqb[:, :NT], start=True, stop=True),
                             reads=[Bsq, Bconst], writes=[Bps])
                        S.op("act", lambda e: e.activation(out=rs1[:, :NT], in_=pss[:, :NT], func=AF.Ln,
                                                           scale=scale, bias=EPS), reads=[Bps], writes=[Br1])
                        S.op("act", lambda e: e.activation(out=rs2[:, :NT], in_=rs1[:, :NT], func=AF.Exp, scale=-0.5),
                             reads=[Br1], writes=[Br2])
                        yield
                        return rs2, Br2

                    def to_tok(src, Bsrc, tk, Btk, col):
                        def one(s):
                            rs = min(128, NT - s * 128)
                            ptr, Bptr = ptr_r.next()
                            S.op("pe", lambda e: e.transpose(
                                out=ptr[:rs, 0, :], in_=src[:, s * 128:s * 128 + rs], identity=idb[:]),
                                reads=[Bsrc, Bconst], writes=[Bptr])
                            S.op("act", lambda e: e.copy(
                                out=tk[:rs, s, col:col + 128], in_=ptr[:rs, 0, :]), reads=[Bptr], writes=[Btk])
                        for s in range(nsub):
                            one(s)

                    def store_tok(dst, tk, Btk):
                        for s in range(nsub):
                            rs = min(128, NT - s * 128)
                            S.dma("sp", lambda e, s=s, rs=rs: e.dma_start(
                                out=dst[gt + s * 128: gt + s * 128 + rs, :], in_=tk[:rs, s, :]), reads=[Btk])

                    tk, Btk = tok_r.next()
                    tkk, Btkk = tok_r.next()
                    tkv, Btkv = tok_r.next()

                    def silu(dst, Bdst, src, Bsrc):
                        ee, Bee = ee_r.next()
                        S.op("act", lambda e: e.activation(out=ee[:, :NT], in_=src, func=AF.Exp, scale=-1.0),
                             reads=[Bsrc], writes=[Bee])
                        yield
                        S.op("act", lambda e: e.activation(out=ee[:, :NT], in_=ee[:, :NT], func=AF.Ln, bias=1.0), reads=[Bee], writes=[Bee])
                        S.op("act", lambda e: e.activation(out=ee[:, :NT], in_=ee[:, :NT], func=AF.Exp, scale=-1.0), reads=[Bee], writes=[Bee])
                        S.op("dve", lambda e: e.tensor_tensor(out=dst[:, :NT], in0=src, in1=ee[:, :NT], op=ALU.mult),
                             reads=[Bsrc, Bee], writes=[Bdst])
                        yield

                    def qk_chunk(cb):
                        pp, Bpp = proj(cb * 128, 128)
                        rs2, Br2 = yield from inv_norm(pp[:, 1:1 + NT], Bpp, blk2, 1.0 / HD_A)
                        ob, Bob = ob_r.next()
                        j = 0 if cb < 4 else 1
                        S.op("dve", lambda e: e.scalar_tensor_tensor(
                            out=ob[:, :NT], in0=pp[:, 1:1 + NT], scalar=qkn[:, j:j + 1], in1=rs2[:, :NT],
                            op0=ALU.mult, op1=ALU.mult), reads=[Bpp, Br2, BcA], writes=[Bob])
                        dst = QT if cb < 4 else KT
                        r0 = (cb % 4) * 128
                        S.dma("sp", lambda e: e.dma_start(
                            out=dst[r0:r0 + 128, gt:gt + NT], in_=ob[:, :NT]), reads=[Bob])

                    def v_chunk(j):
                        pp, Bpp = proj(1024 + j * 128, 128)
                        ob, Bob = ob_r.next()
                        S.op("act", lambda e: e.copy(out=ob[:, :NT], in_=pp[:, 1:1 + NT]),
                             reads=[Bpp], writes=[Bob])
                        yield
                        to_tok(ob, Bob, tk, Btk, j * 128)

                    def dn_chunk(jb):
                        pp, Bpp = proj(1536 + jb * 128, 128)
                        cbf, Bcb = cb_r.next()
                        sl, Bsl = sl_r.next()
                        S.op("act", lambda e: e.activation(
                            out=cbf[:, :NT], in_=pp[:, 1:1 + NT], func=AF.Copy, scale=dcw[:, jb, 1:2]),
                            reads=[Bpp, BcA], writes=[Bcb])
                        yield
                        S.op("dve", lambda e: e.scalar_tensor_tensor(
                            out=cbf[:, :NT], in0=pp[:, 0:NT], scalar=dcw[:, jb, 0:1], in1=cbf[:, :NT],
                            op0=ALU.mult, op1=ALU.add), reads=[Bpp, BcA, Bcb], writes=[Bcb])
                        S.op("dve", lambda e: e.scalar_tensor_tensor(
                            out=cbf[:, :NT], in0=pp[:, 2:2 + NT], scalar=dcw[:, jb, 2:3], in1=cbf[:, :NT],
                            op0=ALU.mult, op1=ALU.add), reads=[Bpp, BcA, Bcb], writes=[Bcb])
                        yield
                        yield from silu(sl, Bsl, cbf[:, :NT], Bcb)
                        ob, Bob = ob_r.next()
                        hh = jb % 4
                        if jb < 8:
                            rs2, Br2 = yield from inv_norm(sl[:, :NT], Bsl, ones_b, 1.0)
                            S.op("dve", lambda e: e.scalar_tensor_tensor(
                                out=ob[:, :NT], in0=sl[:, :NT], scalar=(HD_D ** -0.5 if jb < 4 else 1.0), in1=rs2[:, :NT],
                                op0=ALU.mult, op1=ALU.mult), reads=[Bsl, Br2], writes=[Bob])
                            dst = DQT if jb < 4 else DKT
                            S.dma("sp", lambda e: e.dma_start(
                                out=dst[hh * 128:(hh + 1) * 128, gt:gt + NT], in_=ob[:, :NT]), reads=[Bob])
                            if jb >= 4:
                                yield
                                to_tok(ob, Bob, tkk, Btkk, hh * 128)
                        else:
                            yield
                            S.op("pool", lambda e: e.tensor_copy(out=ob[:, :NT], in_=sl[:, :NT]),
                                 reads=[Bsl], writes=[Bob])
                            yield
                            to_tok(ob, Bob, tkv, Btkv, hh * 128)

                    def z_chunk(j):
                        pp, Bpp = proj(3072 + j * 128, 128)
                        ob, Bob = ob_r.next()
                        yield
                        yield from silu(ob, Bob, pp[:, 1:1 + NT], Bpp)
                        S.dma("sp", lambda e: e.dma_start(
                            out=ZT[j * 128:(j + 1) * 128, gt:gt + NT], in_=ob[:, :NT]), reads=[Bob])

                    gens = [qk_chunk(cb) for cb in range(8)] + [v_chunk(j) for j in range(4)] + \
                           [dn_chunk(jb) for jb in range(12)] + [z_chunk(j) for j in range(4)]
                    interleave(gens, 3)
                    store_tok(VX, tk, Btk)
                    store_tok(DKK, tkk, Btkk)
                    store_tok(DVV, tkv, Btkv)

                    pp, Bpp = proj(3584, 16)
                    gs, Bgs = gs_r.next()
                    S.op("act", lambda e: e.copy(out=gs[:, :NT], in_=pp[:16, 1:1 + NT]),
                         reads=[Bpp], writes=[Bgs])
                    S.dma("sp", lambda e: e.dma_start(out=GR[:, gt:gt + NT], in_=gs[:, :NT]), reads=[Bgs])

                work = [(b, t0, NT) for b in range(NSEQ) for (t0, NT) in tiles_of(seqs[b])]
                nxt = load_x(*work[0])
                cur_b = -1
                bcs = None
                for wi, (b, t0, NT) in enumerate(work):
                    xt, Bxt = nxt
                    if wi + 1 < len(work):
                        nxt = load_x(*work[wi + 1])
                    if b != cur_b:
                        cur_b = b
                        bcs = bcr.next()
                        load_bcast(bcs[0], bcs[1], b, (1, 0))
                    do_tile(b, t0, NT, xt, Bxt, bcs[0], bcs[1])
            S.barrier()

        if any(c in stages for c in "BCD"):
            with ExitStack() as esM:
                rbt, BcM = None, None
                wo = sb("wo", [128, 8, D], BF16, esM)
                don = sb("don", [128, 1], F32, esM)
                Bwo, BcD = Buf("wo"), Buf("constD")
                if "D" in stages:
                    zrow = sb("zrow", [1, D], F32, esM)
                    Bz = Buf("zrow")
                    S.op("pool", lambda e: e.memset(zrow[:], 0.0), writes=[Bz])
                    S.dma("pool", lambda e: e.dma_start(out=XN[0:1, :], in_=zrow[:]), reads=[Bz])
                    S.dma("pool", lambda e: e.dma_start(out=XN[NTOK + 1:NTOK + 2, :], in_=zrow[:]), reads=[Bz])
                    for kc in range(8):
                        S.dma("pool", lambda e, kc=kc: e.dma_start(out=wo[:, kc, :], in_=wo_d[kc * 128:(kc + 1) * 128, :]), writes=[Bwo])
                    S.dma("sp", lambda e: e.dma_start(out=don[:], in_=don_d), writes=[BcD])
                for b in range(NSEQ):
                    if "B" in stages:
                        stage_B(b, rbt, BcM)
                        S.barrier()
                    if "C" in stages:
                        stage_C(b)
                        S.barrier()
                    if "D" in stages:
                        stage_D(b, wo, Bwo, don, BcD)
                        S.barrier()

        if "E" in stages:
            stage_E()

        S.emit(nc)
    return nc


def _consts(nseq):
    ident = np.eye(128, dtype=np.float32)
    blk2 = np.zeros((128, 128), np.float32)
    blk2[:64, :64] = 1.0
    blk2[64:, 64:] = 1.0
    i = np.arange(128)[:, None]
    j = np.arange(128)[None, :]
    same = (i // 64) == (j // 64)
    mk = [same & (j < i), same & (j > i), same & (j >= i), same & (j <= i)]
    masks = np.stack([np.where(m, np.float32(0.0), np.float32(NEG)) for m in mk], axis=1).astype(np.float32)
    m01 = np.ones((4, 512), np.float32)
    m01[:, ::64] = 0.0
    mk7 = [(i // 8) == (j // 8)]
    for m_ in (8, 16, 32):
        mk7.append(((i // (2 * m_)) == (j // (2 * m_))) & ((i // m_) % 2 == 1) & ((j // m_) % 2 == 0))
    for m_ in (8, 16, 32):
        mk7.append(((i // (2 * m_)) == (j // (2 * m_))) & ((i // m_) % 2 == 0) & ((j // m_) % 2 == 1))
    mk7 = np.ascontiguousarray(np.stack(mk7, axis=1).astype(np.float32))
    return {"ident": ident, "blk2": blk2, "masks": np.ascontiguousarray(masks), "m01": m01, "mk7": mk7}


def _rb_table(att_rpb):
    rpb = np.asarray(att_rpb, np.float32)
    krl = np.arange(2)[:, None, None, None]
    kc = np.arange(64)[None, :, None, None]
    e = np.arange(16)[None, None, :, None]
    qc = np.arange(64)[None, None, None, :]
    dr = krl + 7 - e
    w = np.clip(qc - 8, 0, 48)
    ok = (dr >= -7) & (dr <= 7) & (kc >= w) & (kc < w + 16)
    ri = np.clip(dr + 7, 0, 14)
    ci = np.clip(kc - qc + 15, 0, 30)
    ri, ci, ok = np.broadcast_arrays(ri, ci, ok)
    tab = rpb[:, ri, ci]
    tab = np.where(ok[None], tab, np.float32(NEG)).astype(np.float32)
    return np.ascontiguousarray(tab.transpose(1, 2, 0, 3, 4).reshape(128, 8, 16, 64))


def shared_inputs(nseq, ada_w, ada_b, norm1_g, norm2_g, w_in, att_q_norm, att_k_norm, att_rpb, dn_conv_w,
                  dn_a_log, dn_dt_bias, dn_out_norm, w_o, ffn_w_up, ffn_conv_w, ffn_conv_b, ffn_w_down):
    f = lambda a: np.ascontiguousarray(np.asarray(a, np.float32))
    m = dict(_consts(nseq))
    m["ada_w"] = f(ada_w[0])
    m["ada_b"] = f(np.broadcast_to(np.asarray(ada_b[0])[None, :], (nseq, 6 * D)))
    m["n12"] = f(np.broadcast_to(np.stack([np.asarray(norm1_g[0]), np.asarray(norm2_g[0])])[None], (nseq, 2, D)))
    m["w_in"] = f(w_in[0])
    m["qkn"] = f(np.stack([np.tile(np.asarray(att_q_norm[0]), 2), np.tile(np.asarray(att_k_norm[0]), 2)], axis=1))
    m["rb"] = _rb_table(att_rpb[0])
    m["dcw"] = f(np.asarray(dn_conv_w[0]).reshape(3, 12, 128).transpose(2, 1, 0))
    m["dgate"] = f(np.stack([np.asarray(dn_a_log[0]).reshape(8), np.asarray(dn_dt_bias[0]).reshape(8)], axis=1))
    m["don"] = f(np.asarray(dn_out_norm[0]).reshape(128, 1))
    m["w_o"] = f(w_o[0])
    m["w_up"] = f(ffn_w_up[0])
    m["fcw"] = f(np.asarray(ffn_conv_w[0]).reshape(3, 44, 128).transpose(2, 1, 0))
    m["fcb"] = f(np.asarray(ffn_conv_b[0]).reshape(44, 128).T)
    m["w_dn"] = f(ffn_w_down[0])
    return m


def core_inputs(shared, xs, cs):
    m = dict(shared)
    x = np.concatenate([np.zeros((1, D), np.float32)] + [np.asarray(a, np.float32) for a in xs]
                       + [np.zeros((1, D), np.float32)], axis=0)
    m["x"] = np.ascontiguousarray(x)
    c = np.stack([np.asarray(a, np.float32) for a in cs])
    m["cT"] = np.ascontiguousarray(c.T.reshape(8, 128, len(cs)).transpose(1, 0, 2))
    return m


N_CORES = 8


def kernel(x_prompt, x_sample, c_prompt, c_sample, ada_w, ada_b, norm1_g, norm2_g, w_in,
           att_q_norm, att_k_norm, att_rpb, dn_conv_w, dn_a_log, dn_dt_bias, dn_out_norm, w_o,
           ffn_w_up, ffn_conv_w, ffn_conv_b, ffn_w_down):
    x_prompt = np.asarray(x_prompt, np.float32)
    x_sample = np.asarray(x_sample, np.float32)
    c_prompt = np.asarray(c_prompt, np.float32)
    c_sample = np.asarray(c_sample, np.float32)
    BP, TP, _ = x_prompt.shape
    BS, TS, _ = x_sample.shape
    ppc = BP // N_CORES
    seqs = [TP] * ppc + [TS]
    shared = shared_inputs(len(seqs), ada_w, ada_b, norm1_g, norm2_g, w_in, att_q_norm, att_k_norm, att_rpb,
                           dn_conv_w, dn_a_log, dn_dt_bias, dn_out_norm, w_o, ffn_w_up, ffn_conv_w, ffn_conv_b,
                           ffn_w_down)
    in_maps = []
    for c in range(N_CORES):
        xs = [x_prompt[c * ppc + i] for i in range(ppc)] + [x_sample[c * BS // N_CORES]]
        cs = [c_prompt[c * ppc + i] for i in range(ppc)] + [c_sample[c * BS // N_CORES]]
        in_maps.append(core_inputs(shared, xs, cs))
    nc = build(seqs)
    res = run_bass_kernel_spmd(nc, in_maps, core_ids=list(range(N_CORES)))
    y_prompt = np.empty_like(x_prompt)
    y_sample = np.empty_like(x_sample)
    half = TS // 2
    for c in range(N_CORES):
        y = np.asarray(res.results[c]["y"], np.float32)
        y_prompt[c * ppc:(c + 1) * ppc] = y[:ppc * TP].reshape(ppc, TP, D)
        sidx = c * BS // N_CORES
        ys = y[ppc * TP:].reshape(TS, D)
        if c % 2 == 0:
            y_sample[sidx, :half] = ys[:half]
        else:
            y_sample[sidx, half:] = ys[half:]
    return (y_prompt, y_sample)
```

```python
import math
from contextlib import ExitStack

import numpy as np
import concourse.bass as bass
import concourse.mybir as mybir
from concourse.bass_utils import run_bass_kernel_spmd

F32 = mybir.dt.float32
BF16 = mybir.dt.bfloat16
AF = mybir.ActivationFunctionType
ALU = mybir.AluOpType

D = 1024
GW = 64
NH_A, HD_A = 8, 64
NH_D, HD_D = 4, 128
DFF = 2816
INC = 3600
EPS = 1e-6
NEG = -30000.0

ENGS = ("pe", "act", "dve", "pool", "sp")
NDMA = 8


class Buf:
    __slots__ = ("name", "lw", "rd")

    def __init__(self, name):
        self.name = name
        self.lw = None
        self.rd = []


class Sched:
    def __init__(self):
        self.q = {e: [] for e in ENGS}
        self.cnt = {e: 0 for e in ENGS}
        self.known = {e: {} for e in ENGS}
        self.dma_n = {e: 0 for e in ENGS}
        self.dma_slot_val = {}
        self.pending = {e: {} for e in ENGS}
        self.nops = 0

    def _deps(self, eng, reads, writes):
        deps = dict(self.pending[eng])
        self.pending[eng] = {}

        def add(ev):
            if ev is None:
                return
            k, v = ev
            if deps.get(k, 0) < v:
                deps[k] = v
        for b in reads:
            add(b.lw)
        for b in writes:
            add(b.lw)
            for r in b.rd:
                add(r)
        return deps

    def _finish(self, eng, fn, deps, ev, reads, writes, inc):
        kn = self.known[eng]
        waits = []
        for k, v in deps.items():
            if kn.get(k, 0) >= v:
                continue
            if eng == "pe" and k == "pe":
                continue
            kn[k] = v
            waits.append((k, v))
        self.q[eng].append((fn, waits, inc))
        self.nops += 1
        for b in reads:
            b.rd.append(ev)
        for b in writes:
            b.lw = ev
            b.rd = []

    def op(self, eng, fn, reads=(), writes=()):
        deps = self._deps(eng, reads, writes)
        self.cnt[eng] += 1
        ev = (eng, self.cnt[eng])
        self._finish(eng, fn, deps, ev, reads, writes, (eng, 1))
        return ev

    def dma(self, queue, fn, reads=(), writes=()):
        deps = self._deps(queue, reads, writes)
        n = self.dma_n[queue]
        self.dma_n[queue] = n + 1
        key = ("dma", queue, n % NDMA)
        prev = self.dma_slot_val.get(key, 0)
        if prev and deps.get(key, 0) < prev:
            deps[key] = prev
        val = prev + 16
        self.dma_slot_val[key] = val
        ev = (key, val)
        self._finish(queue, fn, deps, ev, reads, writes, (key, 16))
        return ev

    def all_events(self):
        evs = {}
        for e in ENGS:
            if self.cnt[e]:
                evs[e] = self.cnt[e]
        for k, v in self.dma_slot_val.items():
            evs[k] = v
        return evs

    def barrier(self):
        evs = self.all_events()
        for e in ENGS:
            p = self.pending[e]
            for k, v in evs.items():
                if p.get(k, 0) < v:
                    p[k] = v

    def emit(self, nc):
        final = self.all_events()
        with ExitStack() as es:
            sems = {}
            for e in ENGS:
                sems[e] = es.enter_context(nc.semaphore("s_" + e))
            for q in ("sp", "pool"):
                for s in range(NDMA):
                    sems[("dma", q, s)] = es.enter_context(nc.semaphore("d_%s%d" % (q, s)))
            block = es.enter_context(nc.Block())

            def run(name):
                def body(eng):
                    for fn, waits, inc in self.q[name]:
                        for k, v in waits:
                            eng.wait_ge(sems[k], v)
                        fn(eng).then_inc(sems[inc[0]], inc[1])
                    if name == "sp":
                        for k, v in final.items():
                            eng.wait_ge(sems[k], v)
                return body

            block.tensor(run("pe"))
            block.scalar(run("act"))
            block.vector(run("dve"))
            block.gpsimd(run("pool"))
            block.sync(run("sp"))


_UID = [0]


class Rot:
    def __init__(self, es, nc, name, n, shape, dt, psum=False):
        self.t = []
        for i in range(n):
            _UID[0] += 1
            nm = "r_%s%d_%d" % (name, i, _UID[0])
            if psum:
                t = es.enter_context(nc.psum_tensor(nm, shape, dt))
            else:
                t = es.enter_context(nc.sbuf_tensor(nm, shape, dt))
            self.t.append((t, Buf(nm)))
        self.i = 0

    def next(self):
        r = self.t[self.i % len(self.t)]
        self.i += 1
        return r


def interleave(gens, width):
    it = iter(gens)
    active = []
    while True:
        while len(active) < width:
            g = next(it, None)
            if g is None:
                break
            active.append(g)
        if not active:
            break
        for g in list(active):
            try:
                next(g)
            except StopIteration:
                active.remove(g)


def tiles_of(T, mx=510):
    n = -(-T // mx)
    base, rem = divmod(T, n)
    out, t0 = [], 0
    for i in range(n):
        s = base + (1 if i < rem else 0)
        out.append((t0, s))
        t0 += s
    return out


def build(seqs, stages="0ABCDE", debug=False):
    NSEQ = len(seqs)
    NTOK = sum(seqs)
    offs = [sum(seqs[:i]) for i in range(NSEQ)]
    nc = bass.Bass("TRN2", target_bir_lowering=False)
    S = Sched()
    kind_dbg = "ExternalOutput" if debug else "Internal"

    def din(name, shape, dt=F32):
        return nc.dram_tensor(name, list(shape), dt, kind="ExternalInput").ap()

    def dscr(name, shape, dt):
        return nc.dram_tensor(name, list(shape), dt, kind=kind_dbg).ap()

    dumped = set()

    def dump(name, ap, B, dt=F32):
        if not debug or name in dumped:
            return
        dumped.add(name)
        t = nc.dram_tensor("dbg_" + name, list(ap.shape), dt, kind="ExternalOutput").ap()
        S.dma("pool", lambda e: e.dma_start(out=t, in_=ap), reads=[B])

    x_d = din("x", [NTOK + 2, D])
    cT_d = din("cT", [128, 8, NSEQ])
    adaw_d = din("ada_w", [D, 6 * D])
    adab_d = din("ada_b", [NSEQ, 6 * D])
    n12_d = din("n12", [NSEQ, 2, D])
    win_d = din("w_in", [D, INC])
    qkn_d = din("qkn", [128, 2])
    rb_d = din("rb", [128, NH_A, 16, GW])
    dcw_d = din("dcw", [128, 12, 3])
    dgate_d = din("dgate", [8, 2])
    don_d = din("don", [128, 1])
    wo_d = din("w_o", [D, D])
    wup_d = din("w_up", [D, 2 * DFF])
    fcw_d = din("fcw", [128, 44, 3])
    fcb_d = din("fcb", [128, 44])
    wdn_d = din("w_dn", [DFF, D])
    ident_d = din("ident", [128, 128])
    blk2_d = din("blk2", [128, 128])
    masks_d = din("masks", [128, 4, 128])
    m01_d = din("m01", [4, 512])
    mk7_d = din("mk7", [128, 7, 128])
    y_d = nc.dram_tensor("y", [NTOK, D], F32, kind="ExternalOutput").ap()

    QT = dscr("QT", [512, NTOK], BF16)
    KT = dscr("KT", [512, NTOK], BF16)
    VX = dscr("VX", [NTOK, 512], BF16)
    DQT = dscr("DQT", [512, NTOK], BF16)
    DKT = dscr("DKT", [512, NTOK], BF16)
    DKK = dscr("DKK", [NTOK, 512], BF16)
    DVV = dscr("DVV", [NTOK, 512], BF16)
    ZT = dscr("ZT", [512, NTOK], BF16)
    GR = dscr("GR", [16, NTOK], F32)
    MOD = dscr("MOD", [NSEQ, 6 * D], F32)
    ATT = dscr("ATT", [NTOK, 512], BF16)
    OSC = dscr("OSC", [NTOK, 512], F32)
    BCS = dscr("BCS", [24, 512], F32)
    OSB = dscr("OSB", [NTOK, 512], F32)
    XN = dscr("XN", [NTOK + 2, D], F32)
    WUB = dscr("WUB", [22, 128, 2, 8, 128], BF16)

    with ExitStack() as es0:
        def sb(name, shape, dt, es=es0):
            _UID[0] += 1
            return es.enter_context(nc.sbuf_tensor("s_%s_%d" % (name, _UID[0]), list(shape), dt))

        idf = sb("idf", [128, 128], F32)
        idb = sb("idb", [128, 128], BF16)
        blk2 = sb("blk2", [128, 128], BF16)
        ones_b = sb("ones_b", [128, 128], BF16)
        Bconst = Buf("const")
        S.dma("sp", lambda e: e.dma_start(out=idf[:], in_=ident_d), writes=[Bconst])
        S.dma("pool", lambda e: e.dma_start(out=blk2[:], in_=blk2_d), writes=[Bconst])
        S.op("dve", lambda e: e.tensor_copy(out=idb[:], in_=idf[:]), reads=[Bconst], writes=[Bconst])
        S.op("pool", lambda e: e.memset(ones_b[:], 1.0), writes=[Bconst])

        if "0" in stages:
            with ExitStack() as es:
                cT = sb("cT", [128, 8, NSEQ], F32, es)
                BcT = Buf("cT")
                modr = sb("modr", [NSEQ, 6 * D], F32, es)
                Bmod = Buf("modr")
                adab = sb("adab", [NSEQ, 6 * D], F32, es)
                Badab = Buf("adab")
                n12 = sb("n12", [NSEQ, 2, D], F32, es)
                Bn12 = Buf("n12")
                awr = Rot(es, nc, "aw", 2, [128, 8, 512], F32)
                pmod = Rot(es, nc, "pmod", 2, [NSEQ, 512], F32, psum=True)
                S.dma("sp", lambda e: e.dma_start(out=cT[:], in_=cT_d), writes=[BcT])
                S.dma("sp", lambda e: e.dma_start(out=adab[:], in_=adab_d), writes=[Badab])
                S.dma("sp", lambda e: e.dma_start(out=n12[:], in_=n12_d), writes=[Bn12])
                S.op("act", lambda e: e.activation(out=cT[:], in_=cT[:], func=AF.Silu), reads=[BcT], writes=[BcT])
                for nb in range(12):
                    aw, Baw = awr.next()
                    S.dma("sp", lambda e, aw=aw, nb=nb: e.dma_start(
                        out=aw[:], in_=adaw_d[:, nb * 512:(nb + 1) * 512].rearrange("(kc p) n -> p kc n", p=128)),
                        writes=[Baw])
                    pm, Bpm = pmod.next()
                    for kc in range(8):
                        S.op("pe", lambda e, pm=pm, aw=aw, kc=kc: e.matmul(
                            pm[:], lhsT=cT[:, kc, :], rhs=aw[:, kc, :], start=(kc == 0), stop=(kc == 7)),
                            reads=[BcT, Baw], writes=[Bpm])
                    S.op("dve", lambda e, pm=pm, nb=nb: e.tensor_tensor(
                        out=modr[:, nb * 512:(nb + 1) * 512], in0=pm[:], in1=adab[:, nb * 512:(nb + 1) * 512], op=ALU.add),
                        reads=[Bpm, Badab], writes=[Bmod])
                for j, c0 in ((0, D), (1, 4 * D)):
                    S.op("dve", lambda e, j=j, c0=c0: e.scalar_tensor_tensor(
                        out=modr[:, c0:c0 + D], in0=modr[:, c0:c0 + D], scalar=1.0, in1=n12[:, j, :],
                        op0=ALU.add, op1=ALU.mult), reads=[Bmod, Bn12], writes=[Bmod])
                S.dma("sp", lambda e: e.dma_start(out=MOD, in_=modr[:]), reads=[Bmod])
            S.barrier()

        def load_bcast(bc, Bbc, b, slots):
            for i, sl in enumerate(slots):
                S.dma("sp", lambda e, i=i, sl=sl: e.dma_start(
                    out=bc[:, i, :], in_=MOD[b:b + 1, sl * D:(sl + 1) * D].partition_broadcast(128)),
                    writes=[Bbc])


        def stage_B(b, rbt, BcM):
            T = seqs[b]
            Rr = T // GW
            NCH = Rr // 2
            o0 = offs[b]

            def r0_of(r):
                return min(max(r - 4, 0), Rr - 8)

            with ExitStack() as es:
                rbt = sb("rbt", [128, NH_A, 16, GW], F32, es)
                BcM = Buf("constM")
                S.dma("sp", lambda e: e.dma_start(out=rbt[:], in_=rb_d), writes=[BcM])
                qT = sb("qT", [128, 4, T], BF16, es)
                kT = sb("kT", [128, 4, T], BF16, es)
                vx = sb("vx", [128, NCH, NH_A, HD_A + 1], BF16, es)
                Bq, Bk, Bv = Buf("qT"), Buf("kT"), Buf("vx")
                for pr in range(4):
                    S.dma("sp", lambda e, pr=pr: e.dma_start(out=qT[:, pr, :], in_=QT[pr * 128:(pr + 1) * 128, o0:o0 + T]),
                          writes=[Bq])
                    S.dma("sp", lambda e, pr=pr: e.dma_start(out=kT[:, pr, :], in_=KT[pr * 128:(pr + 1) * 128, o0:o0 + T]),
                          writes=[Bk])
                S.op("pool", lambda e: e.memset(vx[:, :, :, HD_A:HD_A + 1], 1.0), writes=[Bv])
                for j in range(NCH):
                    S.dma("sp", lambda e, j=j: e.dma_start(
                        out=vx[:, j, :, 0:HD_A],
                        in_=VX[o0 + j * 128:o0 + (j + 1) * 128, :].rearrange("p (h d) -> p h d", d=HD_A)), writes=[Bv])
                ps_r = Rot(es, nc, "psB", 2, [128, 1024], F32, psum=True)
                po_r = Rot(es, nc, "poB", 2, [128, 4, HD_A + 1], F32, psum=True)
                ss_r = Rot(es, nc, "ssB", 2, [128, 768], F32)
                pT_r = Rot(es, nc, "pTB", 24, [128, 768], BF16)
                rd_r = Rot(es, nc, "rdB", 2, [128, 4, 1], F32)
                ab_r = Rot(es, nc, "abB", 3, [128, 4, HD_A], BF16)

                def do_hg(hg):
                    pts = {}

                    def do_chunk(j, hl):
                        h = hg * 4 + hl
                        pr, hb = h // 2, (h % 2) * 64
                        rows = [r for r in range(Rr) if r0_of(r) <= 2 * j + 1 and r0_of(r) + 7 >= 2 * j]
                        lo, nq = rows[0], len(rows)
                        e_lo = lo - 2 * j + 7
                        ps, Bps = ps_r.next()
                        n1 = min(512, nq * 64)
                        S.op("pe", lambda e: e.matmul(
                            ps[:, 0:n1], lhsT=kT[hb:hb + 64, pr, j * 128:(j + 1) * 128],
                            rhs=qT[hb:hb + 64, pr, lo * 64:lo * 64 + n1], start=True, stop=True),
                            reads=[Bk, Bq], writes=[Bps])
                        if nq * 64 > 512:
                            S.op("pe", lambda e: e.matmul(
                                ps[:, 512:nq * 64], lhsT=kT[hb:hb + 64, pr, j * 128:(j + 1) * 128],
                                rhs=qT[hb:hb + 64, pr, lo * 64 + 512:(lo + nq) * 64], start=True, stop=True),
                                reads=[Bk, Bq], writes=[Bps])
                        ss, Bss = ss_r.next()
                        pt, Bpt = pT_r.next()
                        for (a0, a1) in ((0, min(nq, 8)), (8, nq)):
                            if a1 <= a0:
                                continue
                            S.op("dve", lambda e, a0=a0, a1=a1: e.tensor_tensor(
                                out=ss[:, a0 * 64:a1 * 64].rearrange("p (a c) -> p a c", c=GW),
                                in0=ps[:, a0 * 64:a1 * 64].rearrange("p (a c) -> p a c", c=GW),
                                in1=rbt[:, h, e_lo + a0:e_lo + a1, :], op=ALU.add), reads=[Bps, BcM], writes=[Bss])
                        S.op("act", lambda e: e.activation(out=pt[:, 0:nq * 64], in_=ss[:, 0:nq * 64], func=AF.Exp),
                             reads=[Bss], writes=[Bpt])
                        for r in rows:
                            r0 = r0_of(r)
                            v0 = r0 <= 2 * j <= r0 + 7
                            v1 = r0 <= 2 * j + 1 <= r0 + 7
                            cc = (r - lo) * 64
                            if not v0:
                                S.op("pool", lambda e, cc=cc: e.memset(pt[0:64, cc:cc + 64], 0.0), reads=[], writes=[Bpt])
                            if not v1:
                                S.op("pool", lambda e, cc=cc: e.memset(pt[64:128, cc:cc + 64], 0.0), reads=[], writes=[Bpt])
                        pts[(j, hl)] = (pt, Bpt, lo)

                    def do_rowpair(rp):
                        po, Bpo = po_r.next()

                        def pv(r, hl):
                            h = hg * 4 + hl
                            r0 = r0_of(r)
                            chunks = list(range(r0 // 2, (r0 + 7) // 2 + 1))
                            ph = (r % 2) * 64
                            for ci, j in enumerate(chunks):
                                pt, Bpt, lo = pts[(j, hl)]
                                k0, k1 = 0, 128
                                c0 = (r - lo) * 64
                                S.op("pe", lambda e, pt=pt, j=j, k0=k0, k1=k1, c0=c0, ci=ci: e.matmul(
                                    po[ph:ph + 64, hl, :], lhsT=pt[k0:k1, c0:c0 + 64], rhs=vx[k0:k1, j, h, :],
                                    start=(ci == 0), stop=(ci == len(chunks) - 1)),
                                    reads=[Bpt, Bv], writes=[Bpo])
                        for r in (2 * rp, 2 * rp + 1):
                            for hl in range(4):
                                pv(r, hl)
                        rd, Brd = rd_r.next()
                        ab, Bab = ab_r.next()
                        S.op("dve", lambda e: e.reciprocal(out=rd[:], in_=po[:, :, HD_A:HD_A + 1]), reads=[Bpo], writes=[Brd])
                        S.op("dve", lambda e: e.tensor_tensor(
                            out=ab[:], in0=po[:, :, 0:HD_A], in1=rd[:].to_broadcast([128, 4, HD_A]),
                            op=ALU.mult), reads=[Bpo, Brd], writes=[Bab])
                        S.dma("pool", lambda e: e.dma_start(
                            out=ATT[o0 + rp * 128:o0 + (rp + 1) * 128, hg * 256:(hg + 1) * 256].rearrange(
                                "p (h d) -> p h d", d=HD_A), in_=ab[:]), reads=[Bab])

                    done_rp = 0
                    for j in range(NCH):
                        for hl in range(4):
                            do_chunk(j, hl)
                        while done_rp < Rr // 2:
                            need = max((r0_of(r) + 7) // 2 for r in (2 * done_rp, 2 * done_rp + 1))
                            if need > j:
                                break
                            do_rowpair(done_rp)
                            done_rp += 1

                for hg_ in range(2):
                    do_hg(hg_)


        def stage_C(b):
            T = seqs[b]
            o0 = offs[b]
            SEG = 512
            NSEG = T // SEG
            NCK = SEG // 64
            NB = T // 128
            BETA, EGC, EDK, NBEG, GB, NGC = range(6)
            with ExitStack() as es:
                masks = sb("masksC", [128, 4, 128], F32, es)
                id4 = sb("id4C", [128, 4, 128], BF16, es)
                m01 = sb("m01C", [4, SEG], F32, es)
                mk7f = sb("mk7fC", [128, 7, 128], F32, es)
                mk7 = sb("mk7C", [128, 7, 4, 128], BF16, es)
                BcC = Buf("constC")
                S.dma("sp", lambda e: e.dma_start(out=mk7f[:], in_=mk7_d), writes=[BcC])
                for h in range(4):
                    S.op("pool", lambda e, h=h: e.tensor_copy(out=mk7[:, :, h, :], in_=mk7f[:]), reads=[BcC], writes=[BcC])
                S.dma("sp", lambda e: e.dma_start(out=masks[:], in_=masks_d), writes=[BcC])
                S.dma("sp", lambda e: e.dma_start(out=m01[:], in_=m01_d), writes=[BcC])
                for h in range(4):
                    S.op("pool", lambda e, h=h: e.tensor_copy(out=id4[:, h, :], in_=idb[:]), reads=[Bconst], writes=[BcC])
                Sf = [sb("SfC%d" % d, [128, 4, 128], F32, es) for d in range(2)]
                Sb_ = [sb("SbC%d" % d, [128, 4, 128], BF16, es) for d in range(2)]
                BSf = [Buf("Sf0"), Buf("Sf1")]
                BSb = [Buf("Sb0"), Buf("Sb1")]
                dg = [sb("dgC%d" % d, [4, 4], F32, es) for d in range(2)]
                Bdg = [Buf("dg0"), Buf("dg1")]
                Bbcs = [Buf("BCS0"), Buf("BCS1")]
                Bosc = Buf("OSC")
                kT_r = [Rot(es, nc, "kTC%d" % d, 3, [128, 4, 128], BF16) for d in range(2)]
                qT_r = [Rot(es, nc, "qTC%d" % d, 3, [128, 4, 128], BF16) for d in range(2)]
                kk_r = [Rot(es, nc, "kkC%d" % d, 3, [128, 512], BF16) for d in range(2)]
                vv_r = Rot(es, nc, "vvC", 3, [128, 512], BF16)
                row_r = Rot(es, nc, "rowC", 16, [4, SEG], F32)
                bc_r = [Rot(es, nc, "bcC%d" % d, 1, [128, 8, 128], F32) for d in range(2)]
                egl_r = [Rot(es, nc, "eglC%d" % d, 2, [128, 4, NCK], F32) for d in range(2)]
                scs_r = [Rot(es, nc, "scsC%d" % d, 2, [128, SEG // 128, 24], F32) for d in range(2)]
                pp_r = Rot(es, nc, "ppC", 3, [128, 4, 128], F32, psum=True)
                pkq_r = Rot(es, nc, "pkqC", 2, [128, 4, 128], F32, psum=True)
                pvo_r = Rot(es, nc, "pvoC", 2, [128, 4, 128], F32, psum=True)
                pS_r = Rot(es, nc, "pSC", 1, [128, 4, 128], F32, psum=True)
                dm_r = Rot(es, nc, "dmC", 4, [128, 128], F32)
                E_r = [Rot(es, nc, "EC%d" % d, 4, [128, 4, 128], BF16) for d in range(2)]
                mb_r = [Rot(es, nc, "mbC%d" % d, 10, [128, 4, 128], BF16) for d in range(2)]
                m0_r = [Rot(es, nc, "m0C%d" % d, 2, [128, 4, 128], BF16) for d in range(2)]
                fin_r = [Rot(es, nc, "finC%d" % d, 6, [128, 4, 128], BF16) for d in range(2)]
                it_r = [Rot(es, nc, "itC%d" % d, 3, [128, 4, 128], BF16) for d in range(2)]
                bv_r = [Rot(es, nc, "bvC%d" % d, 3, [128, 4, 128], F32) for d in range(2)]
                r_r = [Rot(es, nc, "rC%d" % d, 2, [128, 4, 128], BF16) for d in range(2)]
                qs_r = [Rot(es, nc, "qsC%d" % d, 2, [128, 4, 128], F32) for d in range(2)]
                vn_r = [Rot(es, nc, "vnC%d" % d, 2, [128, 4, 128], BF16) for d in range(2)]
                vd_r = [Rot(es, nc, "vdC%d" % d, 2, [128, 4, 128], BF16) for d in range(2)]
                o_r = [Rot(es, nc, "oC%d" % d, 2, [128, 4, 128], F32) for d in range(2)]

                def seg_prep(d, sg):
                    t0 = o0 + sg * SEG
                    def row():
                        return row_r.next()
                    bl, Bbl = row(); al, Bal = row()
                    S.dma("sp", lambda e: e.dma_start(out=bl[:], in_=GR[d * 4:d * 4 + 4, t0:t0 + SEG]), writes=[Bbl])
                    S.dma("sp", lambda e: e.dma_start(out=al[:], in_=GR[8 + d * 4:12 + d * 4, t0:t0 + SEG]), writes=[Bal])
                    l1, Bl1 = row(); sp_, Bsp = row(); g, Bg = row(); pre, Bpre = row()
                    S.op("act", lambda e: e.activation(out=l1[:], in_=bl[:], func=AF.Exp, scale=-1.0), reads=[Bbl], writes=[Bl1])
                    S.op("act", lambda e: e.activation(out=l1[:], in_=l1[:], func=AF.Ln, bias=1.0), reads=[Bl1], writes=[Bl1])
                    S.op("act", lambda e: e.activation(out=sp_[:], in_=al[:], func=AF.Exp, bias=dg[d][:, 1:2]), reads=[Bal, Bdg[d]], writes=[Bsp])
                    S.op("act", lambda e: e.activation(out=sp_[:], in_=sp_[:], func=AF.Ln, bias=1.0), reads=[Bsp], writes=[Bsp])
                    S.op("dve", lambda e: e.tensor_scalar(out=g[:], in0=sp_[:], scalar1=dg[d][:, 2:3], scalar2=None, op0=ALU.mult),
                         reads=[Bsp, Bdg[d]], writes=[Bg])
                    S.op("dve", lambda e: e.tensor_tensor_scan(out=pre[:], data0=m01[:], data1=g[:], initial=0.0,
                                                               op0=ALU.mult, op1=ALU.add), reads=[Bg, BcC], writes=[Bpre])
                    tot = pre[:].rearrange("p (n c) -> p n c", c=64)[:, :, 63:64]
                    def v3(t):
                        return t[:].rearrange("p (n c) -> p n c", c=64)
                    if d == 0:
                        gc, Bgc = pre, Bpre
                    else:
                        gc, Bgc = row()
                        S.op("dve", lambda e: e.tensor_tensor(out=v3(gc), in0=tot.to_broadcast([4, NCK, 64]), in1=v3(pre),
                                                              op=ALU.subtract), reads=[Bpre], writes=[Bgc])
                        S.op("dve", lambda e: e.tensor_tensor(out=gc[:], in0=gc[:], in1=g[:], op=ALU.add),
                             reads=[Bgc, Bg], writes=[Bgc])
                    ngc, Bngc = row(); gb, Bgb = row(); egc, Begc = row(); beta, Bbeta = row()
                    nbeg, Bnbeg = row(); edk, Bedk = row(); egl, Begl = row()
                    S.op("dve", lambda e: e.tensor_scalar(out=ngc[:], in0=gc[:], scalar1=-1.0, scalar2=None, op0=ALU.mult),
                         reads=[Bgc], writes=[Bngc])
                    S.op("dve", lambda e: e.tensor_tensor(out=gb[:], in0=gc[:], in1=l1[:], op=ALU.subtract),
                         reads=[Bgc, Bl1], writes=[Bgb])
                    S.op("act", lambda e: e.activation(out=egc[:], in_=gc[:], func=AF.Exp), reads=[Bgc], writes=[Begc])
                    S.op("act", lambda e: e.activation(out=beta[:], in_=l1[:], func=AF.Exp, scale=-1.0), reads=[Bl1], writes=[Bbeta])
                    S.op("dve", lambda e: e.scalar_tensor_tensor(out=nbeg[:], in0=beta[:], scalar=-1.0, in1=egc[:],
                                                                 op0=ALU.mult, op1=ALU.mult), reads=[Bbeta, Begc], writes=[Bnbeg])
                    S.op("dve", lambda e: e.tensor_tensor(out=v3(edk), in0=tot.to_broadcast([4, NCK, 64]), in1=v3(gc),
                                                          op=ALU.subtract), reads=[Bpre, Bgc], writes=[Bedk])
                    S.op("act", lambda e: e.activation(out=edk[:], in_=edk[:], func=AF.Exp), reads=[Bedk], writes=[Bedk])
                    S.op("act", lambda e: e.activation(out=egl[:, 0:NCK].rearrange("p (n o) -> p n o", o=1), in_=tot,
                                                       func=AF.Exp), reads=[Bpre], writes=[Begl])
                    r0 = d * 12
                    S.dma("pool", lambda e: e.dma_start(out=BCS[r0:r0 + 4, 0:SEG], in_=ngc[:]), reads=[Bngc], writes=[Bbcs[d]])
                    S.dma("pool", lambda e: e.dma_start(out=BCS[r0 + 4:r0 + 8, 0:SEG], in_=gb[:]), reads=[Bgb], writes=[Bbcs[d]])
                    S.dma("pool", lambda e: e.dma_start(out=BCS[r0 + 8:r0 + 12, 0:NCK], in_=egl[:, 0:NCK]), reads=[Begl], writes=[Bbcs[d]])
                    eglb, Beglb = egl_r[d].next()
                    for h in range(4):
                        S.dma("sp", lambda e, h=h: e.dma_start(out=eglb[:, h, :], in_=BCS[r0 + 8 + h:r0 + 9 + h, 0:NCK].partition_broadcast(128)),
                              reads=[Bbcs[d]], writes=[Beglb])
                    scs, Bscs = scs_r[d].next()
                    rows = ((beta, Bbeta), (egc, Begc), (edk, Bedk), (nbeg, Bnbeg), (gb, Bgb), (ngc, Bngc))
                    for jb in range(SEG // 128):
                        psc, Bpsc = pp_r.next()
                        for qi, (rt, Brt) in enumerate(rows):
                            S.op("pe", lambda e, qi=qi, rt=rt, jb=jb, psc=psc: e.transpose(
                                out=psc[:, 0, qi * 4:qi * 4 + 4], in_=rt[:, jb * 128:(jb + 1) * 128], identity=idf[0:4, 0:4]),
                                reads=[Brt, Bconst], writes=[Bpsc])
                        S.op("dve", lambda e, jb=jb, psc=psc: e.tensor_copy(out=scs[:, jb, :], in_=psc[:, 0, 0:24]),
                             reads=[Bpsc], writes=[Bscs])
                    return dict(scs=scs, Bscs=Bscs, eglb=eglb, Beglb=Beglb)

                def mm4(lhs, Blhs, rhs, Brhs):
                    pt, Bpt = pp_r.next()
                    for h in range(4):
                        S.op("pe", lambda e, h=h: e.matmul(pt[:, h, :], lhsT=lhs[:, h, :], rhs=rhs[:, h, :], start=True, stop=True),
                             reads=[Blhs, Brhs], writes=[Bpt])
                    return pt, Bpt

                def evac(d, pt, Bpt, pool=None, acc=None):
                    t, Bt = (pool or mb_r[d]).next()
                    if acc is None:
                        S.op("act", lambda e: e.copy(out=t[:], in_=pt[:]), reads=[Bpt], writes=[Bt])
                    else:
                        S.op("dve", lambda e: e.tensor_tensor(out=t[:], in0=pt[:], in1=acc[0][:], op=ALU.add),
                             reads=[Bpt, acc[1]], writes=[Bt])
                    return t, Bt

                def masked(d, src, Bsrc, mi_, eng):
                    t, Bt = mb_r[d].next()
                    S.op(eng, lambda e: e.tensor_tensor(out=t[:], in0=src[:], in1=mk7[:, mi_, :, :], op=ALU.mult),
                         reads=[Bsrc, BcC], writes=[Bt])
                    return t, Bt

                def plus_id(d, src, Bsrc):
                    t, Bt = mb_r[d].next()
                    S.op("pool", lambda e: e.tensor_tensor(out=t[:], in0=src[:], in1=id4[:], op=ALU.add), reads=[Bsrc, BcC], writes=[Bt])
                    return t, Bt

                def inverse(d, Ml, BMl, Nu, BNu):
                    Q, BQ = masked(d, Ml, BMl, 0, "dve")
                    P, BP = masked(d, Nu, BNu, 0, "pool")
                    Z, BZ = plus_id(d, Q, BQ)
                    Y, BY = plus_id(d, P, BP)
                    yield
                    for lvl in range(2):
                        pQ, BpQ = mm4(P, BP, Q, BQ)
                        Q2, BQ2 = evac(d, pQ, BpQ)
                        pP, BpP = mm4(Q, BQ, P, BP)
                        P2, BP2 = evac(d, pP, BpP)
                        Q, BQ, P, BP = Q2, BQ2, P2, BP2
                        yield
                        pY, BpY = mm4(Q, BQ, Y, BY)
                        Y2, BY2 = evac(d, pY, BpY, acc=(Y, BY))
                        pZ, BpZ = mm4(P, BP, Z, BZ)
                        Z2, BZ2 = evac(d, pZ, BpZ, acc=(Z, BZ))
                        Y, BY, Z, BZ = Y2, BY2, Z2, BZ2
                        yield
                    X, BX, XT, BXT = Z, BZ, Y, BY
                    for li in range(3):
                        Mo, BMo = masked(d, Ml, BMl, 1 + li, "dve")
                        No, BNo = masked(d, Nu, BNu, 4 + li, "pool")
                        pB, BpB = mm4(No, BNo, X, BX)
                        Bm, BBm = evac(d, pB, BpB)
                        pB2, BpB2 = mm4(Mo, BMo, XT, BXT)
                        Bm2, BBm2 = evac(d, pB2, BpB2)
                        yield
                        fin = fin_r[d] if li == 2 else None
                        pC, BpC = mm4(XT, BXT, Bm, BBm)
                        Xn, BXn = evac(d, pC, BpC, fin, acc=(X, BX))
                        pC2, BpC2 = mm4(X, BX, Bm2, BBm2)
                        XTn, BXTn = evac(d, pC2, BpC2, fin, acc=(XT, BXT))
                        X, BX, XT, BXT = Xn, BXn, XTn, BXTn
                        yield
                    return X, BX, XT, BXT

                ready = [[], []]
                segcache = [{}, {}]

                def prep_thread(d):
                    order = range(NB) if d == 0 else range(NB - 1, -1, -1)
                    for bi in order:
                        while len(ready[d]) >= 1:
                            yield
                        sg, jb = divmod(bi, SEG // 128)
                        if sg not in segcache[d]:
                            segcache[d] = {sg: seg_prep(d, sg)}
                            yield
                        yield from prep_block(d, bi, sg, jb)

                def prep_block(d, bi, sg, jb):
                    sd = segcache[d][sg]
                    scs, Bsc = sd["scs"], sd["Bscs"]
                    gtok = o0 + bi * 128
                    kTb, BkT = kT_r[d].next(); qTb, BqT = qT_r[d].next(); kkb, Bkk = kk_r[d].next(); vvb, Bvv = vv_r.next()
                    bc, Bbc = bc_r[d].next()
                    S.dma("sp", lambda e: e.dma_start(out=kTb[:], in_=DKT[:, gtok:gtok + 128].rearrange("(h p) t -> p h t", p=128)), writes=[BkT])
                    S.dma("sp", lambda e: e.dma_start(out=qTb[:], in_=DQT[:, gtok:gtok + 128].rearrange("(h p) t -> p h t", p=128)), writes=[BqT])
                    S.dma("sp", lambda e: e.dma_start(out=kkb[:], in_=DKK[gtok:gtok + 128, :]), writes=[Bkk])
                    S.dma("sp", lambda e: e.dma_start(out=vvb[:], in_=DVV[gtok:gtok + 128, :]), writes=[Bvv])
                    r0 = d * 12
                    c0 = jb * 128
                    for i in range(8):
                        S.dma("sp", lambda e, i=i: e.dma_start(out=bc[:, i, :], in_=BCS[r0 + i:r0 + i + 1, c0:c0 + 128].partition_broadcast(128)),
                              reads=[Bbcs[d]], writes=[Bbc])
                    yield
                    E1, BE1 = E_r[d].next(); E1T, BE1T = E_r[d].next(); E2T, BE2T = E_r[d].next()
                    mi = (0, 1, 2) if d == 0 else (1, 0, 3)

                    def emat(E, BE, h, src_i, sgn, mk, bias_q):
                        dm, Bdm = dm_r.next()
                        S.op("pool", lambda e: e.tensor_tensor(
                            out=dm[:], in0=masks[:, mk, :], in1=bc[:, src_i * 4 + h, :], op=(ALU.add if sgn > 0 else ALU.subtract)),
                            reads=[Bbc, BcC], writes=[Bdm])
                        S.op("act", lambda e: e.activation(out=E[:, h, :], in_=dm[:], func=AF.Exp,
                                                           bias=scs[:, jb, bias_q * 4 + h:bias_q * 4 + h + 1]),
                             reads=[Bdm, Bsc], writes=[BE])
                    for h in range(4):
                        emat(E1, BE1, h, 0, 1.0, mi[0], GB)
                        emat(E1T, BE1T, h, 1, 1.0, mi[1], NGC)
                    yield
                    for h in range(4):
                        emat(E2T, BE2T, h, 0, -1.0, mi[2], NGC)
                    Q0, BQ0 = m0_r[d].next(); P0, BP0 = m0_r[d].next()
                    pG, BpG = pp_r.next()
                    for h in range(4):
                        S.op("pe", lambda e, h=h: e.matmul(pG[:, h, :], lhsT=kTb[:, h, :], rhs=kTb[:, h, :], start=True, stop=True),
                             reads=[BkT], writes=[BpG])
                    S.op("dve", lambda e: e.scalar_tensor_tensor(out=Q0[:], in0=pG[:], scalar=-1.0, in1=E1[:], op0=ALU.mult, op1=ALU.mult),
                         reads=[BpG, BE1], writes=[BQ0])
                    S.op("dve", lambda e: e.scalar_tensor_tensor(out=P0[:], in0=pG[:], scalar=-1.0, in1=E1T[:], op0=ALU.mult, op1=ALU.mult),
                         reads=[BpG, BE1T], writes=[BP0])
                    yield
                    pKQ, BpKQ = pp_r.next()
                    for h in range(4):
                        S.op("pe", lambda e, h=h: e.matmul(pKQ[:, h, :], lhsT=kTb[:, h, :], rhs=qTb[:, h, :], start=True, stop=True),
                             reads=[BkT, BqT], writes=[BpKQ])
                    iT, BiT = it_r[d].next()
                    S.op("dve", lambda e: e.tensor_tensor(out=iT[:], in0=pKQ[:], in1=E2T[:], op=ALU.mult),
                         reads=[BpKQ, BE2T], writes=[BiT])
                    bv, Bbv = bv_r[d].next()
                    S.op("pool", lambda e: e.tensor_tensor(
                        out=bv[:], in0=vvb[:].rearrange("p (h d) -> p h d", d=128),
                        in1=scs[:, jb, BETA * 4:BETA * 4 + 4].unsqueeze(2).to_broadcast([128, 4, 128]), op=ALU.mult),
                        reads=[Bvv, Bsc], writes=[Bbv])
                    yield
                    if d == 0:
                        X_, BX_, XT_, BXT_ = yield from inverse(d, Q0, BQ0, P0, BP0)
                        Y, BY = XT_, BXT_
                    else:
                        X_, BX_, XT_, BXT_ = yield from inverse(d, P0, BP0, Q0, BQ0)
                        Y, BY = X_, BX_
                    ready[d].append(dict(bi=bi, jb=jb, sd=sd, kTb=kTb, BkT=BkT, qTb=qTb, BqT=BqT, kkb=kkb, Bkk=Bkk,
                                         Y=Y, BY=BY, iT=iT, BiT=BiT, bv=bv, Bbv=Bbv))
                    yield

                def scan_thread(d):
                    for _ in range(NB):
                        while not ready[d]:
                            yield
                        yield from scan_block(d, ready[d].pop(0))

                def scan_block(d, pr):
                    ob, Bob = o_r[d].next()
                    for cp in ((0, 64) if d == 0 else (64, 0)):
                        yield from chunk(d, cp, pr, ob, Bob)
                    gtok = o0 + pr["bi"] * 128
                    dst = (OSC if d == 0 else OSB)[gtok:gtok + 128, :].rearrange("p (h d) -> p h d", d=128)
                    S.dma("pool", lambda e: e.dma_start(out=dst, in_=ob[:]), reads=[Bob])
                    yield

                def chunk(d, cp, pr, ob, Bob):
                    sd = pr["sd"]; jb = pr["jb"]
                    scs, Bsc, eglb, Beglb = sd["scs"], sd["Bscs"], sd["eglb"], sd["Beglb"]
                    kTb, BkT, qTb, BqT, kkb, Bkk = pr["kTb"], pr["BkT"], pr["qTb"], pr["BqT"], pr["kkb"], pr["Bkk"]
                    Y, BY, iT, BiT, bv, Bbv = pr["Y"], pr["BY"], pr["iT"], pr["BiT"], pr["bv"], pr["Bbv"]
                    ck = (jb * 128 + cp) // 64
                    sl = slice(cp, cp + 64)
                    def scol(q, h):
                        return scs[sl, jb, q * 4 + h:q * 4 + h + 1]
                    pk, Bpk = pkq_r.next(); pq, Bpq = pkq_r.next()
                    for h in range(4):
                        S.op("pe", lambda e, h=h: e.matmul(pk[sl, h, :], lhsT=kTb[:, h, cp:cp + 64], rhs=Sb_[d][:, h, :], start=True, stop=True),
                             reads=[BkT, BSb[d]], writes=[Bpk])
                    for h in range(4):
                        S.op("pe", lambda e, h=h: e.matmul(pq[sl, h, :], lhsT=qTb[:, h, cp:cp + 64], rhs=Sb_[d][:, h, :], start=True, stop=True),
                             reads=[BqT, BSb[d]], writes=[Bpq])
                    r, Br = r_r[d].next(); qs, Bqs = qs_r[d].next(); vn, Bvn = vn_r[d].next(); vd, Bvd = vd_r[d].next()
                    for h in range(4):
                        S.op("dve", lambda e, h=h: e.scalar_tensor_tensor(
                            out=r[sl, h, :], in0=pk[sl, h, :], scalar=scol(NBEG, h), in1=bv[sl, h, :],
                            op0=ALU.mult, op1=ALU.add), reads=[Bpk, Bsc, Bbv], writes=[Br])
                    for h in range(4):
                        S.op("act", lambda e, h=h: e.activation(out=qs[sl, h, :], in_=pq[sl, h, :], func=AF.Copy, scale=scol(EGC, h)),
                             reads=[Bpq, Bsc], writes=[Bqs])
                    yield
                    pv, Bpv = pvo_r.next()
                    for h in range(4):
                        S.op("pe", lambda e, h=h: e.matmul(pv[sl, h, :], lhsT=Y[sl, h, cp:cp + 64], rhs=r[sl, h, :], start=True, stop=True),
                             reads=[BY, Br], writes=[Bpv])
                    S.op("dve", lambda e: e.tensor_copy(out=vn[sl, :, :], in_=pv[sl, :, :]), reads=[Bpv], writes=[Bvn])
                    for h in range(4):
                        S.op("dve", lambda e, h=h: e.tensor_scalar(out=vd[sl, h, :], in0=pv[sl, h, :], scalar1=scol(EDK, h),
                                                                   scalar2=None, op0=ALU.mult), reads=[Bpv, Bsc], writes=[Bvd])
                    yield
                    po, Bpo = pvo_r.next()
                    for h in range(4):
                        S.op("pe", lambda e, h=h: e.matmul(po[sl, h, :], lhsT=iT[sl, h, cp:cp + 64], rhs=vn[sl, h, :], start=True, stop=True),
                             reads=[BiT, Bvn], writes=[Bpo])
                    S.op("dve", lambda e: e.tensor_tensor(out=ob[sl, :, :], in0=po[sl, :, :], in1=qs[sl, :, :], op=ALU.add),
                         reads=[Bpo, Bqs], writes=[Bob])
                    pS, BpS = pS_r.next()
                    for h in range(4):
                        S.op("pe", lambda e, h=h: e.matmul(pS[:, h, :], lhsT=kkb[sl, h * 128:(h + 1) * 128], rhs=vd[sl, h, :],
                                                           start=True, stop=True), reads=[Bkk, Bvd], writes=[BpS])
                    for h in range(4):
                        S.op("dve", lambda e, h=h: e.scalar_tensor_tensor(
                            out=Sf[d][:, h, :], in0=Sf[d][:, h, :], scalar=eglb[:, h, ck:ck + 1], in1=pS[:, h, :],
                            op0=ALU.mult, op1=ALU.add), reads=[BSf[d], Beglb, BpS], writes=[BSf[d]])
                    S.op("act", lambda e: e.copy(out=Sb_[d][:], in_=Sf[d][:]), reads=[BSf[d]], writes=[BSb[d]])
                    yield

                for d in range(2):
                    def init_dir(d):
                        S.dma("sp", lambda e: e.dma_start(out=dg[d][:, 0:2], in_=dgate_d[d * 4:d * 4 + 4, :]), writes=[Bdg[d]])
                        S.op("act", lambda e: e.activation(out=dg[d][:, 2:3], in_=dg[d][:, 0:1], func=AF.Exp), reads=[Bdg[d]], writes=[Bdg[d]])
                        S.op("dve", lambda e: e.tensor_scalar(out=dg[d][:, 2:3], in0=dg[d][:, 2:3], scalar1=-1.0, scalar2=None, op0=ALU.mult),
                             reads=[Bdg[d]], writes=[Bdg[d]])
                        S.op("pool", lambda e: e.memset(Sf[d][:], 0.0), writes=[BSf[d]])
                        S.op("pool", lambda e: e.memset(Sb_[d][:], 0.0), writes=[BSb[d]])
                    init_dir(d)
                interleave([prep_thread(0), scan_thread(0), prep_thread(1), scan_thread(1)], 4)

        def stage_D(b, wo, Bwo, don, BcD):
            T = seqs[b]
            o0 = offs[b]
            NTL = T // 128
            with ExitStack() as es:
                bc = sb("bcD", [128, D], F32, es)
                Bbc = Buf("bcD")
                S.dma("sp", lambda e: e.dma_start(out=bc[:], in_=MOD[b:b + 1, 2 * D:3 * D].partition_broadcast(128)), writes=[Bbc])
                at_r = Rot(es, nc, "atD", 5, [128, 512], BF16)
                os_r = Rot(es, nc, "osD", 5, [128, 4, 128], F32)
                osb_r = Rot(es, nc, "osbD", 5, [128, 4, 128], F32)
                zt_r = Rot(es, nc, "ztD", 5, [128, 4, 128], BF16)
                x_r = Rot(es, nc, "xD", 5, [128, D], F32)
                junk = sb("junkD", [128, 128], F32, es)
                ss_r = Rot(es, nc, "ssD", 5, [128, 4, 1], F32)
                on_r = Rot(es, nc, "onD", 5, [128, 4, 128], BF16)
                ct_r = Rot(es, nc, "ctD", 5, [128, 8, 128], BF16)
                ptr_r = Rot(es, nc, "ptrD", 2, [128, 8, 128], BF16, psum=True)
                po_r = Rot(es, nc, "poD", 4, [128, 512], F32, psum=True)
                t_r = Rot(es, nc, "tD", 4, [128, D], F32)
                xn_r = Rot(es, nc, "xnD", 4, [128, D], F32)

                def loads(i):
                    g = o0 + i * 128
                    at, Bat = at_r.next(); osc, Bos = os_r.next(); zt, Bzt = zt_r.next(); xt, Bxt = x_r.next()
                    S.dma("sp", lambda e: e.dma_start(out=at[:], in_=ATT[g:g + 128, :]), writes=[Bat])
                    S.dma("sp", lambda e: e.dma_start(out=osc[:], in_=OSC[g:g + 128, :].rearrange("p (h d) -> p h d", d=128)), writes=[Bos])
                    osb, Bosb = osb_r.next()
                    S.dma("sp", lambda e: e.dma_start(out=osb[:], in_=OSB[g:g + 128, :].rearrange("p (h d) -> p h d", d=128)), writes=[Bosb])
                    S.op("pool", lambda e: e.tensor_tensor(out=osc[:], in0=osc[:], in1=osb[:], op=ALU.add), reads=[Bos, Bosb], writes=[Bos])
                    S.dma("sp", lambda e: e.dma_start(out=zt[:], in_=ZT[:, g:g + 128].rearrange("(h p) t -> p h t", p=128)), writes=[Bzt])
                    S.dma("sp", lambda e: e.dma_start(out=xt[:], in_=x_d[g + 1:g + 129, :]), writes=[Bxt])
                    return (at, Bat, osc, Bos, zt, Bzt, xt, Bxt)

                def tile(i):
                    at, Bat, osc, Bos, zt, Bzt, xt, Bxt = loads(i)
                    g = o0 + i * 128
                    ss, Bss = ss_r.next(); on, Bon = on_r.next(); ct, Bct = ct_r.next()
                    yield
                    for h in range(4):
                        S.op("act", lambda e, h=h: e.activation(out=junk[:], in_=osc[:, h, :], func=AF.Square, accum_out=ss[:, h, :]),
                             reads=[Bos], writes=[Bss])
                    S.op("act", lambda e: e.activation(out=ss[:], in_=ss[:], func=AF.Ln, scale=1.0 / HD_D, bias=EPS), reads=[Bss], writes=[Bss])
                    S.op("act", lambda e: e.activation(out=ss[:], in_=ss[:], func=AF.Exp, scale=-0.5), reads=[Bss], writes=[Bss])
                    yield
                    S.op("dve", lambda e: e.tensor_tensor(out=on[:], in0=osc[:], in1=ss[:].to_broadcast([128, 4, 128]), op=ALU.mult),
                         reads=[Bos, Bss], writes=[Bon])
                    yield
                    ptr, Bptr = ptr_r.next()
                    for j in range(4):
                        S.op("pe", lambda e, j=j: e.transpose(out=ptr[:, j, :], in_=at[:, j * 128:(j + 1) * 128], identity=idb[:]),
                             reads=[Bat, Bconst], writes=[Bptr])
                    for h in range(4):
                        S.op("pe", lambda e, h=h: e.transpose(out=ptr[:, 4 + h, :], in_=on[:, h, :], identity=idb[:]),
                             reads=[Bon, Bconst], writes=[Bptr])
                    S.op("act", lambda e: e.copy(out=ct[:, 0:4, :], in_=ptr[:, 0:4, :]), reads=[Bptr], writes=[Bct])
                    S.op("dve", lambda e: e.scalar_tensor_tensor(out=ct[:, 4:8, :], in0=ptr[:, 4:8, :], scalar=don[:, 0:1], in1=zt[:],
                                                                 op0=ALU.mult, op1=ALU.mult), reads=[Bptr, Bzt, BcD], writes=[Bct])
                    yield
                    xn, Bxn = xn_r.next()
                    tt, Btt = t_r.next()
                    for hf in range(2):
                        po, Bpo = po_r.next()
                        for kc in range(8):
                            S.op("pe", lambda e, kc=kc, po=po, hf=hf: e.matmul(
                                po[:], lhsT=ct[:, kc, :], rhs=wo[:, kc, hf * 512:(hf + 1) * 512], start=(kc == 0), stop=(kc == 7)),
                                reads=[Bct, Bwo], writes=[Bpo])
                        S.op("dve", lambda e, po=po, hf=hf: e.tensor_tensor(
                            out=tt[:, hf * 512:(hf + 1) * 512], in0=po[:], in1=bc[:, hf * 512:(hf + 1) * 512], op=ALU.mult),
                            reads=[Bpo, Bbc], writes=[Btt])
                    yield
                    S.op("pool", lambda e: e.tensor_tensor(out=xn[:], in0=tt[:], in1=xt[:], op=ALU.add), reads=[Btt, Bxt], writes=[Bxn])
                    S.dma("pool", lambda e: e.dma_start(out=XN[g + 1:g + 129, :], in_=xn[:]), reads=[Bxn])

                interleave([tile(i) for i in range(NTL)], 3)

        def stage_E():
            with ExitStack() as es:
                wdn = sb("wdn", [128, 22, D], BF16, es)
                Bwdn = Buf("wdn")
                for c in range(22):
                    S.dma("pool", lambda e, c=c: e.dma_start(out=wdn[:, c, :], in_=wdn_d[c * 128:(c + 1) * 128, :]), writes=[Bwdn])
                fcw = sb("fcw", [128, 44, 3], F32, es)
                fcb = sb("fcb", [128, 44], F32, es)
                BcE = Buf("constE")
                S.dma("sp", lambda e: e.dma_start(out=fcw[:], in_=fcw_d), writes=[BcE])
                S.dma("sp", lambda e: e.dma_start(out=fcb[:], in_=fcb_d), writes=[BcE])
                wu_r = Rot(es, nc, "wuE", 3, [128, 2, 8, 128], BF16)
                Bwub = Buf("WUB")
                for c in range(22):
                    def cast_pair(c):
                        wu, Bwu = wu_r.next()
                        for ab in range(2):
                            col = ab * DFF + c * 128
                            S.dma("pool", lambda e, ab=ab, col=col: e.dma_start(
                                out=wu[:, ab, :, :], in_=wup_d[:, col:col + 128].rearrange("(kc p) n -> p kc n", p=128)), writes=[Bwu])
                        S.dma("sp", lambda e: e.dma_start(out=WUB[c], in_=wu[:]), reads=[Bwu], writes=[Bwub])
                    cast_pair(c)
                bcr = Rot(es, nc, "bcE", 1, [128, 3, D], F32)
                xtr = Rot(es, nc, "xtE", 2, [128, 4, D], F32)
                h2r = Rot(es, nc, "h2T", 2, [128, 8, 512], BF16)
                junk = sb("junkE", [128, D], F32, es)
                st_r = Rot(es, nc, "stE", 4, [128, 4], F32)
                hf_r = Rot(es, nc, "hfE", 1, [128, D], F32)
                hb_r = Rot(es, nc, "hbE", 2, [128, D], BF16)
                gt_r = Rot(es, nc, "gtE", 1, [128, 22, 512], BF16)
                ca_r = Rot(es, nc, "caE", 3, [128, 512], F32)
                cb_r = Rot(es, nc, "cbE", 3, [128, 512], F32)
                sa_r = Rot(es, nc, "saE", 3, [128, 512], F32)
                yt_r = Rot(es, nc, "ytE", 2, [128, D], F32)
                ptr_r = Rot(es, nc, "ptrE", 2, [128, 8, 128], BF16, psum=True)
                pab_r = Rot(es, nc, "pabE", 4, [128, 512], F32, psum=True)
                po_r = Rot(es, nc, "poE", 2, [128, 512], F32, psum=True)
                for t, B_ in xtr.t + gt_r.t:
                    S.op("pool", lambda e, t=t: e.memset(t[:], 0.0), writes=[B_])

                def load_x(b, t0, NT):
                    xt, Bxt = xtr.next()
                    NC = NT + 2
                    g0 = offs[b] + t0
                    for s in range(-(-NC // 128)):
                        rs = min(128, NC - s * 128)
                        S.dma("sp", lambda e, s=s, rs=rs: e.dma_start(
                            out=xt[:rs, s, :], in_=XN[g0 + s * 128: g0 + s * 128 + rs, :]), writes=[Bxt])
                    return xt, Bxt

                def do_tile(b, t0, NT, xt, Bxt, bc, Bbc):
                    NC = NT + 2
                    nsubc = -(-NC // 128)
                    h2T, Bh2 = h2r.next()
                    gt = offs[b] + t0

                    def norm_sub(s):
                        rs = min(128, NC - s * 128)
                        st, Bst = st_r.next(); hf, Bhf = hf_r.next(); hb, Bhb = hb_r.next(); ptr, Bptr = ptr_r.next()
                        S.op("act", lambda e: e.activation(out=junk[:rs, :], in_=xt[:rs, s, :], func=AF.Square, accum_out=st[:rs, 0:1]),
                             reads=[Bxt], writes=[Bst])
                        S.op("act", lambda e: e.activation(out=st[:rs, 1:2], in_=st[:rs, 0:1], func=AF.Ln, scale=1.0 / D, bias=EPS),
                             reads=[Bst], writes=[Bst])
                        S.op("act", lambda e: e.activation(out=st[:rs, 2:3], in_=st[:rs, 1:2], func=AF.Exp, scale=-0.5), reads=[Bst], writes=[Bst])
                        S.op("dve", lambda e: e.scalar_tensor_tensor(out=hf[:rs, :], in0=xt[:rs, s, :], scalar=st[:rs, 2:3], in1=bc[:rs, 0, :],
                                                                     op0=ALU.mult, op1=ALU.mult), reads=[Bxt, Bst, Bbc], writes=[Bhf])
                        S.op("pool", lambda e: e.tensor_tensor(out=hb[:rs, :], in0=hf[:rs, :], in1=bc[:rs, 1, :], op=ALU.add),
                             reads=[Bhf, Bbc], writes=[Bhb])
                        for kc in range(8):
                            S.op("pe", lambda e, kc=kc: e.transpose(out=ptr[:, kc, 0:rs], in_=hb[:rs, kc * 128:(kc + 1) * 128],
                                                                    identity=idb[:rs, :rs]), reads=[Bhb, Bconst], writes=[Bptr])
                        S.op("act", lambda e: e.copy(out=h2T[:, :, s * 128:s * 128 + rs], in_=ptr[:, :, 0:rs]), reads=[Bptr], writes=[Bh2])
                    for s in range(nsubc):
                        norm_sub(s)
                    if t0 == 0:
                        S.op("pool", lambda e: e.memset(h2T[:, :, 0:1], 0.0), writes=[Bh2])
                    if t0 + NT == seqs[b]:
                        S.op("pool", lambda e: e.memset(h2T[:, :, NC - 1:NC], 0.0), writes=[Bh2])

                    gtd, Bgt = gt_r.next()

                    def pair(c):
                        wu, Bwu = wu_r.next()
                        S.dma("sp", lambda e: e.dma_start(out=wu[:], in_=WUB[c]), reads=[Bwub], writes=[Bwu])
                        res = []
                        for ab, tr in ((0, ca_r), (1, cb_r)):
                            pp, Bpp = pab_r.next()
                            for kc in range(8):
                                S.op("pe", lambda e, kc=kc, ab=ab, pp=pp: e.matmul(pp[:, :NC], lhsT=wu[:, ab, kc, :], rhs=h2T[:, kc, 0:NC],
                                                                                 start=(kc == 0), stop=(kc == 7)), reads=[Bwu, Bh2], writes=[Bpp])
                            cv, Bcv = tr.next()
                            ch = ab * 22 + c
                            S.op("act", lambda e, pp=pp, cv=cv, ch=ch: e.activation(out=cv[:, :NT], in_=pp[:, 1:1 + NT], func=AF.Identity,
                                                                                   scale=fcw[:, ch, 1:2], bias=fcb[:, ch:ch + 1]),
                                 reads=[Bpp, BcE], writes=[Bcv])
                            res.append((pp, Bpp, cv, Bcv, ch))
                        yield
                        for (pp, Bpp, cv, Bcv, ch) in res:
                            S.op("dve", lambda e, pp=pp, cv=cv, ch=ch: e.scalar_tensor_tensor(
                                out=cv[:, :NT], in0=pp[:, 0:NT], scalar=fcw[:, ch, 0:1], in1=cv[:, :NT],
                                op0=ALU.mult, op1=ALU.add), reads=[Bpp, BcE, Bcv], writes=[Bcv])
                            S.op("dve", lambda e, pp=pp, cv=cv, ch=ch: e.scalar_tensor_tensor(
                                out=cv[:, :NT], in0=pp[:, 2:2 + NT], scalar=fcw[:, ch, 2:3], in1=cv[:, :NT],
                                op0=ALU.mult, op1=ALU.add), reads=[Bpp, BcE, Bcv], writes=[Bcv])
                        yield
                        ca, Bca = res[0][2], res[0][3]
                        cb2, Bcb2 = res[1][2], res[1][3]
                        sa, Bsa = sa_r.next()
                        S.op("act", lambda e: e.activation(out=sa[:, :NT], in_=ca[:, :NT], func=AF.Silu), reads=[Bca], writes=[Bsa])
                        S.op("pool", lambda e: e.tensor_tensor(out=gtd[:, c, 1:1 + NT], in0=sa[:, :NT], in1=cb2[:, :NT], op=ALU.mult),
                             reads=[Bsa, Bcb2], writes=[Bgt])
                    interleave([pair(c) for c in range(22)], 2)

                    def down(s):
                        rs = min(128, NC - s * 128)
                        yt, Byt = yt_r.next()
                        for hf_ in range(2):
                            def dhalf(hf_):
                                po, Bpo = po_r.next()
                                for c in range(22):
                                    S.op("pe", lambda e, c=c: e.matmul(po[:rs, :], lhsT=gtd[:, c, s * 128:s * 128 + rs],
                                                                       rhs=wdn[:, c, hf_ * 512:(hf_ + 1) * 512], start=(c == 0), stop=(c == 21)),
                                         reads=[Bgt, Bwdn], writes=[Bpo])
                                S.op("dve", lambda e: e.tensor_tensor(out=yt[:rs, hf_ * 512:(hf_ + 1) * 512], in0=po[:rs, :],
                                                                      in1=bc[:rs, 2, hf_ * 512:(hf_ + 1) * 512], op=ALU.mult),
                                     reads=[Bpo, Bbc], writes=[Byt])
                            dhalf(hf_)
                        S.op("pool", lambda e: e.tensor_tensor(out=yt[:rs, :], in0=yt[:rs, :], in1=xt[:rs, s, :], op=ALU.add),
                             reads=[Byt, Bxt], writes=[Byt])
                        p_lo = 1 if s == 0 else 0
                        p_hi = min(rs, NT + 1 - s * 128)
                        if p_hi > p_lo:
                            tok = gt - 1 + s * 128
                            S.dma("pool", lambda e: e.dma_start(out=y_d[tok + p_lo:tok + p_hi, :], in_=yt[p_lo:p_hi, :]), reads=[Byt])
                    for s in range(nsubc):
                        down(s)

                work = [(b, t0, NT) for b in range(NSEQ) for (t0, NT) in tiles_of(seqs[b])]
                nxt = load_x(*work[0])
                cur_b = -1
                bcs = None
                for wi, (b, t0, NT) in enumerate(work):
                    xt, Bxt = nxt
                    if wi + 1 < len(work):
                        nxt = load_x(*work[wi + 1])
                    if b != cur_b:
                        cur_b = b
                        bcs = bcr.next()
                        load_bcast(bcs[0], bcs[1], b, (4, 3, 5))
                    do_tile(b, t0, NT, xt, Bxt, bcs[0], bcs[1])

        if "A" in stages:
            with ExitStack() as es:
                win = sb("win", [128, 8, INC], BF16, es)
                Bwin = Buf("win")
                for kc in range(8):
                    S.dma("pool", lambda e, kc=kc: e.dma_start(out=win[:, kc, :], in_=win_d[kc * 128:(kc + 1) * 128, :]),
                          writes=[Bwin])
                qkn = sb("qkn", [128, 2], F32, es)
                dcw = sb("dcw", [128, 12, 3], F32, es)
                BcA = Buf("constA")
                S.dma("sp", lambda e: e.dma_start(out=qkn[:], in_=qkn_d), writes=[BcA])
                S.dma("sp", lambda e: e.dma_start(out=dcw[:], in_=dcw_d), writes=[BcA])
                S.op("dve", lambda e: e.tensor_scalar(out=qkn[:, 0:1], in0=qkn[:, 0:1], scalar1=HD_A ** -0.5, scalar2=None,
                                                      op0=ALU.mult), reads=[BcA], writes=[BcA])
                bcr = Rot(es, nc, "bcA", 2, [128, 2, D], F32)
                xtr = Rot(es, nc, "xt", 2, [128, 4, D], F32)
                h1r = Rot(es, nc, "h1T", 2, [128, 8, 512], BF16)
                junk = sb("junkA", [128, D], F32, es)
                st_r = Rot(es, nc, "stA", 4, [128, 4], F32)
                hf_r = Rot(es, nc, "hfA", 2, [128, D], F32)
                hb_r = Rot(es, nc, "hbA", 2, [128, D], BF16)
                ptr_r = Rot(es, nc, "ptrA", 2, [128, 8, 128], BF16, psum=True)
                pp_r = Rot(es, nc, "ppA", 4, [128, 512], F32, psum=True)
                pss_r = Rot(es, nc, "pssA", 2, [128, 512], F32, psum=True)
                sqb_r = Rot(es, nc, "sqbA", 4, [128, 512], BF16)
                rs1_r = Rot(es, nc, "rs1A", 4, [128, 512], F32)
                rs2_r = Rot(es, nc, "rs2A", 4, [128, 512], F32)
                ob_r = Rot(es, nc, "obA", 6, [128, 512], BF16)
                cb_r = Rot(es, nc, "cbA", 4, [128, 512], F32)
                sl_r = Rot(es, nc, "slA", 4, [128, 512], F32)
                ee_r = Rot(es, nc, "eeA", 4, [128, 512], F32)
                tok_r = Rot(es, nc, "tokA", 3, [128, 4, 512], BF16)
                gs_r = Rot(es, nc, "gsA", 2, [16, 512], F32)
                for t, _ in xtr.t:
                    S.op("pool", lambda e, t=t: e.memset(t[:], 0.0), writes=[_])

                def load_x(b, t0, NT):
                    xt, Bxt = xtr.next()
                    NC = NT + 2
                    g0 = offs[b] + t0
                    for s in range(-(-NC // 128)):
                        rs = min(128, NC - s * 128)
                        S.dma("sp", lambda e, xt=xt, s=s, rs=rs, g0=g0: e.dma_start(
                            out=xt[:rs, s, :], in_=x_d[g0 + s * 128: g0 + s * 128 + rs, :]), writes=[Bxt])
                    return xt, Bxt

                def do_tile(b, t0, NT, xt, Bxt, bc, Bbc):
                    NC = NT + 2
                    h1T, Bh1 = h1r.next()
                    gt = offs[b] + t0
                    nsub = -(-NT // 128)

                    def norm_sub(s):
                        rs = min(128, NC - s * 128)
                        st, Bst = st_r.next()
                        hf, Bhf = hf_r.next()
                        hb, Bhb = hb_r.next()
                        ptr, Bptr = ptr_r.next()
                        S.op("act", lambda e: e.activation(
                            out=junk[:rs, :], in_=xt[:rs, s, :], func=AF.Square, accum_out=st[:rs, 0:1]),
                            reads=[Bxt], writes=[Bst])
                        S.op("act", lambda e: e.activation(
                            out=st[:rs, 1:2], in_=st[:rs, 0:1], func=AF.Ln, scale=1.0 / D, bias=EPS),
                            reads=[Bst], writes=[Bst])
                        S.op("act", lambda e: e.activation(out=st[:rs, 2:3], in_=st[:rs, 1:2], func=AF.Exp, scale=-0.5),
                             reads=[Bst], writes=[Bst])
                        S.op("dve", lambda e: e.scalar_tensor_tensor(
                            out=hf[:rs, :], in0=xt[:rs, s, :], scalar=st[:rs, 2:3], in1=bc[:rs, 0, :],
                            op0=ALU.mult, op1=ALU.mult), reads=[Bxt, Bst, Bbc], writes=[Bhf])
                        S.op("pool", lambda e: e.tensor_tensor(
                            out=hb[:rs, :], in0=hf[:rs, :], in1=bc[:rs, 1, :], op=ALU.add),
                            reads=[Bhf, Bbc], writes=[Bhb])
                        for kc in range(8):
                            S.op("pe", lambda e, kc=kc: e.transpose(
                                out=ptr[:, kc, 0:rs], in_=hb[:rs, kc * 128:(kc + 1) * 128], identity=idb[:rs, :rs]),
                                reads=[Bhb, Bconst], writes=[Bptr])
                        S.op("act", lambda e: e.copy(
                            out=h1T[:, :, s * 128:s * 128 + rs], in_=ptr[:, :, 0:rs]), reads=[Bptr], writes=[Bh1])

                    for s in range(-(-NC // 128)):
                        norm_sub(s)
                    if t0 == 0:
                        S.op("pool", lambda e: e.memset(h1T[:, :, 0:1], 0.0), writes=[Bh1])
                    if t0 + NT == seqs[b]:
                        S.op("pool", lambda e: e.memset(h1T[:, :, NC - 1:NC], 0.0), writes=[Bh1])

                    def proj(c0, ncols):
                        pp, Bpp = pp_r.next()
                        for kc in range(8):
                            S.op("pe", lambda e, kc=kc: e.matmul(
                                pp[:ncols, :NC], lhsT=win[:, kc, c0:c0 + ncols], rhs=h1T[:, kc, 0:NC],
                                start=(kc == 0), stop=(kc == 7)), reads=[Bwin, Bh1], writes=[Bpp])
                        return pp, Bpp

                    def inv_norm(src, Bsrc, lhs, scale):
                        sqb, Bsq = sqb_r.next()
                        pss, Bps = pss_r.next()
                        rs1, Br1 = rs1_r.next()
                        rs2, Br2 = rs2_r.next()
                        S.op("act", lambda e: e.activation(out=sqb[:, :NT], in_=src, func=AF.Square),
                             reads=[Bsrc], writes=[Bsq])
                        yield
                        S.op("pe", lambda e: e.matmul(pss[:, :NT], lhsT=lhs[:], rhs=sqb[:, :NT], start=True, stop=True),
                             reads=[Bsq, Bconst], writes=[Bps])
                        S.op("act", lambda e: e.activation(out=rs1[:, :NT], in_=pss[:, :NT], func=AF.Ln,
                                                           scale=scale, bias=EPS), reads=[Bps], writes=[Br1])
                        S.op("act", lambda e: e.activation(out=rs2[:, :NT], in_=rs1[:, :NT], func=AF.Exp, scale=-0.5),
                             reads=[Br1], writes=[Br2])
                        yield
                        return rs2, Br2

                    def to_tok(src, Bsrc, tk, Btk, col):
                        def one(s):
                            rs = min(128, NT - s * 128)
                            ptr, Bptr = ptr_r.next()
                            S.op("pe", lambda e: e.transpose(
                                out=ptr[:rs, 0, :], in_=src[:, s * 128:s * 128 + rs], identity=idb[:]),
                                reads=[Bsrc, Bconst], writes=[Bptr])
                            S.op("act", lambda e: e.copy(
                                out=tk[:rs, s, col:col + 128], in_=ptr[:rs, 0, :]), reads=[Bptr], writes=[Btk])
                        for s in range(nsub):
                            one(s)

                    def store_tok(dst, tk, Btk):
                        for s in range(nsub):
                            rs = min(128, NT - s * 128)
                            S.dma("sp", lambda e, s=s, rs=rs: e.dma_start(
                                out=dst[gt + s * 128: gt + s * 128 + rs, :], in_=tk[:rs, s, :]), reads=[Btk])

                    tk, Btk = tok_r.next()
                    tkk, Btkk = tok_r.next()
                    tkv, Btkv = tok_r.next()

                    def silu(dst, Bdst, src, Bsrc):
                        ee, Bee = ee_r.next()
                        S.op("act", lambda e: e.activation(out=ee[:, :NT], in_=src, func=AF.Exp, scale=-1.0),
                             reads=[Bsrc], writes=[Bee])
                        yield
                        S.op("act", lambda e: e.activation(out=ee[:, :NT], in_=ee[:, :NT], func=AF.Ln, bias=1.0), reads=[Bee], writes=[Bee])
                        S.op("act", lambda e: e.activation(out=ee[:, :NT], in_=ee[:, :NT], func=AF.Exp, scale=-1.0), reads=[Bee], writes=[Bee])
                        S.op("dve", lambda e: e.tensor_tensor(out=dst[:, :NT], in0=src, in1=ee[:, :NT], op=ALU.mult),
                             reads=[Bsrc, Bee], writes=[Bdst])
                        yield

                    def qk_chunk(cb):
                        pp, Bpp = proj(cb * 128, 128)
                        rs2, Br2 = yield from inv_norm(pp[:, 1:1 + NT], Bpp, blk2, 1.0 / HD_A)
                        ob, Bob = ob_r.next()
                        j = 0 if cb < 4 else 1
                        S.op("dve", lambda e: e.scalar_tensor_tensor(
                            out=ob[:, :NT], in0=pp[:, 1:1 + NT], scalar=qkn[:, j:j + 1], in1=rs2[:, :NT],
                            op0=ALU.mult, op1=ALU.mult), reads=[Bpp, Br2, BcA], writes=[Bob])
                        dst = QT if cb < 4 else KT
                        r0 = (cb % 4) * 128
                        S.dma("sp", lambda e: e.dma_start(
                            out=dst[r0:r0 + 128, gt:gt + NT], in_=ob[:, :NT]), reads=[Bob])

                    def v_chunk(j):
                        pp, Bpp = proj(1024 + j * 128, 128)
                        ob, Bob = ob_r.next()
                        S.op("act", lambda e: e.copy(out=ob[:, :NT], in_=pp[:, 1:1 + NT]),
                             reads=[Bpp], writes=[Bob])
                        yield
                        to_tok(ob, Bob, tk, Btk, j * 128)

                    def dn_chunk(jb):
                        pp, Bpp = proj(1536 + jb * 128, 128)
                        cbf, Bcb = cb_r.next()
                        sl, Bsl = sl_r.next()
                        S.op("act", lambda e: e.activation(
                            out=cbf[:, :NT], in_=pp[:, 1:1 + NT], func=AF.Copy, scale=dcw[:, jb, 1:2]),
                            reads=[Bpp, BcA], writes=[Bcb])
                        yield
                        S.op("dve", lambda e: e.scalar_tensor_tensor(
                            out=cbf[:, :NT], in0=pp[:, 0:NT], scalar=dcw[:, jb, 0:1], in1=cbf[:, :NT],
                            op0=ALU.mult, op1=ALU.add), reads=[Bpp, BcA, Bcb], writes=[Bcb])
                        S.op("dve", lambda e: e.scalar_tensor_tensor(
                            out=cbf[:, :NT], in0=pp[:, 2:2 + NT], scalar=dcw[:, jb, 2:3], in1=cbf[:, :NT],
                            op0=ALU.mult, op1=ALU.add), reads=[Bpp, BcA, Bcb], writes=[Bcb])
                        yield
                        yield from silu(sl, Bsl, cbf[:, :NT], Bcb)
                        ob, Bob = ob_r.next()
                        hh = jb % 4
                        if jb < 8:
                            rs2, Br2 = yield from inv_norm(sl[:, :NT], Bsl, ones_b, 1.0)
                            S.op("dve", lambda e: e.scalar_tensor_tensor(
                                out=ob[:, :NT], in0=sl[:, :NT], scalar=(HD_D ** -0.5 if jb < 4 else 1.0), in1=rs2[:, :NT],
                                op0=ALU.mult, op1=ALU.mult), reads=[Bsl, Br2], writes=[Bob])
                            dst = DQT if jb < 4 else DKT
                            S.dma("sp", lambda e: e.dma_start(
                                out=dst[hh * 128:(hh + 1) * 128, gt:gt + NT], in_=ob[:, :NT]), reads=[Bob])
                            if jb >= 4:
                                yield
                                to_tok(ob, Bob, tkk, Btkk, hh * 128)
                        else:
                            yield
                            S.op("pool", lambda e: e.tensor_copy(out=ob[:, :NT], in_=sl[:, :NT]),
                                 reads=[Bsl], writes=[Bob])
                            yield
                            to_tok(ob, Bob, tkv, Btkv, hh * 128)

                    def z_chunk(j):
                        pp, Bpp = proj(3072 + j * 128, 128)
                        ob, Bob = ob_r.next()
                        yield
                        yield from silu(ob, Bob, pp[:, 1:1 + NT], Bpp)
                        S.dma("sp", lambda e: e.dma_start(
                            out=ZT[j * 128:(j + 1) * 128, gt:gt + NT], in_=ob[:, :NT]), reads=[Bob])

                    gens = [qk_chunk(cb) for cb in range(8)] + [v_chunk(j) for j in range(4)] + \
                           [dn_chunk(jb) for jb in range(12)] + [z_chunk(j) for j in range(4)]
                    interleave(gens, 3)
                    store_tok(VX, tk, Btk)
                    store_tok(DKK, tkk, Btkk)
                    store_tok(DVV, tkv, Btkv)

                    pp, Bpp = proj(3584, 16)
                    gs, Bgs = gs_r.next()
                    S.op("act", lambda e: e.copy(out=gs[:, :NT], in_=pp[:16, 1:1 + NT]),
                         reads=[Bpp], writes=[Bgs])
                    S.dma("sp", lambda e: e.dma_start(out=GR[:, gt:gt + NT], in_=gs[:, :NT]), reads=[Bgs])

                work = [(b, t0, NT) for b in range(NSEQ) for (t0, NT) in tiles_of(seqs[b])]
                nxt = load_x(*work[0])
                cur_b = -1
                bcs = None
                for wi, (b, t0, NT) in enumerate(work):
                    xt, Bxt = nxt
                    if wi + 1 < len(work):
                        nxt = load_x(*work[wi + 1])
                    if b != cur_b:
                        cur_b = b
                        bcs = bcr.next()
                        load_bcast(bcs[0], bcs[1], b, (1, 0))
                    do_tile(b, t0, NT, xt, Bxt, bcs[0], bcs[1])
            S.barrier()

        if any(c in stages for c in "BCD"):
            with ExitStack() as esM:
                rbt, BcM = None, None
                wo = sb("wo", [128, 8, D], BF16, esM)
                don = sb("don", [128, 1], F32, esM)
                Bwo, BcD = Buf("wo"), Buf("constD")
                if "D" in stages:
                    zrow = sb("zrow", [1, D], F32, esM)
                    Bz = Buf("zrow")
                    S.op("pool", lambda e: e.memset(zrow[:], 0.0), writes=[Bz])
                    S.dma("pool", lambda e: e.dma_start(out=XN[0:1, :], in_=zrow[:]), reads=[Bz])
                    S.dma("pool", lambda e: e.dma_start(out=XN[NTOK + 1:NTOK + 2, :], in_=zrow[:]), reads=[Bz])
                    for kc in range(8):
                        S.dma("pool", lambda e, kc=kc: e.dma_start(out=wo[:, kc, :], in_=wo_d[kc * 128:(kc + 1) * 128, :]), writes=[Bwo])
                    S.dma("sp", lambda e: e.dma_start(out=don[:], in_=don_d), writes=[BcD])
                for b in range(NSEQ):
                    if "B" in stages:
                        stage_B(b, rbt, BcM)
                        S.barrier()
                    if "C" in stages:
                        stage_C(b)
                        S.barrier()
                    if "D" in stages:
                        stage_D(b, wo, Bwo, don, BcD)
                        S.barrier()

        if "E" in stages:
            stage_E()

        S.emit(nc)
    return nc


def _consts(nseq):
    ident = np.eye(128, dtype=np.float32)
    blk2 = np.zeros((128, 128), np.float32)
    blk2[:64, :64] = 1.0
    blk2[64:, 64:] = 1.0
    i = np.arange(128)[:, None]
    j = np.arange(128)[None, :]
    same = (i // 64) == (j // 64)
    mk = [same & (j < i), same & (j > i), same & (j >= i), same & (j <= i)]
    masks = np.stack([np.where(m, np.float32(0.0), np.float32(NEG)) for m in mk], axis=1).astype(np.float32)
    m01 = np.ones((4, 512), np.float32)
    m01[:, ::64] = 0.0
    mk7 = [(i // 8) == (j // 8)]
    for m_ in (8, 16, 32):
        mk7.append(((i // (2 * m_)) == (j // (2 * m_))) & ((i // m_) % 2 == 1) & ((j // m_) % 2 == 0))
    for m_ in (8, 16, 32):
        mk7.append(((i // (2 * m_)) == (j // (2 * m_))) & ((i // m_) % 2 == 0) & ((j // m_) % 2 == 1))
    mk7 = np.ascontiguousarray(np.stack(mk7, axis=1).astype(np.float32))
    return {"ident": ident, "blk2": blk2, "masks": np.ascontiguousarray(masks), "m01": m01, "mk7": mk7}


def _rb_table(att_rpb):
    rpb = np.asarray(att_rpb, np.float32)
    krl = np.arange(2)[:, None, None, None]
    kc = np.arange(64)[None, :, None, None]
    e = np.arange(16)[None, None, :, None]
    qc = np.arange(64)[None, None, None, :]
    dr = krl + 7 - e
    w = np.clip(qc - 8, 0, 48)
    ok = (dr >= -7) & (dr <= 7) & (kc >= w) & (kc < w + 16)
    ri = np.clip(dr + 7, 0, 14)
    ci = np.clip(kc - qc + 15, 0, 30)
    ri, ci, ok = np.broadcast_arrays(ri, ci, ok)
    tab = rpb[:, ri, ci]
    tab = np.where(ok[None], tab, np.float32(NEG)).astype(np.float32)
    return np.ascontiguousarray(tab.transpose(1, 2, 0, 3, 4).reshape(128, 8, 16, 64))


def shared_inputs(nseq, ada_w, ada_b, norm1_g, norm2_g, w_in, att_q_norm, att_k_norm, att_rpb, dn_conv_w,
                  dn_a_log, dn_dt_bias, dn_out_norm, w_o, ffn_w_up, ffn_conv_w, ffn_conv_b, ffn_w_down):
    f = lambda a: np.ascontiguousarray(np.asarray(a, np.float32))
    m = dict(_consts(nseq))
    m["ada_w"] = f(ada_w[0])
    m["ada_b"] = f(np.broadcast_to(np.asarray(ada_b[0])[None, :], (nseq, 6 * D)))
    m["n12"] = f(np.broadcast_to(np.stack([np.asarray(norm1_g[0]), np.asarray(norm2_g[0])])[None], (nseq, 2, D)))
    m["w_in"] = f(w_in[0])
    m["qkn"] = f(np.stack([np.tile(np.asarray(att_q_norm[0]), 2), np.tile(np.asarray(att_k_norm[0]), 2)], axis=1))
    m["rb"] = _rb_table(att_rpb[0])
    m["dcw"] = f(np.asarray(dn_conv_w[0]).reshape(3, 12, 128).transpose(2, 1, 0))
    m["dgate"] = f(np.stack([np.asarray(dn_a_log[0]).reshape(8), np.asarray(dn_dt_bias[0]).reshape(8)], axis=1))
    m["don"] = f(np.asarray(dn_out_norm[0]).reshape(128, 1))
    m["w_o"] = f(w_o[0])
    m["w_up"] = f(ffn_w_up[0])
    m["fcw"] = f(np.asarray(ffn_conv_w[0]).reshape(3, 44, 128).transpose(2, 1, 0))
    m["fcb"] = f(np.asarray(ffn_conv_b[0]).reshape(44, 128).T)
    m["w_dn"] = f(ffn_w_down[0])
    return m


def core_inputs(shared, xs, cs):
    m = dict(shared)
    x = np.concatenate([np.zeros((1, D), np.float32)] + [np.asarray(a, np.float32) for a in xs]
                       + [np.zeros((1, D), np.float32)], axis=0)
    m["x"] = np.ascontiguousarray(x)
    c = np.stack([np.asarray(a, np.float32) for a in cs])
    m["cT"] = np.ascontiguousarray(c.T.reshape(8, 128, len(cs)).transpose(1, 0, 2))
    return m


N_CORES = 8


def kernel(x_prompt, x_sample, c_prompt, c_sample, ada_w, ada_b, norm1_g, norm2_g, w_in,
           att_q_norm, att_k_norm, att_rpb, dn_conv_w, dn_a_log, dn_dt_bias, dn_out_norm, w_o,
           ffn_w_up, ffn_conv_w, ffn_conv_b, ffn_w_down):
    x_prompt = np.asarray(x_prompt, np.float32)
    x_sample = np.asarray(x_sample, np.float32)
    c_prompt = np.asarray(c_prompt, np.float32)
    c_sample = np.asarray(c_sample, np.float32)
    BP, TP, _ = x_prompt.shape
    BS, TS, _ = x_sample.shape
    ppc = BP // N_CORES
    seqs = [TP] * ppc + [TS]
    shared = shared_inputs(len(seqs), ada_w, ada_b, norm1_g, norm2_g, w_in, att_q_norm, att_k_norm, att_rpb,
                           dn_conv_w, dn_a_log, dn_dt_bias, dn_out_norm, w_o, ffn_w_up, ffn_conv_w, ffn_conv_b,
                           ffn_w_down)
    in_maps = []
    for c in range(N_CORES):
        xs = [x_prompt[c * ppc + i] for i in range(ppc)] + [x_sample[c * BS // N_CORES]]
        cs = [c_prompt[c * ppc + i] for i in range(ppc)] + [c_sample[c * BS // N_CORES]]
        in_maps.append(core_inputs(shared, xs, cs))
    nc = build(seqs)
    res = run_bass_kernel_spmd(nc, in_maps, core_ids=list(range(N_CORES)))
    y_prompt = np.empty_like(x_prompt)
    y_sample = np.empty_like(x_sample)
    half = TS // 2
    for c in range(N_CORES):
        y = np.asarray(res.results[c]["y"], np.float32)
        y_prompt[c * ppc:(c + 1) * ppc] = y[:ppc * TP].reshape(ppc, TP, D)
        sidx = c * BS // N_CORES
        ys = y[ppc * TP:].reshape(TS, D)
        if c % 2 == 0:
            y_sample[sidx, :half] = ys[:half]
        else:
            y_sample[sidx, half:] = ys[half:]
    return (y_prompt, y_sample)
```

```python
import math
from contextlib import ExitStack

import numpy as np
import concourse.bass as bass
import concourse.mybir as mybir
from concourse.bass_utils import run_bass_kernel_spmd

F32 = mybir.dt.float32
BF16 = mybir.dt.bfloat16
AF = mybir.ActivationFunctionType
ALU = mybir.AluOpType

D = 1024
GW = 64
NH_A, HD_A = 8, 64
NH_D, HD_D = 4, 128
DFF = 2816
INC = 3600
EPS = 1e-6
NEG = -30000.0

ENGS = ("pe", "act", "dve", "pool", "sp")
NDMA = 8


class Buf:
    __slots__ = ("name", "lw", "rd")

    def __init__(self, name):
        self.name = name
        self.lw = None
        self.rd = []


class Sched:
    def __init__(self):
        self.q = {e: [] for e in ENGS}
        self.cnt = {e: 0 for e in ENGS}
        self.known = {e: {} for e in ENGS}
        self.dma_n = {e: 0 for e in ENGS}
        self.dma_slot_val = {}
        self.pending = {e: {} for e in ENGS}
        self.nops = 0

    def _deps(self, eng, reads, writes):
        deps = dict(self.pending[eng])
        self.pending[eng] = {}

        def add(ev):
            if ev is None:
                return
            k, v = ev
            if deps.get(k, 0) < v:
                deps[k] = v
        for b in reads:
            add(b.lw)
        for b in writes:
            add(b.lw)
            for r in b.rd:
                add(r)
        return deps

    def _finish(self, eng, fn, deps, ev, reads, writes, inc):
        kn = self.known[eng]
        waits = []
        for k, v in deps.items():
            if kn.get(k, 0) >= v:
                continue
            if eng == "pe" and k == "pe":
                continue
            kn[k] = v
            waits.append((k, v))
        self.q[eng].append((fn, waits, inc))
        self.nops += 1
        for b in reads:
            b.rd.append(ev)
        for b in writes:
            b.lw = ev
            b.rd = []

    def op(self, eng, fn, reads=(), writes=()):
        deps = self._deps(eng, reads, writes)
        self.cnt[eng] += 1
        ev = (eng, self.cnt[eng])
        self._finish(eng, fn, deps, ev, reads, writes, (eng, 1))
        return ev

    def dma(self, queue, fn, reads=(), writes=()):
        deps = self._deps(queue, reads, writes)
        n = self.dma_n[queue]
        self.dma_n[queue] = n + 1
        key = ("dma", queue, n % NDMA)
        prev = self.dma_slot_val.get(key, 0)
        if prev and deps.get(key, 0) < prev:
            deps[key] = prev
        val = prev + 16
        self.dma_slot_val[key] = val
        ev = (key, val)
        self._finish(queue, fn, deps, ev, reads, writes, (key, 16))
        return ev

    def all_events(self):
        evs = {}
        for e in ENGS:
            if self.cnt[e]:
                evs[e] = self.cnt[e]
        for k, v in self.dma_slot_val.items():
            evs[k] = v
        return evs

    def barrier(self):
        evs = self.all_events()
        for e in ENGS:
            p = self.pending[e]
            for k, v in evs.items():
                if p.get(k, 0) < v:
                    p[k] = v

    def emit(self, nc):
        final = self.all_events()
        with ExitStack() as es:
            sems = {}
            for e in ENGS:
                sems[e] = es.enter_context(nc.semaphore("s_" + e))
            for q in ("sp", "pool"):
                for s in range(NDMA):
                    sems[("dma", q, s)] = es.enter_context(nc.semaphore("d_%s%d" % (q, s)))
            block = es.enter_context(nc.Block())

            def run(name):
                def body(eng):
                    for fn, waits, inc in self.q[name]:
                        for k, v in waits:
                            eng.wait_ge(sems[k], v)
                        fn(eng).then_inc(sems[inc[0]], inc[1])
                    if name == "sp":
                        for k, v in final.items():
                            eng.wait_ge(sems[k], v)
                return body

            block.tensor(run("pe"))
            block.scalar(run("act"))
            block.vector(run("dve"))
            block.gpsimd(run("pool"))
            block.sync(run("sp"))


_UID = [0]


class Rot:
    def __init__(self, es, nc, name, n, shape, dt, psum=False):
        self.t = []
        for i in range(n):
            _UID[0] += 1
            nm = "r_%s%d_%d" % (name, i, _UID[0])
            if psum:
                t = es.enter_context(nc.psum_tensor(nm, shape, dt))
            else:
                t = es.enter_context(nc.sbuf_tensor(nm, shape, dt))
            self.t.append((t, Buf(nm)))
        self.i = 0

    def next(self):
        r = self.t[self.i % len(self.t)]
        self.i += 1
        return r


def interleave(gens, width):
    it = iter(gens)
    active = []
    while True:
        while len(active) < width:
            g = next(it, None)
            if g is None:
                break
            active.append(g)
        if not active:
            break
        for g in list(active):
            try:
                next(g)
            except StopIteration:
                active.remove(g)


def tiles_of(T, mx=510):
    n = -(-T // mx)
    base, rem = divmod(T, n)
    out, t0 = [], 0
    for i in range(n):
        s = base + (1 if i < rem else 0)
        out.append((t0, s))
        t0 += s
    return out


def build(seqs, stages="0ABCDE", debug=False):
    NSEQ = len(seqs)
    NTOK = sum(seqs)
    offs = [sum(seqs[:i]) for i in range(NSEQ)]
    nc = bass.Bass("TRN2", target_bir_lowering=False)
    S = Sched()
    kind_dbg = "ExternalOutput" if debug else "Internal"

    def din(name, shape, dt=F32):
        return nc.dram_tensor(name, list(shape), dt, kind="ExternalInput").ap()

    def dscr(name, shape, dt):
        return nc.dram_tensor(name, list(shape), dt, kind=kind_dbg).ap()

    dumped = set()

    def dump(name, ap, B, dt=F32):
        if not debug or name in dumped:
            return
        dumped.add(name)
        t = nc.dram_tensor("dbg_" + name, list(ap.shape), dt, kind="ExternalOutput").ap()
        S.dma("pool", lambda e: e.dma_start(out=t, in_=ap), reads=[B])

    x_d = din("x", [NTOK + 2, D])
    cT_d = din("cT", [128, 8, NSEQ])
    adaw_d = din("ada_w", [D, 6 * D])
    adab_d = din("ada_b", [NSEQ, 6 * D])
    n12_d = din("n12", [NSEQ, 2, D])
    win_d = din("w_in", [D, INC])
    qkn_d = din("qkn", [128, 2])
    rb_d = din("rb", [128, NH_A, 16, GW])
    dcw_d = din("dcw", [128, 12, 3])
    dgate_d = din("dgate", [8, 2])
    don_d = din("don", [128, 1])
    wo_d = din("w_o", [D, D])
    wup_d = din("w_up", [D, 2 * DFF])
    fcw_d = din("fcw", [128, 44, 3])
    fcb_d = din("fcb", [128, 44])
    wdn_d = din("w_dn", [DFF, D])
    ident_d = din("ident", [128, 128])
    blk2_d = din("blk2", [128, 128])
    masks_d = din("masks", [128, 4, 128])
    m01_d = din("m01", [4, 512])
    mk7_d = din("mk7", [128, 7, 128])
    y_d = nc.dram_tensor("y", [NTOK, D], F32, kind="ExternalOutput").ap()

    QT = dscr("QT", [512, NTOK], BF16)
    KT = dscr("KT", [512, NTOK], BF16)
    VX = dscr("VX", [NTOK, 512], BF16)
    DQT = dscr("DQT", [512, NTOK], BF16)
    DKT = dscr("DKT", [512, NTOK], BF16)
    DKK = dscr("DKK", [NTOK, 512], BF16)
    DVV = dscr("DVV", [NTOK, 512], BF16)
    ZT = dscr("ZT", [512, NTOK], BF16)
    GR = dscr("GR", [16, NTOK], F32)
    MOD = dscr("MOD", [NSEQ, 6 * D], F32)
    ATT = dscr("ATT", [NTOK, 512], BF16)
    OSC = dscr("OSC", [NTOK, 512], F32)
    BCS = dscr("BCS", [24, 512], F32)
    OSB = dscr("OSB", [NTOK, 512], F32)
    XN = dscr("XN", [NTOK + 2, D], F32)
    WUB = dscr("WUB", [22, 128, 2, 8, 128], BF16)

    with ExitStack() as es0:
        def sb(name, shape, dt, es=es0):
            _UID[0] += 1
            return es.enter_context(nc.sbuf_tensor("s_%s_%d" % (name, _UID[0]), list(shape), dt))

        idf = sb("idf", [128, 128], F32)
        idb = sb("idb", [128, 128], BF16)
        blk2 = sb("blk2", [128, 128], BF16)
        ones_b = sb("ones_b", [128, 128], BF16)
        Bconst = Buf("const")
        S.dma("sp", lambda e: e.dma_start(out=idf[:], in_=ident_d), writes=[Bconst])
        S.dma("pool", lambda e: e.dma_start(out=blk2[:], in_=blk2_d), writes=[Bconst])
        S.op("dve", lambda e: e.tensor_copy(out=idb[:], in_=idf[:]), reads=[Bconst], writes=[Bconst])
        S.op("pool", lambda e: e.memset(ones_b[:], 1.0), writes=[Bconst])

        if "0" in stages:
            with ExitStack() as es:
                cT = sb("cT", [128, 8, NSEQ], F32, es)
                BcT = Buf("cT")
                modr = sb("modr", [NSEQ, 6 * D], F32, es)
                Bmod = Buf("modr")
                adab = sb("adab", [NSEQ, 6 * D], F32, es)
                Badab = Buf("adab")
                n12 = sb("n12", [NSEQ, 2, D], F32, es)
                Bn12 = Buf("n12")
                awr = Rot(es, nc, "aw", 2, [128, 8, 512], F32)
                pmod = Rot(es, nc, "pmod", 2, [NSEQ, 512], F32, psum=True)
                S.dma("sp", lambda e: e.dma_start(out=cT[:], in_=cT_d), writes=[BcT])
                S.dma("sp", lambda e: e.dma_start(out=adab[:], in_=adab_d), writes=[Badab])
                S.dma("sp", lambda e: e.dma_start(out=n12[:], in_=n12_d), writes=[Bn12])
                S.op("act", lambda e: e.activation(out=cT[:], in_=cT[:], func=AF.Silu), reads=[BcT], writes=[BcT])
                for nb in range(12):
                    aw, Baw = awr.next()
                    S.dma("sp", lambda e, aw=aw, nb=nb: e.dma_start(
                        out=aw[:], in_=adaw_d[:, nb * 512:(nb + 1) * 512].rearrange("(kc p) n -> p kc n", p=128)),
                        writes=[Baw])
                    pm, Bpm = pmod.next()
                    for kc in range(8):
                        S.op("pe", lambda e, pm=pm, aw=aw, kc=kc: e.matmul(
                            pm[:], lhsT=cT[:, kc, :], rhs=aw[:, kc, :], start=(kc == 0), stop=(kc == 7)),
                            reads=[BcT, Baw], writes=[Bpm])
                    S.op("dve", lambda e, pm=pm, nb=nb: e.tensor_tensor(
                        out=modr[:, nb * 512:(nb + 1) * 512], in0=pm[:], in1=adab[:, nb * 512:(nb + 1) * 512], op=ALU.add),
                        reads=[Bpm, Badab], writes=[Bmod])
                for j, c0 in ((0, D), (1, 4 * D)):
                    S.op("dve", lambda e, j=j, c0=c0: e.scalar_tensor_tensor(
                        out=modr[:, c0:c0 + D], in0=modr[:, c0:c0 + D], scalar=1.0, in1=n12[:, j, :],
                        op0=ALU.add, op1=ALU.mult), reads=[Bmod, Bn12], writes=[Bmod])
                S.dma("sp", lambda e: e.dma_start(out=MOD, in_=modr[:]), reads=[Bmod])
            S.barrier()

        def load_bcast(bc, Bbc, b, slots):
            for i, sl in enumerate(slots):
                S.dma("sp", lambda e, i=i, sl=sl: e.dma_start(
                    out=bc[:, i, :], in_=MOD[b:b + 1, sl * D:(sl + 1) * D].partition_broadcast(128)),
                    writes=[Bbc])


        def stage_B(b, rbt, BcM):
            T = seqs[b]
            Rr = T // GW
            NCH = Rr // 2
            o0 = offs[b]

            def r0_of(r):
                return min(max(r - 4, 0), Rr - 8)

            with ExitStack() as es:
                rbt = sb("rbt", [128, NH_A, 16, GW], F32, es)
                BcM = Buf("constM")
                S.dma("sp", lambda e: e.dma_start(out=rbt[:], in_=rb_d), writes=[BcM])
                qT = sb("qT", [128, 4, T], BF16, es)
                kT = sb("kT", [128, 4, T], BF16, es)
                vx = sb("vx", [128, NCH, NH_A, HD_A + 1], BF16, es)
                Bq, Bk, Bv = Buf("qT"), Buf("kT"), Buf("vx")
                for pr in range(4):
                    S.dma("sp", lambda e, pr=pr: e.dma_start(out=qT[:, pr, :], in_=QT[pr * 128:(pr + 1) * 128, o0:o0 + T]),
                          writes=[Bq])
                    S.dma("sp", lambda e, pr=pr: e.dma_start(out=kT[:, pr, :], in_=KT[pr * 128:(pr + 1) * 128, o0:o0 + T]),
                          writes=[Bk])
                S.op("pool", lambda e: e.memset(vx[:, :, :, HD_A:HD_A + 1], 1.0), writes=[Bv])
                for j in range(NCH):
                    S.dma("sp", lambda e, j=j: e.dma_start(
                        out=vx[:, j, :, 0:HD_A],
                        in_=VX[o0 + j * 128:o0 + (j + 1) * 128, :].rearrange("p (h d) -> p h d", d=HD_A)), writes=[Bv])
                ps_r = Rot(es, nc, "psB", 2, [128, 1024], F32, psum=True)
                po_r = Rot(es, nc, "poB", 2, [128, 4, HD_A + 1], F32, psum=True)
                ss_r = Rot(es, nc, "ssB", 2, [128, 768], F32)
                pT_r = Rot(es, nc, "pTB", 24, [128, 768], BF16)
                rd_r = Rot(es, nc, "rdB", 2, [128, 4, 1], F32)
                ab_r = Rot(es, nc, "abB", 3, [128, 4, HD_A], BF16)

                def do_hg(hg):
                    pts = {}

                    def do_chunk(j, hl):
                        h = hg * 4 + hl
                        pr, hb = h // 2, (h % 2) * 64
                        rows = [r for r in range(Rr) if r0_of(r) <= 2 * j + 1 and r0_of(r) + 7 >= 2 * j]
                        lo, nq = rows[0], len(rows)
                        e_lo = lo - 2 * j + 7
                        ps, Bps = ps_r.next()
                        n1 = min(512, nq * 64)
                        S.op("pe", lambda e: e.matmul(
                            ps[:, 0:n1], lhsT=kT[hb:hb + 64, pr, j * 128:(j + 1) * 128],
                            rhs=qT[hb:hb + 64, pr, lo * 64:lo * 64 + n1], start=True, stop=True),
                            reads=[Bk, Bq], writes=[Bps])
                        if nq * 64 > 512:
                            S.op("pe", lambda e: e.matmul(
                                ps[:, 512:nq * 64], lhsT=kT[hb:hb + 64, pr, j * 128:(j + 1) * 128],
                                rhs=qT[hb:hb + 64, pr, lo * 64 + 512:(lo + nq) * 64], start=True, stop=True),
                                reads=[Bk, Bq], writes=[Bps])
                        ss, Bss = ss_r.next()
                        pt, Bpt = pT_r.next()
                        for (a0, a1) in ((0, min(nq, 8)), (8, nq)):
                            if a1 <= a0:
                                continue
                            S.op("dve", lambda e, a0=a0, a1=a1: e.tensor_tensor(
                                out=ss[:, a0 * 64:a1 * 64].rearrange("p (a c) -> p a c", c=GW),
                                in0=ps[:, a0 * 64:a1 * 64].rearrange("p (a c) -> p a c", c=GW),
                                in1=rbt[:, h, e_lo + a0:e_lo + a1, :], op=ALU.add), reads=[Bps, BcM], writes=[Bss])
                        S.op("act", lambda e: e.activation(out=pt[:, 0:nq * 64], in_=ss[:, 0:nq * 64], func=AF.Exp),
                             reads=[Bss], writes=[Bpt])
                        for r in rows:
                            r0 = r0_of(r)
                            v0 = r0 <= 2 * j <= r0 + 7
                            v1 = r0 <= 2 * j + 1 <= r0 + 7
                            cc = (r - lo) * 64
                            if not v0:
                                S.op("pool", lambda e, cc=cc: e.memset(pt[0:64, cc:cc + 64], 0.0), reads=[], writes=[Bpt])
                            if not v1:
                                S.op("pool", lambda e, cc=cc: e.memset(pt[64:128, cc:cc + 64], 0.0), reads=[], writes=[Bpt])
                        pts[(j, hl)] = (pt, Bpt, lo)

                    def do_rowpair(rp):
                        po, Bpo = po_r.next()

                        def pv(r, hl):
                            h = hg * 4 + hl
                            r0 = r0_of(r)
                            chunks = list(range(r0 // 2, (r0 + 7) // 2 + 1))
                            ph = (r % 2) * 64
                            for ci, j in enumerate(chunks):
                                pt, Bpt, lo = pts[(j, hl)]
                                k0, k1 = 0, 128
                                c0 = (r - lo) * 64
                                S.op("pe", lambda e, pt=pt, j=j, k0=k0, k1=k1, c0=c0, ci=ci: e.matmul(
                                    po[ph:ph + 64, hl, :], lhsT=pt[k0:k1, c0:c0 + 64], rhs=vx[k0:k1, j, h, :],
                                    start=(ci == 0), stop=(ci == len(chunks) - 1)),
                                    reads=[Bpt, Bv], writes=[Bpo])
                        for r in (2 * rp, 2 * rp + 1):
                            for hl in range(4):
                                pv(r, hl)
                        rd, Brd = rd_r.next()
                        ab, Bab = ab_r.next()
                        S.op("dve", lambda e: e.reciprocal(out=rd[:], in_=po[:, :, HD_A:HD_A + 1]), reads=[Bpo], writes=[Brd])
                        S.op("dve", lambda e: e.tensor_tensor(
                            out=ab[:], in0=po[:, :, 0:HD_A], in1=rd[:].to_broadcast([128, 4, HD_A]),
                            op=ALU.mult), reads=[Bpo, Brd], writes=[Bab])
                        S.dma("pool", lambda e: e.dma_start(
                            out=ATT[o0 + rp * 128:o0 + (rp + 1) * 128, hg * 256:(hg + 1) * 256].rearrange(
                                "p (h d) -> p h d", d=HD_A), in_=ab[:]), reads=[Bab])

                    done_rp = 0
                    for j in range(NCH):
                        for hl in range(4):
                            do_chunk(j, hl)
                        while done_rp < Rr // 2:
                            need = max((r0_of(r) + 7) // 2 for r in (2 * done_rp, 2 * done_rp + 1))
                            if need > j:
                                break
                            do_rowpair(done_rp)
                            done_rp += 1

                for hg_ in range(2):
                    do_hg(hg_)


        def stage_C(b):
            T = seqs[b]
            o0 = offs[b]
            SEG = 512
            NSEG = T // SEG
            NCK = SEG // 64
            NB = T // 128
            BETA, EGC, EDK, NBEG, GB, NGC = range(6)
            with ExitStack() as es:
                masks = sb("masksC", [128, 4, 128], F32, es)
                id4 = sb("id4C", [128, 4, 128], BF16, es)
                m01 = sb("m01C", [4, SEG], F32, es)
                mk7f = sb("mk7fC", [128, 7, 128], F32, es)
                mk7 = sb("mk7C", [128, 7, 4, 128], BF16, es)
                BcC = Buf("constC")
                S.dma("sp", lambda e: e.dma_start(out=mk7f[:], in_=mk7_d), writes=[BcC])
                for h in range(4):
                    S.op("pool", lambda e, h=h: e.tensor_copy(out=mk7[:, :, h, :], in_=mk7f[:]), reads=[BcC], writes=[BcC])
                S.dma("sp", lambda e: e.dma_start(out=masks[:], in_=masks_d), writes=[BcC])
                S.dma("sp", lambda e: e.dma_start(out=m01[:], in_=m01_d), writes=[BcC])
                for h in range(4):
                    S.op("pool", lambda e, h=h: e.tensor_copy(out=id4[:, h, :], in_=idb[:]), reads=[Bconst], writes=[BcC])
                Sf = [sb("SfC%d" % d, [128, 4, 128], F32, es) for d in range(2)]
                Sb_ = [sb("SbC%d" % d, [128, 4, 128], BF16, es) for d in range(2)]
                BSf = [Buf("Sf0"), Buf("Sf1")]
                BSb = [Buf("Sb0"), Buf("Sb1")]
                dg = [sb("dgC%d" % d, [4, 4], F32, es) for d in range(2)]
                Bdg = [Buf("dg0"), Buf("dg1")]
                Bbcs = [Buf("BCS0"), Buf("BCS1")]
                Bosc = Buf("OSC")
                kT_r = [Rot(es, nc, "kTC%d" % d, 3, [128, 4, 128], BF16) for d in range(2)]
                qT_r = [Rot(es, nc, "qTC%d" % d, 3, [128, 4, 128], BF16) for d in range(2)]
                kk_r = [Rot(es, nc, "kkC%d" % d, 3, [128, 512], BF16) for d in range(2)]
                vv_r = Rot(es, nc, "vvC", 3, [128, 512], BF16)
                row_r = Rot(es, nc, "rowC", 16, [4, SEG], F32)
                bc_r = [Rot(es, nc, "bcC%d" % d, 1, [128, 8, 128], F32) for d in range(2)]
                egl_r = [Rot(es, nc, "eglC%d" % d, 2, [128, 4, NCK], F32) for d in range(2)]
                scs_r = [Rot(es, nc, "scsC%d" % d, 2, [128, SEG // 128, 24], F32) for d in range(2)]
                pp_r = Rot(es, nc, "ppC", 3, [128, 4, 128], F32, psum=True)
                pkq_r = Rot(es, nc, "pkqC", 2, [128, 4, 128], F32, psum=True)
                pvo_r = Rot(es, nc, "pvoC", 2, [128, 4, 128], F32, psum=True)
                pS_r = Rot(es, nc, "pSC", 1, [128, 4, 128], F32, psum=True)
                dm_r = Rot(es, nc, "dmC", 4, [128, 128], F32)
                E_r = [Rot(es, nc, "EC%d" % d, 4, [128, 4, 128], BF16) for d in range(2)]
                mb_r = [Rot(es, nc, "mbC%d" % d, 10, [128, 4, 128], BF16) for d in range(2)]
                m0_r = [Rot(es, nc, "m0C%d" % d, 2, [128, 4, 128], BF16) for d in range(2)]
                fin_r = [Rot(es, nc, "finC%d" % d, 6, [128, 4, 128], BF16) for d in range(2)]
                it_r = [Rot(es, nc, "itC%d" % d, 3, [128, 4, 128], BF16) for d in range(2)]
                bv_r = [Rot(es, nc, "bvC%d" % d, 3, [128, 4, 128], F32) for d in range(2)]
                r_r = [Rot(es, nc, "rC%d" % d, 2, [128, 4, 128], BF16) for d in range(2)]
                qs_r = [Rot(es, nc, "qsC%d" % d, 2, [128, 4, 128], F32) for d in range(2)]
                vn_r = [Rot(es, nc, "vnC%d" % d, 2, [128, 4, 128], BF16) for d in range(2)]
                vd_r = [Rot(es, nc, "vdC%d" % d, 2, [128, 4, 128], BF16) for d in range(2)]
                o_r = [Rot(es, nc, "oC%d" % d, 2, [128, 4, 128], F32) for d in range(2)]

                def seg_prep(d, sg):
                    t0 = o0 + sg * SEG
                    def row():
                        return row_r.next()
                    bl, Bbl = row(); al, Bal = row()
                    S.dma("sp", lambda e: e.dma_start(out=bl[:], in_=GR[d * 4:d * 4 + 4, t0:t0 + SEG]), writes=[Bbl])
                    S.dma("sp", lambda e: e.dma_start(out=al[:], in_=GR[8 + d * 4:12 + d * 4, t0:t0 + SEG]), writes=[Bal])
                    l1, Bl1 = row(); sp_, Bsp = row(); g, Bg = row(); pre, Bpre = row()
                    S.op("act", lambda e: e.activation(out=l1[:], in_=bl[:], func=AF.Exp, scale=-1.0), reads=[Bbl], writes=[Bl1])
                    S.op("act", lambda e: e.activation(out=l1[:], in_=l1[:], func=AF.Ln, bias=1.0), reads=[Bl1], writes=[Bl1])
                    S.op("act", lambda e: e.activation(out=sp_[:], in_=al[:], func=AF.Exp, bias=dg[d][:, 1:2]), reads=[Bal, Bdg[d]], writes=[Bsp])
                    S.op("act", lambda e: e.activation(out=sp_[:], in_=sp_[:], func=AF.Ln, bias=1.0), reads=[Bsp], writes=[Bsp])
                    S.op("dve", lambda e: e.tensor_scalar(out=g[:], in0=sp_[:], scalar1=dg[d][:, 2:3], scalar2=None, op0=ALU.mult),
                         reads=[Bsp, Bdg[d]], writes=[Bg])
                    S.op("dve", lambda e: e.tensor_tensor_scan(out=pre[:], data0=m01[:], data1=g[:], initial=0.0,
                                                               op0=ALU.mult, op1=ALU.add), reads=[Bg, BcC], writes=[Bpre])
                    tot = pre[:].rearrange("p (n c) -> p n c", c=64)[:, :, 63:64]
                    def v3(t):
                        return t[:].rearrange("p (n c) -> p n c", c=64)
                    if d == 0:
                        gc, Bgc = pre, Bpre
                    else:
                        gc, Bgc = row()
                        S.op("dve", lambda e: e.tensor_tensor(out=v3(gc), in0=tot.to_broadcast([4, NCK, 64]), in1=v3(pre),
                                                              op=ALU.subtract), reads=[Bpre], writes=[Bgc])
                        S.op("dve", lambda e: e.tensor_tensor(out=gc[:], in0=gc[:], in1=g[:], op=ALU.add),
                             reads=[Bgc, Bg], writes=[Bgc])
                    ngc, Bngc = row(); gb, Bgb = row(); egc, Begc = row(); beta, Bbeta = row()
                    nbeg, Bnbeg = row(); edk, Bedk = row(); egl, Begl = row()
                    S.op("dve", lambda e: e.tensor_scalar(out=ngc[:], in0=gc[:], scalar1=-1.0, scalar2=None, op0=ALU.mult),
                         reads=[Bgc], writes=[Bngc])
                    S.op("dve", lambda e: e.tensor_tensor(out=gb[:], in0=gc[:], in1=l1[:], op=ALU.subtract),
                         reads=[Bgc, Bl1], writes=[Bgb])
                    S.op("act", lambda e: e.activation(out=egc[:], in_=gc[:], func=AF.Exp), reads=[Bgc], writes=[Begc])
                    S.op("act", lambda e: e.activation(out=beta[:], in_=l1[:], func=AF.Exp, scale=-1.0), reads=[Bl1], writes=[Bbeta])
                    S.op("dve", lambda e: e.scalar_tensor_tensor(out=nbeg[:], in0=beta[:], scalar=-1.0, in1=egc[:],
                                                                 op0=ALU.mult, op1=ALU.mult), reads=[Bbeta, Begc], writes=[Bnbeg])
                    S.op("dve", lambda e: e.tensor_tensor(out=v3(edk), in0=tot.to_broadcast([4, NCK, 64]), in1=v3(gc),
                                                          op=ALU.subtract), reads=[Bpre, Bgc], writes=[Bedk])
                    S.op("act", lambda e: e.activation(out=edk[:], in_=edk[:], func=AF.Exp), reads=[Bedk], writes=[Bedk])
                    S.op("act", lambda e: e.activation(out=egl[:, 0:NCK].rearrange("p (n o) -> p n o", o=1), in_=tot,
                                                       func=AF.Exp), reads=[Bpre], writes=[Begl])
                    r0 = d * 12
                    S.dma("pool", lambda e: e.dma_start(out=BCS[r0:r0 + 4, 0:SEG], in_=ngc[:]), reads=[Bngc], writes=[Bbcs[d]])
                    S.dma("pool", lambda e: e.dma_start(out=BCS[r0 + 4:r0 + 8, 0:SEG], in_=gb[:]), reads=[Bgb], writes=[Bbcs[d]])
                    S.dma("pool", lambda e: e.dma_start(out=BCS[r0 + 8:r0 + 12, 0:NCK], in_=egl[:, 0:NCK]), reads=[Begl], writes=[Bbcs[d]])
                    eglb, Beglb = egl_r[d].next()
                    S.dma("sp", lambda e: e.dma_start(out=eglb[:], in_=BCS[r0 + 8:r0 + 12, 0:NCK].partition_broadcast(128)),
                          reads=[Bbcs[d]], writes=[Beglb])
                    scs, Bscs = scs_r[d].next()
                    rows = ((beta, Bbeta), (egc, Begc), (edk, Bedk), (nbeg, Bnbeg), (gb, Bgb), (ngc, Bngc))
                    for jb in range(SEG // 128):
                        psc, Bpsc = pp_r.next()
                        for qi, (rt, Brt) in enumerate(rows):
                            S.op("pe", lambda e, qi=qi, rt=rt, jb=jb, psc=psc: e.transpose(
                                out=psc[:, 0, qi * 4:qi * 4 + 4], in_=rt[:, jb * 128:(jb + 1) * 128], identity=idf[0:4, 0:4]),
                                reads=[Brt, Bconst], writes=[Bpsc])
                        S.op("dve", lambda e, jb=jb, psc=psc: e.tensor_copy(out=scs[:, jb, :], in_=psc[:, 0, 0:24]),
                             reads=[Bpsc], writes=[Bscs])
                    return dict(scs=scs, Bscs=Bscs, eglb=eglb, Beglb=Beglb)

                def mm4(lhs, Blhs, rhs, Brhs):
                    pt, Bpt = pp_r.next()
                    for h in range(4):
                        S.op("pe", lambda e, h=h: e.matmul(pt[:, h, :], lhsT=lhs[:, h, :], rhs=rhs[:, h, :], start=True, stop=True),
                             reads=[Blhs, Brhs], writes=[Bpt])
                    return pt, Bpt

                def evac(d, pt, Bpt, pool=None, acc=None):
                    t, Bt = (pool or mb_r[d]).next()
                    if acc is None:
                        S.op("act", lambda e: e.copy(out=t[:], in_=pt[:]), reads=[Bpt], writes=[Bt])
                    else:
                        S.op("dve", lambda e: e.tensor_tensor(out=t[:], in0=pt[:], in1=acc[0][:], op=ALU.add),
                             reads=[Bpt, acc[1]], writes=[Bt])
                    return t, Bt

                def masked(d, src, Bsrc, mi_, eng):
                    t, Bt = mb_r[d].next()
                    S.op(eng, lambda e: e.tensor_tensor(out=t[:], in0=src[:], in1=mk7[:, mi_, :, :], op=ALU.mult),
                         reads=[Bsrc, BcC], writes=[Bt])
                    return t, Bt

                def plus_id(d, src, Bsrc):
                    t, Bt = mb_r[d].next()
                    S.op("pool", lambda e: e.tensor_tensor(out=t[:], in0=src[:], in1=id4[:], op=ALU.add), reads=[Bsrc, BcC], writes=[Bt])
                    return t, Bt

                def inverse(d, Ml, BMl, Nu, BNu):
                    Q, BQ = masked(d, Ml, BMl, 0, "dve")
                    P, BP = masked(d, Nu, BNu, 0, "pool")
                    Z, BZ = plus_id(d, Q, BQ)
                    Y, BY = plus_id(d, P, BP)
                    yield
                    for lvl in range(2):
                        pQ, BpQ = mm4(P, BP, Q, BQ)
                        Q2, BQ2 = evac(d, pQ, BpQ)
                        pP, BpP = mm4(Q, BQ, P, BP)
                        P2, BP2 = evac(d, pP, BpP)
                        Q, BQ, P, BP = Q2, BQ2, P2, BP2
                        yield
                        pY, BpY = mm4(Q, BQ, Y, BY)
                        Y2, BY2 = evac(d, pY, BpY, acc=(Y, BY))
                        pZ, BpZ = mm4(P, BP, Z, BZ)
                        Z2, BZ2 = evac(d, pZ, BpZ, acc=(Z, BZ))
                        Y, BY, Z, BZ = Y2, BY2, Z2, BZ2
                        yield
                    X, BX, XT, BXT = Z, BZ, Y, BY
                    for li in range(3):
                        Mo, BMo = masked(d, Ml, BMl, 1 + li, "dve")
                        No, BNo = masked(d, Nu, BNu, 4 + li, "pool")
                        pB, BpB = mm4(No, BNo, X, BX)
                        Bm, BBm = evac(d, pB, BpB)
                        pB2, BpB2 = mm4(Mo, BMo, XT, BXT)
                        Bm2, BBm2 = evac(d, pB2, BpB2)
                        yield
                        fin = fin_r[d] if li == 2 else None
                        pC, BpC = mm4(XT, BXT, Bm, BBm)
                        Xn, BXn = evac(d, pC, BpC, fin, acc=(X, BX))
                        pC2, BpC2 = mm4(X, BX, Bm2, BBm2)
                        XTn, BXTn = evac(d, pC2, BpC2, fin, acc=(XT, BXT))
                        X, BX, XT, BXT = Xn, BXn, XTn, BXTn
                        yield
                    return X, BX, XT, BXT

                ready = [[], []]
                segcache = [{}, {}]

                def prep_thread(d):
                    order = range(NB) if d == 0 else range(NB - 1, -1, -1)
                    for bi in order:
                        while len(ready[d]) >= 1:
                            yield
                        sg, jb = divmod(bi, SEG // 128)
                        if sg not in segcache[d]:
                            segcache[d] = {sg: seg_prep(d, sg)}
                            yield
                        yield from prep_block(d, bi, sg, jb)

                def prep_block(d, bi, sg, jb):
                    sd = segcache[d][sg]
                    scs, Bsc = sd["scs"], sd["Bscs"]
                    gtok = o0 + bi * 128
                    kTb, BkT = kT_r[d].next(); qTb, BqT = qT_r[d].next(); kkb, Bkk = kk_r[d].next(); vvb, Bvv = vv_r.next()
                    bc, Bbc = bc_r[d].next()
                    S.dma("sp", lambda e: e.dma_start(out=kTb[:], in_=DKT[:, gtok:gtok + 128].rearrange("(h p) t -> p h t", p=128)), writes=[BkT])
                    S.dma("sp", lambda e: e.dma_start(out=qTb[:], in_=DQT[:, gtok:gtok + 128].rearrange("(h p) t -> p h t", p=128)), writes=[BqT])
                    S.dma("sp", lambda e: e.dma_start(out=kkb[:], in_=DKK[gtok:gtok + 128, :]), writes=[Bkk])
                    S.dma("sp", lambda e: e.dma_start(out=vvb[:], in_=DVV[gtok:gtok + 128, :]), writes=[Bvv])
                    r0 = d * 12
                    c0 = jb * 128
                    S.dma("sp", lambda e: e.dma_start(out=bc[:], in_=BCS[r0:r0 + 8, c0:c0 + 128].partition_broadcast(128)),
                          reads=[Bbcs[d]], writes=[Bbc])
                    yield
                    E1, BE1 = E_r[d].next(); E1T, BE1T = E_r[d].next(); E2T, BE2T = E_r[d].next()
                    mi = (0, 1, 2) if d == 0 else (1, 0, 3)

                    def emat(E, BE, h, src_i, sgn, mk, bias_q):
                        dm, Bdm = dm_r.next()
                        S.op("pool", lambda e: e.tensor_tensor(
                            out=dm[:], in0=masks[:, mk, :], in1=bc[:, src_i * 4 + h, :], op=(ALU.add if sgn > 0 else ALU.subtract)),
                            reads=[Bbc, BcC], writes=[Bdm])
                        S.op("act", lambda e: e.activation(out=E[:, h, :], in_=dm[:], func=AF.Exp,
                                                           bias=scs[:, jb, bias_q * 4 + h:bias_q * 4 + h + 1]),
                             reads=[Bdm, Bsc], writes=[BE])
                    for h in range(4):
                        emat(E1, BE1, h, 0, 1.0, mi[0], GB)
                        emat(E1T, BE1T, h, 1, 1.0, mi[1], NGC)
                    yield
                    for h in range(4):
                        emat(E2T, BE2T, h, 0, -1.0, mi[2], NGC)
                    Q0, BQ0 = m0_r[d].next(); P0, BP0 = m0_r[d].next()
                    pG, BpG = pp_r.next()
                    for h in range(4):
                        S.op("pe", lambda e, h=h: e.matmul(pG[:, h, :], lhsT=kTb[:, h, :], rhs=kTb[:, h, :], start=True, stop=True),
                             reads=[BkT], writes=[BpG])
                    S.op("dve", lambda e: e.scalar_tensor_tensor(out=Q0[:], in0=pG[:], scalar=-1.0, in1=E1[:], op0=ALU.mult, op1=ALU.mult),
                         reads=[BpG, BE1], writes=[BQ0])
                    S.op("dve", lambda e: e.scalar_tensor_tensor(out=P0[:], in0=pG[:], scalar=-1.0, in1=E1T[:], op0=ALU.mult, op1=ALU.mult),
                         reads=[BpG, BE1T], writes=[BP0])
                    yield
                    pKQ, BpKQ = pp_r.next()
                    for h in range(4):
                        S.op("pe", lambda e, h=h: e.matmul(pKQ[:, h, :], lhsT=kTb[:, h, :], rhs=qTb[:, h, :], start=True, stop=True),
                             reads=[BkT, BqT], writes=[BpKQ])
                    iT, BiT = it_r[d].next()
                    S.op("dve", lambda e: e.tensor_tensor(out=iT[:], in0=pKQ[:], in1=E2T[:], op=ALU.mult),
                         reads=[BpKQ, BE2T], writes=[BiT])
                    bv, Bbv = bv_r[d].next()
                    S.op("pool", lambda e: e.tensor_tensor(
                        out=bv[:], in0=vvb[:].rearrange("p (h d) -> p h d", d=128),
                        in1=scs[:, jb, BETA * 4:BETA * 4 + 4].unsqueeze(2).to_broadcast([128, 4, 128]), op=ALU.mult),
                        reads=[Bvv, Bsc], writes=[Bbv])
                    yield
                    if d == 0:
                        X_, BX_, XT_, BXT_ = yield from inverse(d, Q0, BQ0, P0, BP0)
                        Y, BY = XT_, BXT_
                    else:
                        X_, BX_, XT_, BXT_ = yield from inverse(d, P0, BP0, Q0, BQ0)
                        Y, BY = X_, BX_
                    ready[d].append(dict(bi=bi, jb=jb, sd=sd, kTb=kTb, BkT=BkT, qTb=qTb, BqT=BqT, kkb=kkb, Bkk=Bkk,
                                         Y=Y, BY=BY, iT=iT, BiT=BiT, bv=bv, Bbv=Bbv))
                    yield

                def scan_thread(d):
                    for _ in range(NB):
                        while not ready[d]:
                            yield
                        yield from scan_block(d, ready[d].pop(0))

                def scan_block(d, pr):
                    ob, Bob = o_r[d].next()
                    for cp in ((0, 64) if d == 0 else (64, 0)):
                        yield from chunk(d, cp, pr, ob, Bob)
                    gtok = o0 + pr["bi"] * 128
                    dst = (OSC if d == 0 else OSB)[gtok:gtok + 128, :].rearrange("p (h d) -> p h d", d=128)
                    S.dma("pool", lambda e: e.dma_start(out=dst, in_=ob[:]), reads=[Bob])
                    yield

                def chunk(d, cp, pr, ob, Bob):
                    sd = pr["sd"]; jb = pr["jb"]
                    scs, Bsc, eglb, Beglb = sd["scs"], sd["Bscs"], sd["eglb"], sd["Beglb"]
                    kTb, BkT, qTb, BqT, kkb, Bkk = pr["kTb"], pr["BkT"], pr["qTb"], pr["BqT"], pr["kkb"], pr["Bkk"]
                    Y, BY, iT, BiT, bv, Bbv = pr["Y"], pr["BY"], pr["iT"], pr["BiT"], pr["bv"], pr["Bbv"]
                    ck = (jb * 128 + cp) // 64
                    sl = slice(cp, cp + 64)
                    def scol(q, h):
                        return scs[sl, jb, q * 4 + h:q * 4 + h + 1]
                    pk, Bpk = pkq_r.next(); pq, Bpq = pkq_r.next()
                    for h in range(4):
                        S.op("pe", lambda e, h=h: e.matmul(pk[sl, h, :], lhsT=kTb[:, h, cp:cp + 64], rhs=Sb_[d][:, h, :], start=True, stop=True),
                             reads=[BkT, BSb[d]], writes=[Bpk])
                    for h in range(4):
                        S.op("pe", lambda e, h=h: e.matmul(pq[sl, h, :], lhsT=qTb[:, h, cp:cp + 64], rhs=Sb_[d][:, h, :], start=True, stop=True),
                             reads=[BqT, BSb[d]], writes=[Bpq])
                    r, Br = r_r[d].next(); qs, Bqs = qs_r[d].next(); vn, Bvn = vn_r[d].next(); vd, Bvd = vd_r[d].next()
                    for h in range(4):
                        S.op("dve", lambda e, h=h: e.scalar_tensor_tensor(
                            out=r[sl, h, :], in0=pk[sl, h, :], scalar=scol(NBEG, h), in1=bv[sl, h, :],
                            op0=ALU.mult, op1=ALU.add), reads=[Bpk, Bsc, Bbv], writes=[Br])
                    for h in range(4):
                        S.op("act", lambda e, h=h: e.activation(out=qs[sl, h, :], in_=pq[sl, h, :], func=AF.Copy, scale=scol(EGC, h)),
                             reads=[Bpq, Bsc], writes=[Bqs])
                    yield
                    pv, Bpv = pvo_r.next()
                    for h in range(4):
                        S.op("pe", lambda e, h=h: e.matmul(pv[sl, h, :], lhsT=Y[sl, h, cp:cp + 64], rhs=r[sl, h, :], start=True, stop=True),
                             reads=[BY, Br], writes=[Bpv])
                    S.op("dve", lambda e: e.tensor_copy(out=vn[sl, :, :], in_=pv[sl, :, :]), reads=[Bpv], writes=[Bvn])
                    for h in range(4):
                        S.op("dve", lambda e, h=h: e.tensor_scalar(out=vd[sl, h, :], in0=pv[sl, h, :], scalar1=scol(EDK, h),
                                                                   scalar2=None, op0=ALU.mult), reads=[Bpv, Bsc], writes=[Bvd])
                    yield
                    po, Bpo = pvo_r.next()
                    for h in range(4):
                        S.op("pe", lambda e, h=h: e.matmul(po[sl, h, :], lhsT=iT[sl, h, cp:cp + 64], rhs=vn[sl, h, :], start=True, stop=True),
                             reads=[BiT, Bvn], writes=[Bpo])
                    S.op("dve", lambda e: e.tensor_tensor(out=ob[sl, :, :], in0=po[sl, :, :], in1=qs[sl, :, :], op=ALU.add),
                         reads=[Bpo, Bqs], writes=[Bob])
                    pS, BpS = pS_r.next()
                    for h in range(4):
                        S.op("pe", lambda e, h=h: e.matmul(pS[:, h, :], lhsT=kkb[sl, h * 128:(h + 1) * 128], rhs=vd[sl, h, :],
                                                           start=True, stop=True), reads=[Bkk, Bvd], writes=[BpS])
                    for h in range(4):
                        S.op("dve", lambda e, h=h: e.scalar_tensor_tensor(
                            out=Sf[d][:, h, :], in0=Sf[d][:, h, :], scalar=eglb[:, h, ck:ck + 1], in1=pS[:, h, :],
                            op0=ALU.mult, op1=ALU.add), reads=[BSf[d], Beglb, BpS], writes=[BSf[d]])
                    S.op("act", lambda e: e.copy(out=Sb_[d][:], in_=Sf[d][:]), reads=[BSf[d]], writes=[BSb[d]])
                    yield

                for d in range(2):
                    def init_dir(d):
                        S.dma("sp", lambda e: e.dma_start(out=dg[d][:, 0:2], in_=dgate_d[d * 4:d * 4 + 4, :]), writes=[Bdg[d]])
                        S.op("act", lambda e: e.activation(out=dg[d][:, 2:3], in_=dg[d][:, 0:1], func=AF.Exp), reads=[Bdg[d]], writes=[Bdg[d]])
                        S.op("dve", lambda e: e.tensor_scalar(out=dg[d][:, 2:3], in0=dg[d][:, 2:3], scalar1=-1.0, scalar2=None, op0=ALU.mult),
                             reads=[Bdg[d]], writes=[Bdg[d]])
                        S.op("pool", lambda e: e.memset(Sf[d][:], 0.0), writes=[BSf[d]])
                        S.op("pool", lambda e: e.memset(Sb_[d][:], 0.0), writes=[BSb[d]])
                    init_dir(d)
                interleave([prep_thread(0), scan_thread(0), prep_thread(1), scan_thread(1)], 4)

        def stage_D(b, wo, Bwo, don, BcD):
            T = seqs[b]
            o0 = offs[b]
            NTL = T // 128
            with ExitStack() as es:
                bc = sb("bcD", [128, D], F32, es)
                Bbc = Buf("bcD")
                S.dma("sp", lambda e: e.dma_start(out=bc[:], in_=MOD[b:b + 1, 2 * D:3 * D].partition_broadcast(128)), writes=[Bbc])
                at_r = Rot(es, nc, "atD", 5, [128, 512], BF16)
                os_r = Rot(es, nc, "osD", 5, [128, 4, 128], F32)
                osb_r = Rot(es, nc, "osbD", 5, [128, 4, 128], F32)
                zt_r = Rot(es, nc, "ztD", 5, [128, 4, 128], BF16)
                x_r = Rot(es, nc, "xD", 5, [128, D], F32)
                junk = sb("junkD", [128, 128], F32, es)
                ss_r = Rot(es, nc, "ssD", 5, [128, 4, 1], F32)
                on_r = Rot(es, nc, "onD", 5, [128, 4, 128], BF16)
                ct_r = Rot(es, nc, "ctD", 5, [128, 8, 128], BF16)
                ptr_r = Rot(es, nc, "ptrD", 2, [128, 8, 128], BF16, psum=True)
                po_r = Rot(es, nc, "poD", 4, [128, 512], F32, psum=True)
                t_r = Rot(es, nc, "tD", 4, [128, D], F32)
                xn_r = Rot(es, nc, "xnD", 4, [128, D], F32)

                def loads(i):
                    g = o0 + i * 128
                    at, Bat = at_r.next(); osc, Bos = os_r.next(); zt, Bzt = zt_r.next(); xt, Bxt = x_r.next()
                    S.dma("sp", lambda e: e.dma_start(out=at[:], in_=ATT[g:g + 128, :]), writes=[Bat])
                    S.dma("sp", lambda e: e.dma_start(out=osc[:], in_=OSC[g:g + 128, :].rearrange("p (h d) -> p h d", d=128)), writes=[Bos])
                    osb, Bosb = osb_r.next()
                    S.dma("sp", lambda e: e.dma_start(out=osb[:], in_=OSB[g:g + 128, :].rearrange("p (h d) -> p h d", d=128)), writes=[Bosb])
                    S.op("pool", lambda e: e.tensor_tensor(out=osc[:], in0=osc[:], in1=osb[:], op=ALU.add), reads=[Bos, Bosb], writes=[Bos])
                    S.dma("sp", lambda e: e.dma_start(out=zt[:], in_=ZT[:, g:g + 128].rearrange("(h p) t -> p h t", p=128)), writes=[Bzt])
                    S.dma("sp", lambda e: e.dma_start(out=xt[:], in_=x_d[g + 1:g + 129, :]), writes=[Bxt])
                    return (at, Bat, osc, Bos, zt, Bzt, xt, Bxt)

                def tile(i):
                    at, Bat, osc, Bos, zt, Bzt, xt, Bxt = loads(i)
                    g = o0 + i * 128
                    ss, Bss = ss_r.next(); on, Bon = on_r.next(); ct, Bct = ct_r.next()
                    yield
                    for h in range(4):
                        S.op("act", lambda e, h=h: e.activation(out=junk[:], in_=osc[:, h, :], func=AF.Square, accum_out=ss[:, h, :]),
                             reads=[Bos], writes=[Bss])
                    S.op("act", lambda e: e.activation(out=ss[:], in_=ss[:], func=AF.Ln, scale=1.0 / HD_D, bias=EPS), reads=[Bss], writes=[Bss])
                    S.op("act", lambda e: e.activation(out=ss[:], in_=ss[:], func=AF.Exp, scale=-0.5), reads=[Bss], writes=[Bss])
                    yield
                    S.op("dve", lambda e: e.tensor_tensor(out=on[:], in0=osc[:], in1=ss[:].to_broadcast([128, 4, 128]), op=ALU.mult),
                         reads=[Bos, Bss], writes=[Bon])
                    yield
                    ptr, Bptr = ptr_r.next()
                    for j in range(4):
                        S.op("pe", lambda e, j=j: e.transpose(out=ptr[:, j, :], in_=at[:, j * 128:(j + 1) * 128], identity=idb[:]),
                             reads=[Bat, Bconst], writes=[Bptr])
                    for h in range(4):
                        S.op("pe", lambda e, h=h: e.transpose(out=ptr[:, 4 + h, :], in_=on[:, h, :], identity=idb[:]),
                             reads=[Bon, Bconst], writes=[Bptr])
                    S.op("act", lambda e: e.copy(out=ct[:, 0:4, :], in_=ptr[:, 0:4, :]), reads=[Bptr], writes=[Bct])
                    S.op("dve", lambda e: e.scalar_tensor_tensor(out=ct[:, 4:8, :], in0=ptr[:, 4:8, :], scalar=don[:, 0:1], in1=zt[:],
                                                                 op0=ALU.mult, op1=ALU.mult), reads=[Bptr, Bzt, BcD], writes=[Bct])
                    yield
                    xn, Bxn = xn_r.next()
                    tt, Btt = t_r.next()
                    for hf in range(2):
                        po, Bpo = po_r.next()
                        for kc in range(8):
                            S.op("pe", lambda e, kc=kc, po=po, hf=hf: e.matmul(
                                po[:], lhsT=ct[:, kc, :], rhs=wo[:, kc, hf * 512:(hf + 1) * 512], start=(kc == 0), stop=(kc == 7)),
                                reads=[Bct, Bwo], writes=[Bpo])
                        S.op("dve", lambda e, po=po, hf=hf: e.tensor_tensor(
                            out=tt[:, hf * 512:(hf + 1) * 512], in0=po[:], in1=bc[:, hf * 512:(hf + 1) * 512], op=ALU.mult),
                            reads=[Bpo, Bbc], writes=[Btt])
                    yield
                    S.op("pool", lambda e: e.tensor_tensor(out=xn[:], in0=tt[:], in1=xt[:], op=ALU.add), reads=[Btt, Bxt], writes=[Bxn])
                    S.dma("pool", lambda e: e.dma_start(out=XN[g + 1:g + 129, :], in_=xn[:]), reads=[Bxn])

                interleave([tile(i) for i in range(NTL)], 3)

        def stage_E():
            with ExitStack() as es:
                wdn = sb("wdn", [128, 22, D], BF16, es)
                Bwdn = Buf("wdn")
                for c in range(22):
                    S.dma("pool", lambda e, c=c: e.dma_start(out=wdn[:, c, :], in_=wdn_d[c * 128:(c + 1) * 128, :]), writes=[Bwdn])
                fcw = sb("fcw", [128, 44, 3], F32, es)
                fcb = sb("fcb", [128, 44], F32, es)
                BcE = Buf("constE")
                S.dma("sp", lambda e: e.dma_start(out=fcw[:], in_=fcw_d), writes=[BcE])
                S.dma("sp", lambda e: e.dma_start(out=fcb[:], in_=fcb_d), writes=[BcE])
                wu_r = Rot(es, nc, "wuE", 3, [128, 2, 8, 128], BF16)
                Bwub = Buf("WUB")
                for c in range(22):
                    def cast_pair(c):
                        wu, Bwu = wu_r.next()
                        for ab in range(2):
                            col = ab * DFF + c * 128
                            S.dma("pool", lambda e, ab=ab, col=col: e.dma_start(
                                out=wu[:, ab, :, :], in_=wup_d[:, col:col + 128].rearrange("(kc p) n -> p kc n", p=128)), writes=[Bwu])
                        S.dma("sp", lambda e: e.dma_start(out=WUB[c], in_=wu[:]), reads=[Bwu], writes=[Bwub])
                    cast_pair(c)
                bcr = Rot(es, nc, "bcE", 1, [128, 3, D], F32)
                xtr = Rot(es, nc, "xtE", 2, [128, 4, D], F32)
                h2r = Rot(es, nc, "h2T", 2, [128, 8, 512], BF16)
                junk = sb("junkE", [128, D], F32, es)
                st_r = Rot(es, nc, "stE", 4, [128, 4], F32)
                hf_r = Rot(es, nc, "hfE", 1, [128, D], F32)
                hb_r = Rot(es, nc, "hbE", 2, [128, D], BF16)
                gt_r = Rot(es, nc, "gtE", 1, [128, 22, 512], BF16)
                ca_r = Rot(es, nc, "caE", 3, [128, 512], F32)
                cb_r = Rot(es, nc, "cbE", 3, [128, 512], F32)
                sa_r = Rot(es, nc, "saE", 3, [128, 512], F32)
                yt_r = Rot(es, nc, "ytE", 2, [128, D], F32)
                ptr_r = Rot(es, nc, "ptrE", 2, [128, 8, 128], BF16, psum=True)
                pab_r = Rot(es, nc, "pabE", 4, [128, 512], F32, psum=True)
                po_r = Rot(es, nc, "poE", 2, [128, 512], F32, psum=True)
                for t, B_ in xtr.t + gt_r.t:
                    S.op("pool", lambda e, t=t: e.memset(t[:], 0.0), writes=[B_])

                def load_x(b, t0, NT):
                    xt, Bxt = xtr.next()
                    NC = NT + 2
                    g0 = offs[b] + t0
                    for s in range(-(-NC // 128)):
                        rs = min(128, NC - s * 128)
                        S.dma("sp", lambda e, s=s, rs=rs: e.dma_start(
                            out=xt[:rs, s, :], in_=XN[g0 + s * 128: g0 + s * 128 + rs, :]), writes=[Bxt])
                    return xt, Bxt

                def do_tile(b, t0, NT, xt, Bxt, bc, Bbc):
                    NC = NT + 2
                    nsubc = -(-NC // 128)
                    h2T, Bh2 = h2r.next()
                    gt = offs[b] + t0

                    def norm_sub(s):
                        rs = min(128, NC - s * 128)
                        st, Bst = st_r.next(); hf, Bhf = hf_r.next(); hb, Bhb = hb_r.next(); ptr, Bptr = ptr_r.next()
                        S.op("act", lambda e: e.activation(out=junk[:rs, :], in_=xt[:rs, s, :], func=AF.Square, accum_out=st[:rs, 0:1]),
                             reads=[Bxt], writes=[Bst])
                        S.op("act", lambda e: e.activation(out=st[:rs, 1:2], in_=st[:rs, 0:1], func=AF.Ln, scale=1.0 / D, bias=EPS),
                             reads=[Bst], writes=[Bst])
                        S.op("act", lambda e: e.activation(out=st[:rs, 2:3], in_=st[:rs, 1:2], func=AF.Exp, scale=-0.5), reads=[Bst], writes=[Bst])
                        S.op("dve", lambda e: e.scalar_tensor_tensor(out=hf[:rs, :], in0=xt[:rs, s, :], scalar=st[:rs, 2:3], in1=bc[:rs, 0, :],
                                                                     op0=ALU.mult, op1=ALU.mult), reads=[Bxt, Bst, Bbc], writes=[Bhf])
                        S.op("pool", lambda e: e.tensor_tensor(out=hb[:rs, :], in0=hf[:rs, :], in1=bc[:rs, 1, :], op=ALU.add),
                             reads=[Bhf, Bbc], writes=[Bhb])
                        for kc in range(8):
                            S.op("pe", lambda e, kc=kc: e.transpose(out=ptr[:, kc, 0:rs], in_=hb[:rs, kc * 128:(kc + 1) * 128],
                                                                    identity=idb[:rs, :rs]), reads=[Bhb, Bconst], writes=[Bptr])
                        S.op("act", lambda e: e.copy(out=h2T[:, :, s * 128:s * 128 + rs], in_=ptr[:, :, 0:rs]), reads=[Bptr], writes=[Bh2])
                    for s in range(nsubc):
                        norm_sub(s)
                    if t0 == 0:
                        S.op("pool", lambda e: e.memset(h2T[:, :, 0:1], 0.0), writes=[Bh2])
                    if t0 + NT == seqs[b]:
                        S.op("pool", lambda e: e.memset(h2T[:, :, NC - 1:NC], 0.0), writes=[Bh2])

                    gtd, Bgt = gt_r.next()

                    def pair(c):
                        wu, Bwu = wu_r.next()
                        S.dma("sp", lambda e: e.dma_start(out=wu[:], in_=WUB[c]), reads=[Bwub], writes=[Bwu])
                        res = []
                        for ab, tr in ((0, ca_r), (1, cb_r)):
                            pp, Bpp = pab_r.next()
                            for kc in range(8):
                                S.op("pe", lambda e, kc=kc, ab=ab, pp=pp: e.matmul(pp[:, :NC], lhsT=wu[:, ab, kc, :], rhs=h2T[:, kc, 0:NC],
                                                                                 start=(kc == 0), stop=(kc == 7)), reads=[Bwu, Bh2], writes=[Bpp])
                            cv, Bcv = tr.next()
                            ch = ab * 22 + c
                            S.op("act", lambda e, pp=pp, cv=cv, ch=ch: e.activation(out=cv[:, :NT], in_=pp[:, 1:1 + NT], func=AF.Identity,
                                                                                   scale=fcw[:, ch, 1:2], bias=fcb[:, ch:ch + 1]),
                                 reads=[Bpp, BcE], writes=[Bcv])
                            res.append((pp, Bpp, cv, Bcv, ch))
                        yield
                        for (pp, Bpp, cv, Bcv, ch) in res:
                            S.op("dve", lambda e, pp=pp, cv=cv, ch=ch: e.scalar_tensor_tensor(
                                out=cv[:, :NT], in0=pp[:, 0:NT], scalar=fcw[:, ch, 0:1], in1=cv[:, :NT],
                                op0=ALU.mult, op1=ALU.add), reads=[Bpp, BcE, Bcv], writes=[Bcv])
                            S.op("dve", lambda e, pp=pp, cv=cv, ch=ch: e.scalar_tensor_tensor(
                                out=cv[:, :NT], in0=pp[:, 2:2 + NT], scalar=fcw[:, ch, 2:3], in1=cv[:, :NT],
                                op0=ALU.mult, op1=ALU.add), reads=[Bpp, BcE, Bcv], writes=[Bcv])
                        yield
                        ca, Bca = res[0][2], res[0][3]
                        cb2, Bcb2 = res[1][2], res[1][3]
                        sa, Bsa = sa_r.next()
                        S.op("act", lambda e: e.activation(out=sa[:, :NT], in_=ca[:, :NT], func=AF.Silu), reads=[Bca], writes=[Bsa])
                        S.op("pool", lambda e: e.tensor_tensor(out=gtd[:, c, 1:1 + NT], in0=sa[:, :NT], in1=cb2[:, :NT], op=ALU.mult),
                             reads=[Bsa, Bcb2], writes=[Bgt])
                    interleave([pair(c) for c in range(22)], 2)

                    def down(s):
                        rs = min(128, NC - s * 128)
                        yt, Byt = yt_r.next()
                        for hf_ in range(2):
                            def dhalf(hf_):
                                po, Bpo = po_r.next()
                                for c in range(22):
                                    S.op("pe", lambda e, c=c: e.matmul(po[:rs, :], lhsT=gtd[:, c, s * 128:s * 128 + rs],
                                                                       rhs=wdn[:, c, hf_ * 512:(hf_ + 1) * 512], start=(c == 0), stop=(c == 21)),
                                         reads=[Bgt, Bwdn], writes=[Bpo])
                                S.op("dve", lambda e: e.tensor_tensor(out=yt[:rs, hf_ * 512:(hf_ + 1) * 512], in0=po[:rs, :],
                                                                      in1=bc[:rs, 2, hf_ * 512:(hf_ + 1) * 512], op=ALU.mult),
                                     reads=[Bpo, Bbc], writes=[Byt])
                            dhalf(hf_)
                        S.op("pool", lambda e: e.tensor_tensor(out=yt[:rs, :], in0=yt[:rs, :], in1=xt[:rs, s, :], op=ALU.add),
                             reads=[Byt, Bxt], writes=[Byt])
                        p_lo = 1 if s == 0 else 0
                        p_hi = min(rs, NT + 1 - s * 128)
                        if p_hi > p_lo:
                            tok = gt - 1 + s * 128
                            S.dma("pool", lambda e: e.dma_start(out=y_d[tok + p_lo:tok + p_hi, :], in_=yt[p_lo:p_hi, :]), reads=[Byt])
                    for s in range(nsubc):
                        down(s)

                work = [(b, t0, NT) for b in range(NSEQ) for (t0, NT) in tiles_of(seqs[b])]
                nxt = load_x(*work[0])
                cur_b = -1
                bcs = None
                for wi, (b, t0, NT) in enumerate(work):
                    xt, Bxt = nxt
                    if wi + 1 < len(work):
                        nxt = load_x(*work[wi + 1])
                    if b != cur_b:
                        cur_b = b
                        bcs = bcr.next()
                        load_bcast(bcs[0], bcs[1], b, (4, 3, 5))
                    do_tile(b, t0, NT, xt, Bxt, bcs[0], bcs[1])

        if "A" in stages:
            with ExitStack() as es:
                win = sb("win", [128, 8, INC], BF16, es)
                Bwin = Buf("win")
                for kc in range(8):
                    S.dma("pool", lambda e, kc=kc: e.dma_start(out=win[:, kc, :], in_=win_d[kc * 128:(kc + 1) * 128, :]),
                          writes=[Bwin])
                qkn = sb("qkn", [128, 2], F32, es)
                dcw = sb("dcw", [128, 12, 3], F32, es)
                BcA = Buf("constA")
                S.dma("sp", lambda e: e.dma_start(out=qkn[:], in_=qkn_d), writes=[BcA])
                S.dma("sp", lambda e: e.dma_start(out=dcw[:], in_=dcw_d), writes=[BcA])
                S.op("dve", lambda e: e.tensor_scalar(out=qkn[:, 0:1], in0=qkn[:, 0:1], scalar1=HD_A ** -0.5, scalar2=None,
                                                      op0=ALU.mult), reads=[BcA], writes=[BcA])
                bcr = Rot(es, nc, "bcA", 2, [128, 2, D], F32)
                xtr = Rot(es, nc, "xt", 2, [128, 4, D], F32)
                h1r = Rot(es, nc, "h1T", 2, [128, 8, 512], BF16)
                junk = sb("junkA", [128, D], F32, es)
                st_r = Rot(es, nc, "stA", 4, [128, 4], F32)
                hf_r = Rot(es, nc, "hfA", 2, [128, D], F32)
                hb_r = Rot(es, nc, "hbA", 2, [128, D], BF16)
                ptr_r = Rot(es, nc, "ptrA", 2, [128, 8, 128], BF16, psum=True)
                pp_r = Rot(es, nc, "ppA", 4, [128, 512], F32, psum=True)
                pss_r = Rot(es, nc, "pssA", 2, [128, 512], F32, psum=True)
                sqb_r = Rot(es, nc, "sqbA", 4, [128, 512], BF16)
                rs1_r = Rot(es, nc, "rs1A", 4, [128, 512], F32)
                rs2_r = Rot(es, nc, "rs2A", 4, [128, 512], F32)
                ob_r = Rot(es, nc, "obA", 6, [128, 512], BF16)
                cb_r = Rot(es, nc, "cbA", 4, [128, 512], F32)
                sl_r = Rot(es, nc, "slA", 4, [128, 512], F32)
                ee_r = Rot(es, nc, "eeA", 4, [128, 512], F32)
                tok_r = Rot(es, nc, "tokA", 3, [128, 4, 512], BF16)
                gs_r = Rot(es, nc, "gsA", 2, [16, 512], F32)
                for t, _ in xtr.t:
                    S.op("pool", lambda e, t=t: e.memset(t[:], 0.0), writes=[_])

                def load_x(b, t0, NT):
                    xt, Bxt = xtr.next()
                    NC = NT + 2
                    g0 = offs[b] + t0
                    for s in range(-(-NC // 128)):
                        rs = min(128, NC - s * 128)
                        S.dma("sp", lambda e, xt=xt, s=s, rs=rs, g0=g0: e.dma_start(
                            out=xt[:rs, s, :], in_=x_d[g0 + s * 128: g0 + s * 128 + rs, :]), writes=[Bxt])
                    return xt, Bxt

                def do_tile(b, t0, NT, xt, Bxt, bc, Bbc):
                    NC = NT + 2
                    h1T, Bh1 = h1r.next()
                    gt = offs[b] + t0
                    nsub = -(-NT // 128)

                    def norm_sub(s):
                        rs = min(128, NC - s * 128)
                        st, Bst = st_r.next()
                        hf, Bhf = hf_r.next()
                        hb, Bhb = hb_r.next()
                        ptr, Bptr = ptr_r.next()
                        S.op("act", lambda e: e.activation(
                            out=junk[:rs, :], in_=xt[:rs, s, :], func=AF.Square, accum_out=st[:rs, 0:1]),
                            reads=[Bxt], writes=[Bst])
                        S.op("act", lambda e: e.activation(
                            out=st[:rs, 1:2], in_=st[:rs, 0:1], func=AF.Ln, scale=1.0 / D, bias=EPS),
                            reads=[Bst], writes=[Bst])
                        S.op("act", lambda e: e.activation(out=st[:rs, 2:3], in_=st[:rs, 1:2], func=AF.Exp, scale=-0.5),
                             reads=[Bst], writes=[Bst])
                        S.op("dve", lambda e: e.scalar_tensor_tensor(
                            out=hf[:rs, :], in0=xt[:rs, s, :], scalar=st[:rs, 2:3], in1=bc[:rs, 0, :],
                            op0=ALU.mult, op1=ALU.mult), reads=[Bxt, Bst, Bbc], writes=[Bhf])
                        S.op("pool", lambda e: e.tensor_tensor(
                            out=hb[:rs, :], in0=hf[:rs, :], in1=bc[:rs, 1, :], op=ALU.add),
                            reads=[Bhf, Bbc], writes=[Bhb])
                        for kc in range(8):
                            S.op("pe", lambda e, kc=kc: e.transpose(
                                out=ptr[:, kc, 0:rs], in_=hb[:rs, kc * 128:(kc + 1) * 128], identity=idb[:rs, :rs]),
                                reads=[Bhb, Bconst], writes=[Bptr])
                        S.op("act", lambda e: e.copy(
                            out=h1T[:, :, s * 128:s * 128 + rs], in_=ptr[:, :, 0:rs]), reads=[Bptr], writes=[Bh1])

                    for s in range(-(-NC // 128)):
                        norm_sub(s)
                    if t0 == 0:
                        S.op("pool", lambda e: e.memset(h1T[:, :, 0:1], 0.0), writes=[Bh1])
                    if t0 + NT == seqs[b]:
                        S.op("pool", lambda e: e.memset(h1T[:, :, NC - 1:NC], 0.0), writes=[Bh1])

                    def proj(c0, ncols):
                        pp, Bpp = pp_r.next()
                        for kc in range(8):
                            S.op("pe", lambda e, kc=kc: e.matmul(
                                pp[:ncols, :NC], lhsT=win[:, kc, c0:c0 + ncols], rhs=h1T[:, kc, 0:NC],
                                start=(kc == 0), stop=(kc == 7)), reads=[Bwin, Bh1], writes=[Bpp])
                        return pp, Bpp

                    def inv_norm(src, Bsrc, lhs, scale):
                        sqb, Bsq = sqb_r.next()
                        pss, Bps = pss_r.next()
                        rs1, Br1 = rs1_r.next()
                        rs2, Br2 = rs2_r.next()
                        S.op("act", lambda e: e.activation(out=sqb[:, :NT], in_=src, func=AF.Square),
                             reads=[Bsrc], writes=[Bsq])
                        yield
                        S.op("pe", lambda e: e.matmul(pss[:, :NT], lhsT=lhs[:], rhs=sqb[:, :NT], start=True, stop=True),
                             reads=[Bsq, Bconst], writes=[Bps])
                        S.op("act", lambda e: e.activation(out=rs1[:, :NT], in_=pss[:, :NT], func=AF.Ln,
                                                           scale=scale, bias=EPS), reads=[Bps], writes=[Br1])
                        S.op("act", lambda e: e.activation(out=rs2[:, :NT], in_=rs1[:, :NT], func=AF.Exp, scale=-0.5),
                             reads=[Br1], writes=[Br2])
                        yield
                        return rs2, Br2

                    def to_tok(src, Bsrc, tk, Btk, col):
                        def one(s):
                            rs = min(128, NT - s * 128)
                            ptr, Bptr = ptr_r.next()
                            S.op("pe", lambda e: e.transpose(
                                out=ptr[:rs, 0, :], in_=src[:, s * 128:s * 128 + rs], identity=idb[:]),
                                reads=[Bsrc, Bconst], writes=[Bptr])
                            S.op("dve", lambda e: e.tensor_copy(
                                out=tk[:rs, s, col:col + 128], in_=ptr[:rs, 0, :]), reads=[Bptr], writes=[Btk])
                        for s in range(nsub):
                            one(s)

                    def store_tok(dst, tk, Btk):
                        for s in range(nsub):
                            rs = min(128, NT - s * 128)
                            S.dma("sp", lambda e, s=s, rs=rs: e.dma_start(
                                out=dst[gt + s * 128: gt + s * 128 + rs, :], in_=tk[:rs, s, :]), reads=[Btk])

                    tk, Btk = tok_r.next()
                    tkk, Btkk = tok_r.next()
                    tkv, Btkv = tok_r.next()

                    def silu(dst, Bdst, src, Bsrc):
                        ee, Bee = ee_r.next()
                        S.op("act", lambda e: e.activation(out=ee[:, :NT], in_=src, func=AF.Exp, scale=-1.0),
                             reads=[Bsrc], writes=[Bee])
                        yield
                        S.op("act", lambda e: e.activation(out=ee[:, :NT], in_=ee[:, :NT], func=AF.Ln, bias=1.0), reads=[Bee], writes=[Bee])
                        S.op("act", lambda e: e.activation(out=ee[:, :NT], in_=ee[:, :NT], func=AF.Exp, scale=-1.0), reads=[Bee], writes=[Bee])
                        S.op("dve", lambda e: e.tensor_tensor(out=dst[:, :NT], in0=src, in1=ee[:, :NT], op=ALU.mult),
                             reads=[Bsrc, Bee], writes=[Bdst])
                        yield

                    def qk_chunk(cb):
                        pp, Bpp = proj(cb * 128, 128)
                        rs2, Br2 = yield from inv_norm(pp[:, 1:1 + NT], Bpp, blk2, 1.0 / HD_A)
                        ob, Bob = ob_r.next()
                        j = 0 if cb < 4 else 1
                        S.op("dve", lambda e: e.scalar_tensor_tensor(
                            out=ob[:, :NT], in0=pp[:, 1:1 + NT], scalar=qkn[:, j:j + 1], in1=rs2[:, :NT],
                            op0=ALU.mult, op1=ALU.mult), reads=[Bpp, Br2, BcA], writes=[Bob])
                        dst = QT if cb < 4 else KT
                        r0 = (cb % 4) * 128
                        S.dma("sp", lambda e: e.dma_start(
                            out=dst[r0:r0 + 128, gt:gt + NT], in_=ob[:, :NT]), reads=[Bob])

                    def v_chunk(j):
                        pp, Bpp = proj(1024 + j * 128, 128)
                        ob, Bob = ob_r.next()
                        S.op("act", lambda e: e.copy(out=ob[:, :NT], in_=pp[:, 1:1 + NT]),
                             reads=[Bpp], writes=[Bob])
                        yield
                        to_tok(ob, Bob, tk, Btk, j * 128)

                    def dn_chunk(jb):
                        pp, Bpp = proj(1536 + jb * 128, 128)
                        cbf, Bcb = cb_r.next()
                        sl, Bsl = sl_r.next()
                        S.op("act", lambda e: e.activation(
                            out=cbf[:, :NT], in_=pp[:, 1:1 + NT], func=AF.Copy, scale=dcw[:, jb, 1:2]),
                            reads=[Bpp, BcA], writes=[Bcb])
                        yield
                        S.op("dve", lambda e: e.scalar_tensor_tensor(
                            out=cbf[:, :NT], in0=pp[:, 0:NT], scalar=dcw[:, jb, 0:1], in1=cbf[:, :NT],
                            op0=ALU.mult, op1=ALU.add), reads=[Bpp, BcA, Bcb], writes=[Bcb])
                        S.op("dve", lambda e: e.scalar_tensor_tensor(
                            out=cbf[:, :NT], in0=pp[:, 2:2 + NT], scalar=dcw[:, jb, 2:3], in1=cbf[:, :NT],
                            op0=ALU.mult, op1=ALU.add), reads=[Bpp, BcA, Bcb], writes=[Bcb])
                        yield
                        yield from silu(sl, Bsl, cbf[:, :NT], Bcb)
                        ob, Bob = ob_r.next()
                        hh = jb % 4
                        if jb < 8:
                            rs2, Br2 = yield from inv_norm(sl[:, :NT], Bsl, ones_b, 1.0)
                            S.op("dve", lambda e: e.scalar_tensor_tensor(
                                out=ob[:, :NT], in0=sl[:, :NT], scalar=(HD_D ** -0.5 if jb < 4 else 1.0), in1=rs2[:, :NT],
                                op0=ALU.mult, op1=ALU.mult), reads=[Bsl, Br2], writes=[Bob])
                            dst = DQT if jb < 4 else DKT
                            S.dma("sp", lambda e: e.dma_start(
                                out=dst[hh * 128:(hh + 1) * 128, gt:gt + NT], in_=ob[:, :NT]), reads=[Bob])
                            if jb >= 4:
                                yield
                                to_tok(ob, Bob, tkk, Btkk, hh * 128)
                        else:
                            yield
                            S.op("pool", lambda e: e.tensor_copy(out=ob[:, :NT], in_=sl[:, :NT]),
                                 reads=[Bsl], writes=[Bob])
                            yield
                            to_tok(ob, Bob, tkv, Btkv, hh * 128)

                    def z_chunk(j):
                        pp, Bpp = proj(3072 + j * 128, 128)
                        ob, Bob = ob_r.next()
                        yield
                        yield from silu(ob, Bob, pp[:, 1:1 + NT], Bpp)
                        S.dma("sp", lambda e: e.dma_start(
                            out=ZT[j * 128:(j + 1) * 128, gt:gt + NT], in_=ob[:, :NT]), reads=[Bob])

                    gens = [qk_chunk(cb) for cb in range(8)] + [v_chunk(j) for j in range(4)] + \
                           [dn_chunk(jb) for jb in range(12)] + [z_chunk(j) for j in range(4)]
                    interleave(gens, 3)
                    store_tok(VX, tk, Btk)
                    store_tok(DKK, tkk, Btkk)
                    store_tok(DVV, tkv, Btkv)

                    pp, Bpp = proj(3584, 16)
                    gs, Bgs = gs_r.next()
                    S.op("act", lambda e: e.copy(out=gs[:, :NT], in_=pp[:16, 1:1 + NT]),
                         reads=[Bpp], writes=[Bgs])
                    S.dma("sp", lambda e: e.dma_start(out=GR[:, gt:gt + NT], in_=gs[:, :NT]), reads=[Bgs])

                work = [(b, t0, NT) for b in range(NSEQ) for (t0, NT) in tiles_of(seqs[b])]
                nxt = load_x(*work[0])
                cur_b = -1
                bcs = None
                for wi, (b, t0, NT) in enumerate(work):
                    xt, Bxt = nxt
                    if wi + 1 < len(work):
                        nxt = load_x(*work[wi + 1])
                    if b != cur_b:
                        cur_b = b
                        bcs = bcr.next()
                        load_bcast(bcs[0], bcs[1], b, (1, 0))
                    do_tile(b, t0, NT, xt, Bxt, bcs[0], bcs[1])
            S.barrier()

        if any(c in stages for c in "BCD"):
            with ExitStack() as esM:
                rbt, BcM = None, None
                wo = sb("wo", [128, 8, D], BF16, esM)
                don = sb("don", [128, 1], F32, esM)
                Bwo, BcD = Buf("wo"), Buf("constD")
                if "D" in stages:
                    zrow = sb("zrow", [1, D], F32, esM)
                    Bz = Buf("zrow")
                    S.op("pool", lambda e: e.memset(zrow[:], 0.0), writes=[Bz])
                    S.dma("pool", lambda e: e.dma_start(out=XN[0:1, :], in_=zrow[:]), reads=[Bz])
                    S.dma("pool", lambda e: e.dma_start(out=XN[NTOK + 1:NTOK + 2, :], in_=zrow[:]), reads=[Bz])
                    for kc in range(8):
                        S.dma("pool", lambda e, kc=kc: e.dma_start(out=wo[:, kc, :], in_=wo_d[kc * 128:(kc + 1) * 128, :]), writes=[Bwo])
                    S.dma("sp", lambda e: e.dma_start(out=don[:], in_=don_d), writes=[BcD])
                for b in range(NSEQ):
                    if "B" in stages:
                        stage_B(b, rbt, BcM)
                        S.barrier()
                    if "C" in stages:
                        stage_C(b)
                        S.barrier()
                    if "D" in stages:
                        stage_D(b, wo, Bwo, don, BcD)
                        S.barrier()

        if "E" in stages:
            stage_E()

        S.emit(nc)
    return nc


def _consts(nseq):
    ident = np.eye(128, dtype=np.float32)
    blk2 = np.zeros((128, 128), np.float32)
    blk2[:64, :64] = 1.0
    blk2[64:, 64:] = 1.0
    i = np.arange(128)[:, None]
    j = np.arange(128)[None, :]
    same = (i // 64) == (j // 64)
    mk = [same & (j < i), same & (j > i), same & (j >= i), same & (j <= i)]
    masks = np.stack([np.where(m, np.float32(0.0), np.float32(NEG)) for m in mk], axis=1).astype(np.float32)
    m01 = np.ones((4, 512), np.float32)
    m01[:, ::64] = 0.0
    mk7 = [(i // 8) == (j // 8)]
    for m_ in (8, 16, 32):
        mk7.append(((i // (2 * m_)) == (j // (2 * m_))) & ((i // m_) % 2 == 1) & ((j // m_) % 2 == 0))
    for m_ in (8, 16, 32):
        mk7.append(((i // (2 * m_)) == (j // (2 * m_))) & ((i // m_) % 2 == 0) & ((j // m_) % 2 == 1))
    mk7 = np.ascontiguousarray(np.stack(mk7, axis=1).astype(np.float32))
    return {"ident": ident, "blk2": blk2, "masks": np.ascontiguousarray(masks), "m01": m01, "mk7": mk7}


def _rb_table(att_rpb):
    rpb = np.asarray(att_rpb, np.float32)
    krl = np.arange(2)[:, None, None, None]
    kc = np.arange(64)[None, :, None, None]
    e = np.arange(16)[None, None, :, None]
    qc = np.arange(64)[None, None, None, :]
    dr = krl + 7 - e
    w = np.clip(qc - 8, 0, 48)
    ok = (dr >= -7) & (dr <= 7) & (kc >= w) & (kc < w + 16)
    ri = np.clip(dr + 7, 0, 14)
    ci = np.clip(kc - qc + 15, 0, 30)
    ri, ci, ok = np.broadcast_arrays(ri, ci, ok)
    tab = rpb[:, ri, ci]
    tab = np.where(ok[None], tab, np.float32(NEG)).astype(np.float32)
    return np.ascontiguousarray(tab.transpose(1, 2, 0, 3, 4).reshape(128, 8, 16, 64))


def shared_inputs(nseq, ada_w, ada_b, norm1_g, norm2_g, w_in, att_q_norm, att_k_norm, att_rpb, dn_conv_w,
                  dn_a_log, dn_dt_bias, dn_out_norm, w_o, ffn_w_up, ffn_conv_w, ffn_conv_b, ffn_w_down):
    f = lambda a: np.ascontiguousarray(np.asarray(a, np.float32))
    m = dict(_consts(nseq))
    m["ada_w"] = f(ada_w[0])
    m["ada_b"] = f(np.broadcast_to(np.asarray(ada_b[0])[None, :], (nseq, 6 * D)))
    m["n12"] = f(np.broadcast_to(np.stack([np.asarray(norm1_g[0]), np.asarray(norm2_g[0])])[None], (nseq, 2, D)))
    m["w_in"] = f(w_in[0])
    m["qkn"] = f(np.stack([np.tile(np.asarray(att_q_norm[0]), 2), np.tile(np.asarray(att_k_norm[0]), 2)], axis=1))
    m["rb"] = _rb_table(att_rpb[0])
    m["dcw"] = f(np.asarray(dn_conv_w[0]).reshape(3, 12, 128).transpose(2, 1, 0))
    m["dgate"] = f(np.stack([np.asarray(dn_a_log[0]).reshape(8), np.asarray(dn_dt_bias[0]).reshape(8)], axis=1))
    m["don"] = f(np.asarray(dn_out_norm[0]).reshape(128, 1))
    m["w_o"] = f(w_o[0])
    m["w_up"] = f(ffn_w_up[0])
    m["fcw"] = f(np.asarray(ffn_conv_w[0]).reshape(3, 44, 128).transpose(2, 1, 0))
    m["fcb"] = f(np.asarray(ffn_conv_b[0]).reshape(44, 128).T)
    m["w_dn"] = f(ffn_w_down[0])
    return m


def core_inputs(shared, xs, cs):
    m = dict(shared)
    x = np.concatenate([np.zeros((1, D), np.float32)] + [np.asarray(a, np.float32) for a in xs]
                       + [np.zeros((1, D), np.float32)], axis=0)
    m["x"] = np.ascontiguousarray(x)
    c = np.stack([np.asarray(a, np.float32) for a in cs])
    m["cT"] = np.ascontiguousarray(c.T.reshape(8, 128, len(cs)).transpose(1, 0, 2))
    return m


N_CORES = 8


def kernel(x_prompt, x_sample, c_prompt, c_sample, ada_w, ada_b, norm1_g, norm2_g, w_in,
           att_q_norm, att_k_norm, att_rpb, dn_conv_w, dn_a_log, dn_dt_bias, dn_out_norm, w_o,
           ffn_w_up, ffn_conv_w, ffn_conv_b, ffn_w_down):
    x_prompt = np.asarray(x_prompt, np.float32)
    x_sample = np.asarray(x_sample, np.float32)
    c_prompt = np.asarray(c_prompt, np.float32)
    c_sample = np.asarray(c_sample, np.float32)
    BP, TP, _ = x_prompt.shape
    BS, TS, _ = x_sample.shape
    ppc = BP // N_CORES
    seqs = [TP] * ppc + [TS]
    shared = shared_inputs(len(seqs), ada_w, ada_b, norm1_g, norm2_g, w_in, att_q_norm, att_k_norm, att_rpb,
                           dn_conv_w, dn_a_log, dn_dt_bias, dn_out_norm, w_o, ffn_w_up, ffn_conv_w, ffn_conv_b,
                           ffn_w_down)
    in_maps = []
    for c in range(N_CORES):
        xs = [x_prompt[c * ppc + i] for i in range(ppc)] + [x_sample[c * BS // N_CORES]]
        cs = [c_prompt[c * ppc + i] for i in range(ppc)] + [c_sample[c * BS // N_CORES]]
        in_maps.append(core_inputs(shared, xs, cs))
    nc = build(seqs)
    res = run_bass_kernel_spmd(nc, in_maps, core_ids=list(range(N_CORES)))
    y_prompt = np.empty_like(x_prompt)
    y_sample = np.empty_like(x_sample)
    half = TS // 2
    for c in range(N_CORES):
        y = np.asarray(res.results[c]["y"], np.float32)
        y_prompt[c * ppc:(c + 1) * ppc] = y[:ppc * TP].reshape(ppc, TP, D)
        sidx = c * BS // N_CORES
        ys = y[ppc * TP:].reshape(TS, D)
        if c % 2 == 0:
            y_sample[sidx, :half] = ys[:half]
        else:
            y_sample[sidx, half:] = ys[half:]
    return (y_prompt, y_sample)
```
